# Optimizing a Trainium2 kernel written in Bass

```python
import jax, jax.numpy as jnp
from jax import lax
import numpy as np

D_MODEL = 2048
BATCH = 16
SEQ = 256
DEPTH = 4
DEC_BATCH = 4
DEC_SEQ = 4096
PAST_LEN = 512

GRID_W = 64
Q_BLOCK = 128
ROPE_BASE = 10000.0
EPS = 1e-6
A_HEADS = 8
A_KV_HEADS = 2
A_GROUP = A_HEADS // A_KV_HEADS
HEAD_DIM = 128
A_SCALE = HEAD_DIM ** -0.5
B_HEADS = 8
B_NOPE = 128
B_ROPE = 64
B_V = 128
KV_RANK = 512
B_SCALE = (B_NOPE + B_ROPE) ** -0.5
A_Q_W = A_HEADS * HEAD_DIM
A_KV_W = A_KV_HEADS * HEAD_DIM
B_Q_W = B_HEADS * (B_NOPE + B_ROPE)
ATTN_SPLITS = (A_Q_W, A_Q_W + A_KV_W, A_Q_W + 2 * A_KV_W, A_Q_W + 2 * A_KV_W + B_Q_W,
               A_Q_W + 2 * A_KV_W + B_Q_W + KV_RANK)
ATTN_IN = A_Q_W + 2 * A_KV_W + B_Q_W + KV_RANK + B_ROPE
ATTN_CAT = A_HEADS * HEAD_DIM + B_HEADS * B_V
CHUNK = 128
C_WIDTH = D_MODEL
C_GROUPS = 8
C_GROUP_W = C_WIDTH // C_GROUPS
FFN_HIDDEN = ((8 * D_MODEL + 3 * 256 - 1) // (3 * 256)) * 256
N_ATTN_LAYERS = (DEPTH + 1) // 2
N_CMLP_LAYERS = DEPTH // 2

kernel_name = "hybrid_diffusion_gqa_mla_chunkmlp_step"

F32 = jnp.float32


def rmsnorm(x, g):
    xf = x.astype(F32)
    y = xf * lax.rsqrt(jnp.mean(xf * xf, axis=-1, keepdims=True) + EPS)
    return (y * g.astype(F32)).astype(x.dtype)


def _rotate(x, ang):
    m = x.shape[-1] // 2
    xf = x.astype(F32)
    x1, x2 = xf[..., :m], xf[..., m:]
    shape = (1, ang.shape[0]) + (1,) * (x.ndim - 3) + (m,)
    cos = jnp.cos(ang).reshape(shape)
    sin = jnp.sin(ang).reshape(shape)
    return jnp.concatenate([x1 * cos - x2 * sin, x1 * sin + x2 * cos], axis=-1)


def axial_rope(x):
    n = x.shape[1]
    rows = n // GRID_W
    t = jnp.arange(rows * GRID_W)
    row = (t // GRID_W).astype(F32)
    col = (t % GRID_W).astype(F32)
    half = x.shape[-1] // 2
    inv = 1.0 / (ROPE_BASE ** (jnp.arange(0, half, 2, dtype=F32) / half))
    out = jnp.concatenate([_rotate(x[..., :half], row[:, None] * inv),
                           _rotate(x[..., half:], col[:, None] * inv)], axis=-1)
    return out.astype(x.dtype)


def blocked_attention(q, k, v, scale):
    b, sq, hk, g, dk = q.shape
    nb = sq // Q_BLOCK
    qb = jnp.moveaxis(q.reshape(b, nb, Q_BLOCK, hk, g, dk), 1, 0)

    def one_block(qi):
        s = jnp.einsum('bqhgd,bkhd->bhgqk', qi, k, preferred_element_type=F32) * scale
        p = jax.nn.softmax(s, axis=-1).astype(v.dtype)
        return jnp.einsum('bhgqk,bkhe->bqhge', p, v)

    o = lax.map(one_block, qb)
    return jnp.moveaxis(o, 0, 1).reshape(b, sq, hk, g, v.shape[-1])


def _attn_project(h, w_in, q_norm, k_norm, kv_norm):
    b, s, _ = h.shape
    z = h @ w_in
    qa, ka, va, qb, ckv, krope = jnp.split(z, list(ATTN_SPLITS), axis=-1)
    qa = rmsnorm(qa.reshape(b, s, A_KV_HEADS, A_GROUP, HEAD_DIM), q_norm)
    ka = rmsnorm(ka.reshape(b, s, A_KV_HEADS, HEAD_DIM), k_norm)
    va = va.reshape(b, s, A_KV_HEADS, HEAD_DIM)
    qb = qb.reshape(b, s, B_HEADS, B_NOPE + B_ROPE)
    ckv = rmsnorm(ckv, kv_norm)
    return qa, ka, va, qb, ckv, krope


def _mla_expand(ckv, krope, w_kv_up):
    b, s, _ = ckv.shape
    kv = (ckv @ w_kv_up).reshape(b, s, B_HEADS, B_NOPE + B_V)
    k_nope, v = kv[..., :B_NOPE], kv[..., B_NOPE:]
    k = jnp.concatenate([k_nope, jnp.broadcast_to(krope[:, :, None, :], (b, s, B_HEADS, B_ROPE))], axis=-1)
    return k, v


def attn_mixer(h, w_in, q_norm, k_norm, kv_norm, w_kv_up, w_out, ctx=None):
    b, s, _ = h.shape
    qa, ka, va, qb, ckv, krope = _attn_project(h, w_in, q_norm, k_norm, kv_norm)
    if ctx is None:
        keys_a, vals_a, ckv_all, krope_all = ka, va, ckv, krope
    else:
        ctx_k, ctx_v, ctx_ckv, ctx_krope = ctx
        qa = axial_rope(qa)
        qb = jnp.concatenate([qb[..., :B_NOPE], axial_rope(qb[..., B_NOPE:])], axis=-1)
        keys_a = jnp.concatenate([axial_rope(ka), ctx_k], axis=1)
        vals_a = jnp.concatenate([va, ctx_v], axis=1)
        ckv_all = jnp.concatenate([ckv, ctx_ckv], axis=1)
        krope_all = jnp.concatenate([axial_rope(krope[:, :, None, :])[:, :, 0, :], ctx_krope], axis=1)
    oa = blocked_attention(qa, keys_a, vals_a, A_SCALE)
    kb, vb = _mla_expand(ckv_all, krope_all, w_kv_up)
    ob = blocked_attention(qb[:, :, :, None, :], kb, vb, B_SCALE)
    o = jnp.concatenate([oa.reshape(b, s, A_HEADS * HEAD_DIM), ob.reshape(b, s, B_HEADS * B_V)], axis=-1)
    return o @ w_out, (ka, va, ckv, krope)


def chunk_mlp(h, w_in, v_norm, w_s, b_s, w_out):
    b, s, _ = h.shape
    z = jax.nn.gelu(h @ w_in)
    u, v = z[..., :C_WIDTH], z[..., C_WIDTH:]
    v = rmsnorm(v, v_norm).reshape(b, s // CHUNK, CHUNK, C_GROUPS, C_GROUP_W)
    sv = jnp.einsum('gpq,bnqgc->bnpgc', w_s, v) + b_s.T[None, None, :, :, None]
    return (u * sv.reshape(b, s, C_WIDTH)) @ w_out


def swiglu(h, w_gate, w_up, w_down):
    return (jax.nn.silu(h @ w_gate) * (h @ w_up)) @ w_down


def ada_mod(cond, w, bias):
    m = jax.nn.silu(cond) @ w + bias
    return jnp.split(m[:, None, :], 6, axis=-1)


def modulate(x, g, shift, scale):
    return rmsnorm(x, g) * (1 + scale) + shift


def setup_inputs(seed: int = 0) -> dict:
    key = jax.random.key(seed)
    ks = jax.random.split(key, 32)

    def nrm(k, shape, scale):
        return jax.random.normal(k, shape, F32) * scale

    def gain(k, shape):
        return 1.0 + 0.02 * jax.random.normal(k, shape, F32)

    D = D_MODEL
    return {
        'x_prompt': nrm(ks[0], (BATCH, SEQ, D), 1.0),
        'x_sample': nrm(ks[1], (DEC_BATCH, DEC_SEQ, D), 1.0),
        'c': nrm(ks[2], (DEC_BATCH, D), 1.0),
        'cache_gqa_k': nrm(ks[3], (DEC_BATCH, N_ATTN_LAYERS, PAST_LEN, A_KV_HEADS, HEAD_DIM), 1.0),
        'cache_gqa_v': nrm(ks[4], (DEC_BATCH, N_ATTN_LAYERS, PAST_LEN, A_KV_HEADS, HEAD_DIM), 1.0),
        'cache_mla_ckv': nrm(ks[5], (DEC_BATCH, N_ATTN_LAYERS, PAST_LEN, KV_RANK), 1.0),
        'cache_mla_krope': nrm(ks[6], (DEC_BATCH, N_ATTN_LAYERS, PAST_LEN, B_ROPE), 1.0),
        'c_ctx': nrm(ks[7], (D,), 1.0),
        'ada_w': nrm(ks[8], (DEPTH, D, 6 * D), 0.5 * D ** -0.5),
        'ada_b': nrm(ks[9], (DEPTH, 6 * D), 0.02),
        'norm_mix': gain(ks[10], (DEPTH, D)),
        'norm_ffn': gain(ks[11], (DEPTH, D)),
        'ffn_gate': nrm(ks[12], (DEPTH, D, FFN_HIDDEN), D ** -0.5),
        'ffn_up': nrm(ks[13], (DEPTH, D, FFN_HIDDEN), D ** -0.5),
        'ffn_down': nrm(ks[14], (DEPTH, FFN_HIDDEN, D), FFN_HIDDEN ** -0.5),
        'attn_w_in': nrm(ks[15], (N_ATTN_LAYERS, D, ATTN_IN), D ** -0.5),
        'attn_q_norm': gain(ks[16], (N_ATTN_LAYERS, HEAD_DIM)),
        'attn_k_norm': gain(ks[17], (N_ATTN_LAYERS, HEAD_DIM)),
        'attn_kv_norm': gain(ks[18], (N_ATTN_LAYERS, KV_RANK)),
        'attn_w_kv_up': nrm(ks[19], (N_ATTN_LAYERS, KV_RANK, B_HEADS * (B_NOPE + B_V)), KV_RANK ** -0.5),
        'attn_w_out': nrm(ks[20], (N_ATTN_LAYERS, ATTN_CAT, D), ATTN_CAT ** -0.5),
        'cmlp_w_in': nrm(ks[21], (N_CMLP_LAYERS, D, 2 * C_WIDTH), D ** -0.5),
        'cmlp_v_norm': gain(ks[22], (N_CMLP_LAYERS, C_WIDTH)),
        'cmlp_w_s': nrm(ks[23], (N_CMLP_LAYERS, C_GROUPS, CHUNK, CHUNK), CHUNK ** -0.5),
        'cmlp_b_s': gain(ks[24], (N_CMLP_LAYERS, C_GROUPS, CHUNK)),
        'cmlp_w_out': nrm(ks[25], (N_CMLP_LAYERS, C_WIDTH, D), C_WIDTH ** -0.5),
        'final_norm': gain(ks[26], (D,)),
    }


def reference(x_prompt, x_sample, c, cache_gqa_k, cache_gqa_v, cache_mla_ckv, cache_mla_krope,
              c_ctx, ada_w, ada_b, norm_mix, norm_ffn, ffn_gate, ffn_up, ffn_down,
              attn_w_in, attn_q_norm, attn_k_norm, attn_kv_norm, attn_w_kv_up, attn_w_out,
              cmlp_w_in, cmlp_v_norm, cmlp_w_s, cmlp_b_s, cmlp_w_out, final_norm):
    xp, xs = x_prompt, x_sample
    cond_p = jnp.broadcast_to(c_ctx[None, :], (xp.shape[0], c_ctx.shape[0]))
    new_k, new_v, new_ckv, new_kr = [], [], [], []
    for i in range(DEPTH):
        j = i // 2
        sh1p, sc1p, g1p, sh2p, sc2p, g2p = ada_mod(cond_p, ada_w[i], ada_b[i])
        sh1s, sc1s, g1s, sh2s, sc2s, g2s = ada_mod(c, ada_w[i], ada_b[i])
        hp = modulate(xp, norm_mix[i], sh1p, sc1p)
        hs = modulate(xs, norm_mix[i], sh1s, sc1s)
        if i % 2 == 0:
            wts = (attn_w_in[j], attn_q_norm[j], attn_k_norm[j], attn_kv_norm[j], attn_w_kv_up[j], attn_w_out[j])
            op, (k_c, v_c, ckv_c, kr_c) = attn_mixer(hp, *wts)
            new_k.append(k_c)
            new_v.append(v_c)
            new_ckv.append(ckv_c)
            new_kr.append(kr_c)
            os_, _ = attn_mixer(hs, *wts, ctx=(cache_gqa_k[:, j], cache_gqa_v[:, j],
                                               cache_mla_ckv[:, j], cache_mla_krope[:, j]))
        else:
            wts = (cmlp_w_in[j], cmlp_v_norm[j], cmlp_w_s[j], cmlp_b_s[j], cmlp_w_out[j])
            op = chunk_mlp(hp, *wts)
            os_ = chunk_mlp(hs, *wts)
        xp = xp + g1p * op
        xs = xs + g1s * os_
        fw = (ffn_gate[i], ffn_up[i], ffn_down[i])
        xp = xp + g2p * swiglu(modulate(xp, norm_ffn[i], sh2p, sc2p), *fw)
        xs = xs + g2s * swiglu(modulate(xs, norm_ffn[i], sh2s, sc2s), *fw)
    y_prompt = rmsnorm(xp, final_norm)
    y_sample = rmsnorm(xs, final_norm)
    state_gqa_k = jnp.stack(new_k, axis=1)
    state_gqa_v = jnp.stack(new_v, axis=1)
    state_mla_ckv = jnp.stack(new_ckv, axis=1)
    state_mla_krope = jnp.stack(new_kr, axis=1)
    return (y_prompt, y_sample, state_gqa_k, state_gqa_v, state_mla_ckv, state_mla_krope)
```

```python
import numpy as np
from contextlib import ExitStack
import concourse.bass as bass
import concourse.mybir as mybir
from concourse.bass_utils import run_bass_kernel_spmd

F32 = mybir.dt.float32
BF16 = mybir.dt.bfloat16
F32R = mybir.dt.float32r
AF = mybir.ActivationFunctionType
ALU = mybir.AluOpType

D = 2048
KC = 16
NT = 2560
NS = 2048
NPR = 512
NU = 5
FF = 5632
JC = 44
EPS = 1e-6
ATTN_IN = 3648
DEPTH = 4

STAGES = None


class Prog:
    ENG = ('pe', 'act', 'dve', 'pool', 'sp')

    def __init__(self, nc, stack):
        self.nc = nc
        self.stack = stack
        self.sems = []
        self.sem_cnt = []
        self.esem = {}
        for e in self.ENG[:4]:
            self.esem[e] = self.new_sem("c_" + e)
        self.dma_sems = {}
        self.q = {e: [] for e in self.ENG}
        self.waited = {e: {} for e in self.ENG}
        self.last_w = {}
        self.readers = {}
        self.nops = 0

    def new_sem(self, name):
        h = self.stack.enter_context(self.nc.semaphore(name))
        self.sems.append(h)
        self.sem_cnt.append(0)
        return len(self.sems) - 1

    def dsem(self, key):
        if key not in self.dma_sems:
            self.dma_sems[key] = self.new_sem("d_%d" % len(self.dma_sems))
        return self.dma_sems[key]

    def op(self, eng, fn, reads=(), writes=(), dma=None, inc=None):
        psr = [r for r in reads if isinstance(r, str) and r.startswith('ps')]
        if psr:
            reads = [r for r in reads if r not in psr]
            writes = list(writes) + psr
        deps = []
        for r in reads:
            w = self.last_w.get(r)
            if w:
                deps.append(w)
        for r in writes:
            w = self.last_w.get(r)
            if w:
                deps.append(w)
            deps.extend(self.readers.get(r, ()))
        if dma is None:
            si = self.esem[eng]
            inc = 1
        else:
            si = self.dsem(dma)
            inc = 16 if inc is None else inc
        self.sem_cnt[si] += inc
        sig = (si, self.sem_cnt[si])
        need = {}
        for (s, v) in deps:
            if eng == 'pe' and s == self.esem['pe']:
                continue
            if self.waited[eng].get(s, 0) < v:
                need[s] = max(need.get(s, 0), v)
        for s, v in need.items():
            self.waited[eng][s] = v
        self.q[eng].append((list(need.items()), fn, si, inc))
        for r in reads:
            self.readers.setdefault(r, []).append(sig)
        for r in writes:
            self.last_w[r] = sig
            self.readers[r] = []
        self.nops += 1
        return sig

    def flush(self):
        nc = self.nc
        need = []
        for si in range(len(self.sems)):
            v = self.sem_cnt[si]
            if v > 0 and self.waited['sp'].get(si, 0) < v:
                need.append((si, v))
                self.waited['sp'][si] = v
        self.q['sp'].append((need, None, None, 0))
        q = self.q
        self.q = {e: [] for e in self.ENG}
        sems = self.sems

        def body(lst):
            def f(e):
                for (waits, fn, si, inc) in lst:
                    for (s, v) in waits:
                        e.wait_ge(sems[s], v)
                    if fn is not None:
                        ins = fn(e)
                        ins.then_inc(sems[si], inc)
            return f
        with nc.Block() as block:
            block.sync(body(q['sp']))
            block.tensor(body(q['pe']))
            block.scalar(body(q['act']))
            block.vector(body(q['dve']))
            block.gpsimd(body(q['pool']))
        self.last_w = {}
        self.readers = {}


class Ring:
    def __init__(self, name, n):
        self.name = name
        self.n = n
        self.i = 0

    def next(self):
        s = self.i % self.n
        self.i += 1
        return s, "%s%d" % (self.name, s)


def build_program(cfg):
    nc = bass.Bass("TRN2", target_bir_lowering=False)

    def din(name, shape):
        return nc.dram_tensor(name, list(shape), F32, kind="ExternalInput").ap()

    def dout(name, shape):
        return nc.dram_tensor(name, list(shape), F32, kind="ExternalOutput").ap()

    xin = din("xin", [NT, D])
    condT = din("condT", [128, KC, 2])
    cache_k = din("cache_k", [2, 512, 256])
    cache_v = din("cache_v", [2, 512, 256])
    cache_ckv = din("cache_ckv", [2, 512, 512])
    cache_kr = din("cache_kr", [2, 512, 64])
    stages = cfg.get("stages")

    def want(name):
        return stages is None or name in stages
    need_layer = [want("mix%d" % L) or want("ffn%d" % L) for L in range(DEPTH)]
    ada_w = [din("ada_w%d" % L, [D, 6 * D]) if need_layer[L] else None for L in range(DEPTH)]
    ada_b = din("ada_b", [DEPTH, 6 * D])
    norm_mix = din("norm_mix", [DEPTH, D])
    norm_ffn = din("norm_ffn", [DEPTH, D])
    ffn_gate = [din("ffn_gate%d" % L, [D, FF]) if want("ffn%d" % L) else None for L in range(DEPTH)]
    ffn_up = [din("ffn_up%d" % L, [D, FF]) if want("ffn%d" % L) else None for L in range(DEPTH)]
    ffn_down = [din("ffn_down%d" % L, [FF, D]) if want("ffn%d" % L) else None for L in range(DEPTH)]
    attn_w_in = [din("attn_w_in%d" % j, [D, ATTN_IN]) if want("mix%d" % (2 * j)) else None for j in range(2)]
    attn_q_norm = din("attn_q_norm", [2, 128])
    attn_k_norm = din("attn_k_norm", [2, 128])
    attn_kv_norm = din("attn_kv_norm", [2, 512])
    attn_w_kv_up = [din("attn_w_kv_up%d" % j, [512, 2048]) if want("mix%d" % (2 * j)) else None for j in range(2)]
    attn_w_out = [din("attn_w_out%d" % j, [D, D]) if want("mix%d" % (2 * j)) else None for j in range(2)]
    cmlp_w_in = [din("cmlp_w_in%d" % j, [D, 2 * D]) if want("mix%d" % (2 * j + 1)) else None for j in range(2)]
    cmlp_v_norm = din("cmlp_v_norm", [2, D])
    cmlp_w_s = din("cmlp_w_s", [2, 8, 128, 128])
    cmlp_b_s = din("cmlp_b_s", [2, 8, 128])
    cmlp_w_out = [din("cmlp_w_out%d" % j, [D, D]) if want("mix%d" % (2 * j + 1)) else None for j in range(2)]
    final_norm = din("final_norm", [D])
    ropeA = din("ropeA", [2, 128, NS])
    ropeB = din("ropeB", [2, 128, NS])
    ident_in = din("ident_in", [128, 128])

    y_out = dout("y_out", [NT, D])
    st_k = dout("st_k", [2, NPR, 256])
    st_v = dout("st_v", [2, NPR, 256])
    st_ckv = dout("st_ckv", [2, NPR, 512])
    st_kr = dout("st_kr", [2, NPR, 64])

    xT = nc.dram_tensor("xT_scratch", [KC, 128, NT], F32).ap()

    with ExitStack() as top:
        P = Prog(nc, top)
        ps = [top.enter_context(nc.psum_tensor("ps%d" % i, [128, 512], F32)) for i in range(8)]

        uniq = [0]

        def sb(stack, name, shape, dt):
            uniq[0] += 1
            return stack.enter_context(nc.sbuf_tensor("%s_%d" % (name, uniq[0]), list(shape), dt))

        ident = sb(top, "ident", [128, 128], F32)
        ones_f = sb(top, "ones_f", [128, 128], F32)
        ones_r = sb(top, "ones_r", [128, 128], F32R)
        ones_b = sb(top, "ones_b", [128, 128], BF16)
        epsb = sb(top, "epsb", [128, 1], F32)
        scond = sb(top, "scond", [128, KC, 2], BF16)
        modT2 = [sb(top, "modT%d" % i, [128, 96, 2], F32) for i in range(2)]
        coefA2 = [sb(top, "coefA%d" % i, [128, 2, KC, 2], F32) for i in range(2)]
        nrmw2 = [sb(top, "nrmw%d" % i, [128, 2, KC], F32) for i in range(2)]
        cur = [0]
        fnw = sb(top, "fnw", [128, KC], F32)

        def dma(eng, out, in_, reads=(), writes=(), key=None, slow=False):
            if slow:
                return P.op(eng, lambda e: e.dma_start(out=out, in_=in_, allow_slow_non_contiguous=True), reads=reads, writes=writes, dma=key)
            return P.op(eng, lambda e: e.dma_start(out=out, in_=in_), reads=reads, writes=writes, dma=key)

        def mod_ap(split, k, c):
            return modT2[cur[0]][:, split * 16 + k, c:c + 1]

        def unit_cond(u):
            return 0 if u < 4 else 1

        def block_init():
            with ExitStack() as st:
                xt = [sb(st, "xt%d" % i, [128, D], F32) for i in range(2)]
                xo = [sb(st, "xo%d" % i, [128, KC, 128], F32) for i in range(2)]
                cnd = sb(st, "cnd", [128, KC, 2], F32)
                dma('sp', ident[:], ident_in[:, :], writes=['ident'], key='c0')
                P.op('dve', lambda e: e.memset(ones_f[:], 1.0), writes=['ones_f'])
                P.op('dve', lambda e: e.tensor_copy(out=ones_r[:], in_=ones_f[:]), reads=['ones_f'], writes=['ones_r'])
                P.op('dve', lambda e: e.memset(ones_b[:], 1.0), writes=['ones_b'])
                P.op('dve', lambda e: e.memset(epsb[:], EPS), writes=['epsb'])
                dma('sp', cnd[:], condT[:, :, :], writes=['cnd'], key='c1')
                P.op('act', lambda e: e.activation(out=scond[:], in_=cnd[:], func=AF.Silu), reads=['cnd'], writes=['scond'])
                dma('sp', fnw[:], final_norm.rearrange("(k p) -> p k", p=128), writes=['fnw'], key='c2', slow=True)
                for t in range(NT // 128):
                    s = t % 2
                    dma('sp', xt[s][:], xin[t * 128:(t + 1) * 128, :], writes=['xt%d' % s], key='xt%d' % s)
                    for g in range(4):
                        b = (t * 4 + g) % 8

                        def tr(e, s=s, g=g, b=b):
                            for i in range(4):
                                k = g * 4 + i
                                ins = e.transpose(ps[b][:, i * 128:(i + 1) * 128], xt[s][:, k * 128:(k + 1) * 128], ident[:])
                            return ins
                        P.op('pe', tr, reads=['xt%d' % s, 'ident'], writes=['ps%d' % b])
                        eng = 'dve' if g % 2 == 0 else 'act'
                        if eng == 'dve':
                            P.op('dve', lambda e, s=s, g=g, b=b: e.tensor_copy(out=xo[s][:, g * 4:(g + 1) * 4, :], in_=ps[b][:].rearrange("p (a n) -> p a n", a=4)),
                                 reads=['ps%d' % b], writes=[('xo', s, g)])
                        else:
                            P.op('act', lambda e, s=s, g=g, b=b: e.copy(out=xo[s][:, g * 4:(g + 1) * 4, :], in_=ps[b][:].rearrange("p (a n) -> p a n", a=4)),
                                 reads=['ps%d' % b], writes=[('xo', s, g)])
                    u = t // 4
                    dma('sp', xT[:, :, t * 128:(t + 1) * 128].rearrange("k p n -> p k n"), xo[s][:],
                        reads=[('xo', s, g) for g in range(4)], writes=[('xTt', t)], key='xo%d' % s)
                P.flush()

        def ada_steps(L, st, bank):
            par = L % 2
            modT, coefA, nrmw = modT2[par], coefA2[par], nrmw2[par]
            wa = [sb(st, "wa%d" % i, [128, KC, 256], BF16) for i in range(2)]
            adab = sb(st, "adab", [128, 96], F32)
            NSL = 48
            names = ['adawa0', 'adawa1']

            def load(s_):
                dma('pool', wa[s_ % 2][:], ada_w[L][:, s_ * 256:(s_ + 1) * 256].rearrange("(k p) n -> p k n", p=128),
                    writes=[names[s_ % 2]], key=names[s_ % 2])

            def first():
                dma('sp', adab[:], ada_b[L].rearrange("(m p) -> p m", p=128), writes=['adab'], key='adac0', slow=True)
                dma('sp', nrmw[:, 0, :], norm_mix[L].rearrange("(k p) -> p k", p=128), writes=['nrmwA%d' % par], key='adac1', slow=True)
                dma('sp', nrmw[:, 1, :], norm_ffn[L].rearrange("(k p) -> p k", p=128), writes=['nrmwB%d' % par], key='adac2', slow=True)
                load(0)

            def step(s_):
                def f():
                    if s_ + 1 < NSL:
                        load(s_ + 1)

                    def mm(e):
                        for mi in range(2):
                            m = s_ * 2 + mi
                            for k in range(KC):
                                ins = e.matmul(ps[bank][:, m * 2:m * 2 + 2], wa[s_ % 2][:, k, mi * 128:(mi + 1) * 128], scond[:, k, :],
                                               start=(k == 0), stop=(k == KC - 1))
                        return ins
                    P.op('pe', mm, reads=[names[s_ % 2], 'scond'], writes=['ps%d' % bank])
                return f

            def last():
                P.op('dve', lambda e: e.tensor_tensor(out=modT[:], in0=ps[bank][:, 0:192].rearrange("p (m c) -> p m c", c=2),
                                                      in1=adab[:].unsqueeze(2).to_broadcast([128, 96, 2]), op=ALU.add),
                     reads=['ps%d' % bank, 'adab'], writes=['modT%d' % par])
                for sub in range(2):
                    sc_split = 1 if sub == 0 else 4
                    for c in range(2):
                        P.op('dve', lambda e, sub=sub, c=c, sc_split=sc_split: e.scalar_tensor_tensor(
                            out=coefA[:, sub, :, c], in0=modT[:, sc_split * 16:(sc_split + 1) * 16, c], scalar=1.0,
                            in1=nrmw[:, sub, :], op0=ALU.add, op1=ALU.mult),
                            reads=['modT%d' % par, 'nrmwA%d' % par, 'nrmwB%d' % par], writes=['coefA%d' % par])
            return [first] + [step(s_) for s_ in range(NSL)] + [last]

        def block_ada(L):
            with ExitStack() as st:
                for f in ada_steps(L, st, 0):
                    f()
                P.flush()

        class PrepBufs:
            def __init__(self, st, nh=1):
                self.xg = sb(st, "xg", [128, KC, 512], F32)
                self.sq = [sb(st, "sq%d" % i, [128, 512], F32R) for i in range(2)]
                self.rstd = sb(st, "rstd", [128, 512], F32)
                self.tmp = [sb(st, "ptmp%d" % i, [128, 512], F32) for i in range(2)]
                self.sqr = Ring('sq', 2)
                self.tmr = Ring('ptmp', 2)

        def prep_unit(pb, u, sub, hT_ap, hname, psb):
            c = unit_cond(u)
            for g in range(4):
                dma('sp', pb.xg[:, g * 4:(g + 1) * 4, :], xT[g * 4:(g + 1) * 4, :, u * 512:(u + 1) * 512].rearrange("k p n -> p k n"),
                    reads=[('xT', k, u) for k in range(g * 4, g * 4 + 4)], writes=[('xg', g)], key='xg%d' % g)
            for k in range(KC):
                s, rn = pb.sqr.next()
                P.op('act', lambda e, k=k, s=s: e.activation(out=pb.sq[s][:], in_=pb.xg[:, k, :], func=AF.Square),
                     reads=[('xg', k // 4)], writes=[rn])
                P.op('pe', lambda e, k=k, s=s: e.matmul(ps[psb][:], ones_r[:], pb.sq[s][:], start=(k == 0), stop=(k == KC - 1)),
                     reads=[rn, 'ones_r'], writes=['ps%d' % psb])
            P.op('act', lambda e: e.activation(out=pb.rstd[:], in_=ps[psb][:], func=AF.Sqrt, scale=1.0 / D, bias=epsb[:]),
                 reads=['ps%d' % psb, 'epsb'], writes=['rstd'])
            P.op('dve', lambda e: e.reciprocal(out=pb.rstd[:], in_=pb.rstd[:]), reads=['rstd'], writes=['rstd'])
            sh_split = 0 if sub == 0 else 3
            for k in range(KC):
                s, rn = pb.tmr.next()
                P.op('dve', lambda e, k=k, s=s: e.scalar_tensor_tensor(out=pb.tmp[s][:], in0=pb.xg[:, k, :], scalar=coefA2[cur[0]][:, sub, k, c:c + 1],
                                                                       in1=pb.rstd[:], op0=ALU.mult, op1=ALU.mult),
                     reads=[('xg', k // 4), 'coefA%d' % cur[0], 'rstd'], writes=[rn])
                P.op('act', lambda e, k=k, s=s: e.activation(out=hT_ap(k), in_=pb.tmp[s][:], func=AF.Identity,
                                                             bias=mod_ap(sh_split, k, c), scale=1.0),
                     reads=[rn, 'modT%d' % cur[0]], writes=[hname])

        class EpiBufs:
            def __init__(self, st, n=3):
                self.xi = [sb(st, "exi%d" % i, [128, 512], F32) for i in range(n)]
                self.xo = [sb(st, "exo%d" % i, [128, 512], F32) for i in range(n)]
                self.ri = Ring('exi', n)
                self.ro = Ring('exo', n)

        def epi_load(eb, m, u):
            s, rn = eb.ri.next()
            dma('sp', eb.xi[s][:], xT[m, :, u * 512:(u + 1) * 512], reads=[('xT', m, u)], writes=[rn], key=rn)
            return s, rn

        def epi_finish(eb, ld, m, u, gsplit, psb):
            s, rn = ld
            so, rno = eb.ro.next()
            c = unit_cond(u)
            P.op('dve', lambda e: e.scalar_tensor_tensor(out=eb.xo[so][:], in0=ps[psb][:], scalar=mod_ap(gsplit, m, c), in1=eb.xi[s][:],
                                                         op0=ALU.mult, op1=ALU.add),
                 reads=['ps%d' % psb, 'modT%d' % cur[0], rn], writes=[rno])
            dma('sp', xT[m, :, u * 512:(u + 1) * 512], eb.xo[so][:], reads=[rno], writes=[('xT', m, u)], key=rno)

        def block_ffn(L):
            for units in [(0, 1), (2, 3), (4,)]:
                ffn_pass(L, units, ada_next=(units == (0, 1) and L + 1 < DEPTH and need_layer[L + 1]))

        def ffn_pass(L, units, ada_next=False):
            nu = len(units)
            HJ = JC // 2
            with ExitStack() as st:
                pb = PrepBufs(st)
                eb = EpiBufs(st, 3)
                hT = sb(st, "hT", [128, KC, nu * 512], BF16)
                act = sb(st, "actb", [128, HJ, nu * 512], BF16)
                wg = [sb(st, "wg%d" % i, [128, KC, 256], BF16) for i in range(2)]
                wu = [sb(st, "wu%d" % i, [128, KC, 256], BF16) for i in range(2)]
                wd = [sb(st, "wd%d" % i, [128, HJ, 128], BF16) for i in range(2)]
                sl = [sb(st, "sl%d" % i, [128, 512], F32) for i in range(2)]
                rg, ru, rd, rs = Ring('wg', 2), Ring('wu', 2), Ring('wd', 2), Ring('sl', 2)
                for i, u in enumerate(units):
                    prep_unit(pb, u, 1, lambda k, i=i: hT[:, k, i * 512:(i + 1) * 512], ('hT', i), 0)
                bank = 0
                asteps = ada_steps(L + 1, st, 7) if ada_next else []

                def ada_tick(asteps=asteps):
                    if asteps:
                        asteps.pop(0)()
                for half in range(2):
                    bank = ffn_half(L, units, half, bank, hT, act, wg, wu, wd, sl, rg, ru, rd, rs, eb, ada_tick)
                while asteps:
                    ada_tick()
                P.flush()

        def ffn_half(L, units, half, bank, hT, act, wg, wu, wd, sl, rg, ru, rd, rs, eb, ada_tick):
                nu = len(units)
                HJ = JC // 2
                for jj in range(HJ // 2):
                    j0 = half * HJ + jj * 2
                    sg, rng_ = rg.next()
                    su, rnu = ru.next()
                    dma('pool', wg[sg][:], ffn_gate[L][:, j0 * 128:(j0 + 2) * 128].rearrange("(k p) n -> p k n", p=128), writes=[rng_], key=rng_)
                    dma('pool', wu[su][:], ffn_up[L][:, j0 * 128:(j0 + 2) * 128].rearrange("(k p) n -> p k n", p=128), writes=[rnu], key=rnu)
                    ada_tick()
                    for jl in range(2):
                        for i in range(nu):
                            bg, bu = bank % 6, (bank + 1) % 6
                            bank += 2

                            def mm(e, w, slot, b, jl=jl, i=i):
                                for k in range(KC):
                                    ins = e.matmul(ps[b][:], w[slot][:, k, jl * 128:(jl + 1) * 128], hT[:, k, i * 512:(i + 1) * 512],
                                                   start=(k == 0), stop=(k == KC - 1))
                                return ins
                            P.op('pe', lambda e, mm=mm, sg=sg, bg=bg: mm(e, wg, sg, bg), reads=[rng_, ('hT', i)], writes=['ps%d' % bg])
                            P.op('pe', lambda e, mm=mm, su=su, bu=bu: mm(e, wu, su, bu), reads=[rnu, ('hT', i)], writes=['ps%d' % bu])
                            ss, rns = rs.next()
                            P.op('act', lambda e, ss=ss, bg=bg: e.activation(out=sl[ss][:], in_=ps[bg][:], func=AF.Silu),
                                 reads=['ps%d' % bg], writes=[rns])
                            jrel = jj * 2 + jl
                            P.op('dve', lambda e, ss=ss, bu=bu, jrel=jrel, i=i: e.tensor_tensor(out=act[:, jrel, i * 512:(i + 1) * 512], in0=ps[bu][:], in1=sl[ss][:], op=ALU.mult),
                                 reads=['ps%d' % bu, rns], writes=[('act', jrel, i)])
                pend = []
                for m in range(KC):
                    sd, rnd = rd.next()
                    dma('pool', wd[sd][:], ffn_down[L][half * HJ * 128:(half + 1) * HJ * 128, m * 128:(m + 1) * 128].rearrange("(j p) n -> p j n", p=128),
                        writes=[rnd], key=rnd)
                    ada_tick()
                    for i, u in enumerate(units):
                        b = bank % 7
                        bank += 1
                        ld = epi_load(eb, m, u)

                        def mm2(e, sd=sd, b=b, i=i):
                            for j in range(HJ):
                                ins = e.matmul(ps[b][:], wd[sd][:, j, :], act[:, j, i * 512:(i + 1) * 512], start=(j == 0), stop=(j == HJ - 1))
                            return ins
                        P.op('pe', mm2, reads=[rnd] + [('act', j, i) for j in range(HJ)], writes=['ps%d' % b])
                        epi_finish(eb, ld, m, u, 5, b)
                return bank


        def block_cmlp(L):
            j = L // 2
            with ExitStack() as st:
                pb = PrepBufs(st)
                eb = EpiBufs(st, 2)
                hT = sb(st, "chT", [128, KC, 512], BF16)
                vtm = sb(st, "vtm", [128, 4, D], F32)
                vhat = sb(st, "vhat", [128, 4, D], BF16)
                gated = sb(st, "gated", [128, KC, 512], BF16)
                wv = [sb(st, "cwv%d" % i, [128, KC, 256], BF16) for i in range(2)]
                wu = [sb(st, "cwu%d" % i, [128, KC, 256], BF16) for i in range(2)]
                wo = [sb(st, "cwo%d" % i, [128, KC, 256], BF16) for i in range(2)]
                wsT = sb(st, "wsT", [128, 8, 128], BF16)
                wsl = [sb(st, "wsl%d" % i, [128, 4, 128], F32) for i in range(2)]
                bsb = sb(st, "bsb", [128, 8, 128], F32)
                vn = sb(st, "cvn", [128, KC], F32)
                ssq = sb(st, "cssq", [128, 4], F32)
                crs = sb(st, "crs", [128, 4], F32)
                usb = [sb(st, "usb%d" % i, [128, 512], F32) for i in range(2)]
                svb = [sb(st, "svb%d" % i, [128, 512], F32) for i in range(2)]
                rv, ru, ro, rus, rsv = Ring('cwv', 2), Ring('cwu', 2), Ring('cwo', 2), Ring('usb', 2), Ring('svb', 2)
                dma('sp', vn[:], cmlp_v_norm[j].rearrange("(k p) -> p k", p=128), writes=['cvn'], key='c0', slow=True)
                dma('sp', bsb[:], cmlp_b_s[j].partition_broadcast(128), writes=['bsb'], key='c1')
                for hh in range(2):
                    dma('sp', wsl[hh][:], cmlp_w_s[j, hh * 4:(hh + 1) * 4].rearrange("g p q -> p g q"), writes=['wsl%d' % hh], key='wsl%d' % hh)

                    def trw(e, hh=hh):
                        for i in range(4):
                            ins = e.transpose(ps[hh][:, i * 128:(i + 1) * 128], wsl[hh][:, i, :], ident[:])
                        return ins
                    P.op('pe', trw, reads=['wsl%d' % hh, 'ident'], writes=['ps%d' % hh])
                    P.op('dve', lambda e, hh=hh: e.tensor_copy(out=wsT[:, hh * 4:(hh + 1) * 4, :], in_=ps[hh][:].rearrange("p (g q) -> p g q", g=4)),
                         reads=['ps%d' % hh], writes=['wsT'])
                bank = 2
                for u in range(NU):
                    prep_unit(pb, u, 0, lambda k: hT[:, k, :], 'chT', bank % 8)
                    bank += 1
                    for n in range(8):
                        sv_, rnv = rv.next()
                        dma('pool', wv[sv_][:], cmlp_w_in[j][:, D + n * 256:D + (n + 1) * 256].rearrange("(k p) n -> p k n", p=128), writes=[rnv], key=rnv)
                        for tc in range(4):
                            b = bank % 8
                            bank += 1

                            def mmv(e, sv_=sv_, tc=tc, b=b):
                                for k in range(KC):
                                    ins = e.matmul(ps[b][:, 0:256], hT[:, k, tc * 128:(tc + 1) * 128], wv[sv_][:, k, :], start=(k == 0), stop=(k == KC - 1))
                                return ins
                            P.op('pe', mmv, reads=[rnv, 'chT'], writes=['ps%d' % b])
                            P.op('act', lambda e, tc=tc, n=n, b=b: e.activation(out=vtm[:, tc, n * 256:(n + 1) * 256], in_=ps[b][:, 0:256], func=AF.Gelu_apprx_tanh),
                                 reads=['ps%d' % b], writes=[('vtm', tc, n)])
                    for tc in range(4):
                        P.op('act', lambda e, tc=tc: e.activation(out=vhat[:, tc, :], in_=vtm[:, tc, :], func=AF.Square, accum_out=ssq[:, tc:tc + 1]),
                             reads=[('vtm', tc, n) for n in range(8)], writes=[('vhat', tc), ('ssq', tc)])
                    P.op('act', lambda e: e.activation(out=crs[:], in_=ssq[:], func=AF.Sqrt, scale=1.0 / D, bias=epsb[:]),
                         reads=[('ssq', tc) for tc in range(4)] + ['epsb'], writes=['crs'])
                    P.op('dve', lambda e: e.reciprocal(out=crs[:], in_=crs[:]), reads=['crs'], writes=['crs'])
                    for tc in range(4):
                        P.op('dve', lambda e, tc=tc: e.tensor_scalar(out=vhat[:, tc, :], in0=vtm[:, tc, :], scalar1=crs[:, tc:tc + 1], scalar2=None, op0=ALU.mult),
                             reads=[('vtm', tc, n) for n in range(8)] + ['crs'], writes=[('vhat', tc)])
                    for mm_ in range(8):
                        su, rnu = ru.next()
                        dma('pool', wu[su][:], cmlp_w_in[j][:, mm_ * 256:(mm_ + 1) * 256].rearrange("(k p) n -> p k n", p=128), writes=[rnu], key=rnu)
                        for ml in range(2):
                            m = mm_ * 2 + ml
                            g = m // 2
                            ba, bb = bank % 8, (bank + 1) % 8
                            bank += 2

                            def mmu(e, su=su, ml=ml, ba=ba):
                                for k in range(KC):
                                    ins = e.matmul(ps[ba][:], wu[su][:, k, ml * 128:(ml + 1) * 128], hT[:, k, :], start=(k == 0), stop=(k == KC - 1))
                                return ins
                            P.op('pe', mmu, reads=[rnu, 'chT'], writes=['ps%d' % ba])
                            s1, rn1 = rus.next()
                            P.op('act', lambda e, s1=s1, ba=ba: e.activation(out=usb[s1][:], in_=ps[ba][:], func=AF.Gelu_apprx_tanh), reads=['ps%d' % ba], writes=[rn1])

                            def mms(e, m=m, g=g, bb=bb):
                                for tc in range(4):
                                    ins = e.matmul(ps[bb][:, tc * 128:(tc + 1) * 128], vhat[:, tc, m * 128:(m + 1) * 128], wsT[:, g, :], start=True, stop=True)
                                return ins
                            P.op('pe', mms, reads=[('vhat', tc) for tc in range(4)] + ['wsT'], writes=['ps%d' % bb])
                            s2, rn2 = rsv.next()
                            P.op('dve', lambda e, s2=s2, bb=bb, m=m, g=g: e.scalar_tensor_tensor(
                                out=svb[s2][:].rearrange("p (t q) -> p t q", t=4), in0=ps[bb][:].rearrange("p (t q) -> p t q", t=4), scalar=vn[:, m:m + 1],
                                in1=bsb[:, g, :].unsqueeze(1).to_broadcast([128, 4, 128]), op0=ALU.mult, op1=ALU.add),
                                reads=['ps%d' % bb, 'cvn', 'bsb'], writes=[rn2])
                            P.op('dve', lambda e, s1=s1, s2=s2, m=m: e.tensor_tensor(out=gated[:, m, :], in0=usb[s1][:], in1=svb[s2][:], op=ALU.mult),
                                 reads=[rn1, rn2], writes=[('gated', m)])
                    for mo2 in range(8):
                        so, rno = ro.next()
                        dma('pool', wo[so][:], cmlp_w_out[j][:, mo2 * 256:(mo2 + 1) * 256].rearrange("(k p) n -> p k n", p=128), writes=[rno], key=rno)
                        for ml in range(2):
                            mo = mo2 * 2 + ml
                            b = bank % 8
                            bank += 1
                            ld = epi_load(eb, mo, u)

                            def mmo(e, so=so, ml=ml, b=b):
                                for k in range(KC):
                                    ins = e.matmul(ps[b][:], wo[so][:, k, ml * 128:(ml + 1) * 128], gated[:, k, :], start=(k == 0), stop=(k == KC - 1))
                                return ins
                            P.op('pe', mmo, reads=[rno] + [('gated', m) for m in range(KC)], writes=['ps%d' % b])
                            epi_finish(eb, ld, mo, u, 2, b)
                P.flush()


        def swap_copy(eng, dst, src, blk, reads, writes):
            dv = dst.rearrange("p k (a two c) -> p k a two c", two=2, c=blk)
            sv = src.rearrange("p k (a two c) -> p k a two c", two=2, c=blk)
            P.op(eng, lambda e: e.tensor_copy(out=dv[:, :, :, 0, :], in_=sv[:, :, :, 1, :]), reads=reads, writes=[writes + '_a'])
            P.op(eng, lambda e: e.tensor_copy(out=dv[:, :, :, 1, :], in_=sv[:, :, :, 0, :]), reads=reads, writes=[writes + '_b'])
            return [writes + '_a', writes + '_b']

        def load_perm_gain(dst, src_vec, blk, width, key):
            nb_ = width // blk
            for b_ in range(nb_):
                pb_ = b_ ^ 1
                dma('sp', dst[b_ * blk:(b_ + 1) * blk, 0:1], src_vec[pb_ * blk:(pb_ + 1) * blk].rearrange("(p o) -> p o", o=1),
                    writes=[(key, b_)], key=key, slow=True)
            return [(key, b_) for b_ in range(nb_)]

        class AttnScratch:
            pass

        def attn_scratch(j):
            a = AttnScratch()

            def dt_(name, shape):
                return nc.dram_tensor("%s_%d" % (name, j), list(shape), BF16)
            a.sendF1 = dt_("sendF1", [512, NS])
            a.recvF1 = dt_("recvF1", [1024, NS])
            a.sendF2 = dt_("sendF2", [384, NS])
            a.recvF2 = dt_("recvF2", [768, NS])

            def sF(blk):
                if blk < 4:
                    return a.sendF1.ap()[blk * 128:(blk + 1) * 128, :]
                return a.sendF2.ap()[(blk - 4) * 128:(blk - 3) * 128, :]

            def rF(r, blk):
                if blk < 4:
                    return a.recvF1.ap()[r * 512 + blk * 128:r * 512 + (blk + 1) * 128, :]
                return a.recvF2.ap()[r * 384 + (blk - 4) * 128:r * 384 + (blk - 3) * 128, :]
            a.sF = sF
            a.rF = rF
            a.sendV = dt_("sendV", [NS, 256])
            a.recvV = dt_("recvV", [2 * NS, 256])
            a.KT_s = dt_("KT_s", [10, 128, 4608]).ap()
            a.KR_s = dt_("KR_s", [128, 4608]).ap()
            a.V_s = dt_("V_s", [4608, 1280]).ap()
            a.KT_p = dt_("KT_p", [10, 128, 512]).ap()
            a.KR_p = dt_("KR_p", [128, 512]).ap()
            a.V_p = dt_("V_p", [512, 1280]).ap()
            a.CK_p = dt_("CK_p", [4, 128, 512]).ap()
            return a

        def rms_rstd(srcs, n_feat, sqring, sqbufs, bank, rstd_tile, rname, src_reads):
            n = len(srcs)
            for i, (src, rd) in enumerate(zip(srcs, src_reads)):
                s_, rn = sqring.next()
                P.op('act', lambda e, src=src, s_=s_: e.activation(out=sqbufs[s_][:], in_=src, func=AF.Square), reads=[rd], writes=[rn])
                P.op('pe', lambda e, s_=s_, i=i: e.matmul(ps[bank][:], ones_r[:], sqbufs[s_][:], start=(i == 0), stop=(i == n - 1)),
                     reads=[rn, 'ones_r'], writes=['ps%d' % bank])
            P.op('act', lambda e: e.activation(out=rstd_tile[:], in_=ps[bank][:], func=AF.Sqrt, scale=1.0 / n_feat, bias=epsb[:]),
                 reads=['ps%d' % bank, 'epsb'], writes=[rname])
            P.op('dve', lambda e: e.reciprocal(out=rstd_tile[:], in_=rstd_tile[:]), reads=[rname], writes=[rname])

        def block_attn(L):
            j = L // 2
            a = attn_scratch(j)
            parts = cfg.get("attn_parts", "123")
            if "1" in parts:
                attn_kv_pass(L, j, a)
            if "2" in parts:
                attn_exchange(L, j, a)
            if "3" in parts:
                attn_main(L, j, a)

        def attn_kv_pass(L, j, a):
            W = attn_w_in[j]
            with ExitStack() as st:
                pb = PrepBufs(st)
                hT = sb(st, "ahT", [128, KC, 512], BF16)
                wk = sb(st, "awk", [128, KC, 256], BF16)
                wkp = sb(st, "awkp", [128, KC, 256], BF16)
                wv = sb(st, "awv", [128, KC, 256], BF16)
                wc = sb(st, "awc", [128, KC, 512], BF16)
                wr = sb(st, "awr", [128, KC, 128], BF16)
                wrp = sb(st, "awrp", [128, KC, 128], BF16)
                gk = sb(st, "agk", [128, 1], F32)
                gkp = sb(st, "agkp", [128, 1], F32)
                gkv = sb(st, "agkv", [128, 4], F32)
                rA = sb(st, "arA", [128, 2, 512], F32)
                rB = sb(st, "arB", [128, 2, 512], F32)
                sq = [sb(st, "asq%d" % i, [128, 512], F32R) for i in range(2)]
                sqr = Ring('asq', 2)
                rstd = sb(st, "arstd", [128, 512], F32)
                t1 = sb(st, "at1", [128, 512], F32)
                t2 = sb(st, "at2", [128, 512], F32)
                kf = sb(st, "akf", [128, 512], F32)
                craw = sb(st, "acraw", [128, 4, 512], F32)
                kb = [sb(st, "akb%d" % i, [128, 512], BF16) for i in range(2)]
                kbr = Ring('akb', 2)
                vb = sb(st, "avb", [128, 4, 256], BF16)
                stk = sb(st, "astk", [128, 4, 256], F32)
                stv = sb(st, "astv", [128, 4, 256], F32)
                stc = sb(st, "astc", [128, 4, 512], F32)
                strr = sb(st, "astr", [128, 4, 64], F32)
                dma('pool', wk[:], W[:, 1024:1280].rearrange("(k p) n -> p k n", p=128), writes=['awk'], key='w0')
                dma('pool', wv[:], W[:, 1280:1536].rearrange("(k p) n -> p k n", p=128), writes=['awv'], key='w1')
                dma('pool', wc[:], W[:, 3072:3584].rearrange("(k p) n -> p k n", p=128), writes=['awc'], key='w2')
                dma('pool', wr[:, :, 0:64], W[:, 3584:3648].rearrange("(k p) n -> p k n", p=128), writes=['awr_a'], key='w3')
                dma('pool', wr[:, :, 64:128], W[:, 3584:3648].rearrange("(k p) n -> p k n", p=128), writes=['awr_b'], key='w3')
                wkp_r = swap_copy('dve', wkp[:], wk[:], 32, ['awk'], 'awkp')
                wrp_r = swap_copy('dve', wrp[:], wr[:], 16, ['awr_a', 'awr_b'], 'awrp')
                dma('sp', gk[:], attn_k_norm[j].rearrange("(p o) -> p o", o=1), writes=['agk'], key='c0', slow=True)
                gkp_r = load_perm_gain(gkp, attn_k_norm[j], 32, 128, 'agkp')
                dma('sp', gkv[:], attn_kv_norm[j].rearrange("(c p) -> p c", p=128), writes=['agkv'], key='c1', slow=True)
                bank = [0]

                def nb():
                    b = bank[0] % 8
                    bank[0] += 1
                    return b

                def proj(wt, c0, b, reads):
                    def f(e):
                        for k in range(KC):
                            ins = e.matmul(ps[b][:], wt[:, k, c0:c0 + 128], hT[:, k, :], start=(k == 0), stop=(k == KC - 1))
                        return ins
                    P.op('pe', f, reads=reads + ['ahT'], writes=['ps%d' % b])

                def transposes_out(src, dst_ap_fn, b, reads, wname, ncol=128):
                    def f(e):
                        for tt in range(4):
                            ins = e.transpose(ps[b][:, tt * 128:(tt + 1) * 128], src[:, tt * 128:(tt + 1) * 128], ident[:])
                        return ins
                    P.op('pe', f, reads=reads + ['ident'], writes=['ps%d' % b])
                    P.op('dve', lambda e: e.tensor_copy(out=dst_ap_fn(), in_=ps[b][:].rearrange("p (t c) -> p t c", t=4)[:, :, 0:ncol]),
                         reads=['ps%d' % b], writes=[wname])

                for u in range(NU):
                    samp = u < 4
                    prep_unit(pb, u, 0, lambda k: hT[:, k, :], 'ahT', nb())
                    if samp:
                        dma('sp', rA[:], ropeA[:, :, u * 512:(u + 1) * 512].rearrange("t p n -> p t n"), writes=['arA'], key='c2')
                        dma('sp', rB[:], ropeB[:, :, u * 512:(u + 1) * 512].rearrange("t p n -> p t n"), writes=['arB'], key='c3')
                    for h in range(2):
                        b0 = nb()
                        proj(wk, h * 128, b0, ['awk'])
                        if samp:
                            b1 = nb()
                            proj(wkp, h * 128, b1, wkp_r)
                        b2 = nb()
                        rms_rstd([ps[b0][:]], 128, sqr, sq, b2, rstd, 'arstd', ['ps%d' % b0])
                        s_, rnk = kbr.next()
                        if samp:
                            P.op('dve', lambda e, b0=b0: e.scalar_tensor_tensor(out=t1[:], in0=ps[b0][:], scalar=gk[:, 0:1], in1=rA[:, 0, :], op0=ALU.mult, op1=ALU.mult),
                                 reads=['ps%d' % b0, 'agk', 'arA'], writes=['at1'])
                            P.op('dve', lambda e, b1=b1: e.scalar_tensor_tensor(out=t2[:], in0=ps[b1][:], scalar=gkp[:, 0:1], in1=rA[:, 1, :], op0=ALU.mult, op1=ALU.mult),
                                 reads=['ps%d' % b1, 'arA'] + gkp_r, writes=['at2'])
                            P.op('dve', lambda e: e.tensor_tensor(out=t1[:], in0=t1[:], in1=t2[:], op=ALU.add), reads=['at1', 'at2'], writes=['at1'])
                            P.op('dve', lambda e, s_=s_: e.tensor_tensor(out=kb[s_][:], in0=t1[:], in1=rstd[:], op=ALU.mult), reads=['at1', 'arstd'], writes=[rnk])
                            dma('sp', a.sF(h)[:, u * 512:(u + 1) * 512], kb[s_][:], reads=[rnk], key=rnk)
                        else:
                            P.op('dve', lambda e, b0=b0: e.scalar_tensor_tensor(out=kf[:], in0=ps[b0][:], scalar=gk[:, 0:1], in1=rstd[:], op0=ALU.mult, op1=ALU.mult),
                                 reads=['ps%d' % b0, 'agk', 'arstd'], writes=['akf'])
                            P.op('act', lambda e, s_=s_: e.copy(out=kb[s_][:], in_=kf[:]), reads=['akf'], writes=[rnk])
                            dma('sp', a.KT_p[h], kb[s_][:], reads=[rnk], key=rnk)
                            transposes_out(kf, lambda h=h: stk[:, :, h * 128:(h + 1) * 128], nb(), ['akf'], ('astk', h))
                    if not samp:
                        dma('sp', st_k[j].rearrange("(t p) c -> p t c", p=128), stk[:], reads=[('astk', 0), ('astk', 1)], key='so0')
                    for tc in range(4):
                        b = nb()

                        def mmv(e, tc=tc, b=b):
                            for k in range(KC):
                                ins = e.matmul(ps[b][:, 0:256], hT[:, k, tc * 128:(tc + 1) * 128], wv[:, k, :], start=(k == 0), stop=(k == KC - 1))
                            return ins
                        P.op('pe', mmv, reads=['awv', 'ahT'], writes=['ps%d' % b])
                        if not samp:
                            P.op('act', lambda e, tc=tc, b=b: e.copy(out=stv[:, tc, :], in_=ps[b][:, 0:256]), reads=['ps%d' % b], writes=[('astv', tc)])
                        P.op('dve', lambda e, tc=tc, b=b: e.tensor_copy(out=vb[:, tc, :], in_=ps[b][:, 0:256]), reads=['ps%d' % b], writes=[('avb', tc)])
                    if samp:
                        dma('sp', a.sendV.ap()[u * 512:(u + 1) * 512, :].rearrange("(t p) c -> p t c", p=128), vb[:], reads=[('avb', tc) for tc in range(4)], key='so1')
                    else:
                        dma('sp', st_v[j].rearrange("(t p) c -> p t c", p=128), stv[:], reads=[('astv', tc) for tc in range(4)], key='so2')
                        dma('sp', a.V_p[:, 0:256].rearrange("(t p) c -> p t c", p=128), vb[:], reads=[('avb', tc) for tc in range(4)], key='so1')
                    for c4 in range(4):
                        b = nb()
                        proj(wc, c4 * 128, b, ['awc'])
                        P.op('act', lambda e, c4=c4, b=b: e.copy(out=craw[:, c4, :], in_=ps[b][:]), reads=['ps%d' % b], writes=[('acraw', c4)])
                    rms_rstd([craw[:, c4, :] for c4 in range(4)], 512, sqr, sq, nb(), rstd, 'arstd', [('acraw', c4) for c4 in range(4)])
                    for c4 in range(4):
                        s_, rnk = kbr.next()
                        if samp:
                            P.op('dve', lambda e, c4=c4, s_=s_: e.scalar_tensor_tensor(out=kb[s_][:], in0=craw[:, c4, :], scalar=gkv[:, c4:c4 + 1], in1=rstd[:], op0=ALU.mult, op1=ALU.mult),
                                 reads=[('acraw', c4), 'agkv', 'arstd'], writes=[rnk])
                            dma('sp', a.sF(2 + c4)[:, u * 512:(u + 1) * 512], kb[s_][:], reads=[rnk], key=rnk)
                        else:
                            P.op('dve', lambda e, c4=c4: e.scalar_tensor_tensor(out=kf[:], in0=craw[:, c4, :], scalar=gkv[:, c4:c4 + 1], in1=rstd[:], op0=ALU.mult, op1=ALU.mult),
                                 reads=[('acraw', c4), 'agkv', 'arstd'], writes=['akf'])
                            P.op('act', lambda e, s_=s_: e.copy(out=kb[s_][:], in_=kf[:]), reads=['akf'], writes=[rnk])
                            dma('sp', a.CK_p[c4], kb[s_][:], reads=[rnk], key=rnk)
                            transposes_out(kf, lambda c4=c4: stc[:, :, c4 * 128:(c4 + 1) * 128], nb(), ['akf'], ('astc', c4))
                    if not samp:
                        dma('sp', st_ckv[j].rearrange("(t p) c -> p t c", p=128), stc[:], reads=[('astc', c4) for c4 in range(4)], key='so3')
                    b0 = nb()
                    proj(wr, 0, b0, ['awr_a', 'awr_b'])
                    s_, rnk = kbr.next()
                    if samp:
                        b1 = nb()
                        proj(wrp, 0, b1, wrp_r)
                        P.op('dve', lambda e, b0=b0: e.tensor_tensor(out=t1[:], in0=ps[b0][:], in1=rB[:, 0, :], op=ALU.mult), reads=['ps%d' % b0, 'arB'], writes=['at1'])
                        P.op('dve', lambda e, b1=b1: e.tensor_tensor(out=t2[:], in0=ps[b1][:], in1=rB[:, 1, :], op=ALU.mult), reads=['ps%d' % b1, 'arB'], writes=['at2'])
                        P.op('dve', lambda e, s_=s_: e.tensor_tensor(out=kb[s_][:], in0=t1[:], in1=t2[:], op=ALU.add), reads=['at1', 'at2'], writes=[rnk])
                        dma('sp', a.sF(6)[:, u * 512:(u + 1) * 512], kb[s_][:], reads=[rnk], key=rnk)
                    else:
                        P.op('act', lambda e, b0=b0: e.copy(out=kf[:], in_=ps[b0][:]), reads=['ps%d' % b0], writes=['akf'])
                        P.op('dve', lambda e, s_=s_: e.tensor_copy(out=kb[s_][:], in_=kf[:]), reads=['akf'], writes=[rnk])
                        dma('sp', a.KR_p, kb[s_][:], reads=[rnk], key=rnk)
                        transposes_out(kf, lambda: strr[:], nb(), ['akf'], 'astr', ncol=64)
                        dma('sp', st_kr[j].rearrange("(t p) c -> p t c", p=128), strr[:], reads=['astr'], key='so4')
                P.flush()

        def attn_exchange(L, j, a):
            with ExitStack() as st:
                ckvT = sb(st, "xckvT", [128, 4, 4608], BF16)
                ckp = sb(st, "xckp", [128, 4, 512], BF16)
                wup = sb(st, "xwup", [128, 4, 2048], BF16)
                lt = [sb(st, "xlt%d" % i, [128, 512], F32) for i in range(3)]
                kcs = sb(st, "xkcs", [128, 2, 512], BF16)
                krc = sb(st, "xkrc", [128, 512], BF16)
                vcs = sb(st, "xvcs", [128, 4, 256], BF16)
                knb = [sb(st, "xknb%d" % i, [128, 4608], BF16) for i in range(2)]
                knr = Ring('xknb', 2)
                vbb = [sb(st, "xvbb%d" % i, [128, 1024], BF16) for i in range(2)]
                vbr = Ring('xvbb', 2)
                cc1, cc2 = 'cc1_%d' % j, 'cc2_%d' % j
                groups = [[0, 1], [2, 3], [4, 5], [6, 7]]
                P.op('pool', lambda e: e.collective_compute("AllGather", ALU.bypass, replica_groups=groups,
                                                            ins=[a.sendF1.ap().opt()], outs=[a.recvF1.ap().opt()]),
                     writes=['recvF1'], dma=cc1, inc=1)
                P.op('pool', lambda e: e.collective_compute("AllGather", ALU.bypass, replica_groups=groups,
                                                            ins=[a.sendF2.ap().opt()], outs=[a.recvF2.ap().opt()]),
                     writes=['recvF2'], dma=cc1 + 'b', inc=1)
                P.op('pool', lambda e: e.collective_compute("AllGather", ALU.bypass, replica_groups=groups,
                                                            ins=[a.sendV.ap().opt()], outs=[a.recvV.ap().opt()]),
                     writes=['recvV'], dma=cc2, inc=1)
                dma('pool', wup[:], attn_w_kv_up[j].rearrange("(c p) n -> p c n", p=128), writes=['xwup'], key='w0')
                rV = a.recvV.ap()
                for r in range(2):
                    for h in range(2):
                        dma('sp', a.KT_s[h, :, r * NS:(r + 1) * NS], a.rF(r, h), reads=['recvF1', 'recvF2'], key='d0')
                    dma('sp', a.KR_s[:, r * NS:(r + 1) * NS], a.rF(r, 6), reads=['recvF1', 'recvF2'], key='d0')
                    for c4 in range(4):
                        dma('sp', ckvT[:, c4, r * NS:(r + 1) * NS], a.rF(r, 2 + c4), reads=['recvF1', 'recvF2'],
                            writes=[('xckvT', r, c4)], key='d1')
                    dma('sp', a.V_s[r * NS:(r + 1) * NS, 0:256], rV[r * NS:(r + 1) * NS, :], reads=['recvV'], key='d0')
                dma('sp', ckp[:], a.CK_p.rearrange("c p n -> p c n"), writes=['xckp'], key='d2')
                bank = [0]

                def nb():
                    b = bank[0] % 8
                    bank[0] += 1
                    return b
                for tt in range(4):
                    dma('sp', lt[0][:, 0:256], cache_k[j, tt * 128:(tt + 1) * 128, :], writes=['xlt0'], key='xlt0')
                    b = nb()

                    def trk(e, b=b):
                        for h in range(2):
                            ins = e.transpose(ps[b][:, h * 128:(h + 1) * 128], lt[0][:, h * 128:(h + 1) * 128], ident[:])
                        return ins
                    P.op('pe', trk, reads=['xlt0', 'ident'], writes=['ps%d' % b])
                    P.op('dve', lambda e, tt=tt, b=b: e.tensor_copy(out=kcs[:, :, tt * 128:(tt + 1) * 128], in_=ps[b][:, 0:256].rearrange("p (h c) -> p h c", h=2)),
                         reads=['ps%d' % b], writes=[('xkcs', tt)])
                    dma('sp', lt[1][:], cache_ckv[j, tt * 128:(tt + 1) * 128, :], writes=['xlt1'], key='xlt1')
                    b = nb()

                    def trc(e, b=b):
                        for c4 in range(4):
                            ins = e.transpose(ps[b][:, c4 * 128:(c4 + 1) * 128], lt[1][:, c4 * 128:(c4 + 1) * 128], ident[:])
                        return ins
                    P.op('pe', trc, reads=['xlt1', 'ident'], writes=['ps%d' % b])
                    P.op('dve', lambda e, tt=tt, b=b: e.tensor_copy(out=ckvT[:, :, 2 * NS + tt * 128:2 * NS + (tt + 1) * 128], in_=ps[b][:].rearrange("p (c n) -> p c n", c=4)),
                         reads=['ps%d' % b], writes=[('xckvTc', tt)])
                    P.op('sp', lambda e, tt=tt: e.dma_start(out=lt[2][:, 0:64], in_=cache_kr[j, tt * 128:(tt + 1) * 128, :]), writes=['xlt2'], dma='xlt2')
                    P.op('sp', lambda e, tt=tt: e.dma_start(out=lt[2][:, 64:128], in_=cache_kr[j, tt * 128:(tt + 1) * 128, :]), reads=['xlt2'], writes=['xlt2b'], dma='xlt2')
                    b = nb()
                    P.op('pe', lambda e, b=b: e.transpose(ps[b][:, 0:128], lt[2][:, 0:128], ident[:]), reads=['xlt2b', 'ident'], writes=['ps%d' % b, 'xlt2'])
                    P.op('dve', lambda e, tt=tt, b=b: e.tensor_copy(out=krc[:, tt * 128:(tt + 1) * 128], in_=ps[b][:, 0:128]), reads=['ps%d' % b], writes=[('xkrc', tt)])
                for h in range(2):
                    dma('sp', a.KT_s[h, :, 2 * NS:2 * NS + 512], kcs[:, h, :], reads=[('xkcs', tt) for tt in range(4)], key='d3')
                dma('sp', a.KR_s[:, 2 * NS:2 * NS + 512], krc[:], reads=[('xkrc', tt) for tt in range(4)], key='d3')
                dma('pool', vcs[:], cache_v[j].rearrange("(t p) c -> p t c", p=128), writes=['xvcs'], key='w1')
                dma('sp', a.V_s[2 * NS:2 * NS + 512, 0:256].rearrange("(t p) c -> p t c", p=128), vcs[:], reads=['xvcs'], key='d3')
                srd_s = [('xckvT', r, c4) for r in range(2) for c4 in range(4)] + [('xckvTc', tt) for tt in range(4)]
                for (src, nk, KTd, Vd, srd) in ((ckvT, 4608, a.KT_s, a.V_s, srd_s), (ckp, 512, a.KT_p, a.V_p, ['xckp'])):
                    for h in range(8):
                        s_, rn = knr.next()
                        for kn in range(nk // 512):
                            b = nb()

                            def mk(e, h=h, kn=kn, b=b, src=src):
                                for c4 in range(4):
                                    ins = e.matmul(ps[b][:], wup[:, c4, h * 256:h * 256 + 128], src[:, c4, kn * 512:(kn + 1) * 512], start=(c4 == 0), stop=(c4 == 3))
                                return ins
                            P.op('pe', mk, reads=['xwup'] + srd, writes=['ps%d' % b])
                            if kn % 2 == 0:
                                P.op('dve', lambda e, s_=s_, kn=kn, b=b: e.tensor_copy(out=knb[s_][:, kn * 512:(kn + 1) * 512], in_=ps[b][:]), reads=['ps%d' % b], writes=[rn])
                            else:
                                P.op('act', lambda e, s_=s_, kn=kn, b=b: e.copy(out=knb[s_][:, kn * 512:(kn + 1) * 512], in_=ps[b][:]), reads=['ps%d' % b], writes=[rn])
                        dma('sp', KTd[2 + h, :, 0:nk], knb[s_][:, 0:nk], reads=[rn], key=rn)
                    wv4 = wup[:].rearrange("p c (h t n) -> p c h t n", h=8, t=2)
                    for kt in range(nk // 128):
                        s_, rn = vbr.next()
                        for hg in range(2):
                            b = nb()

                            def mv(e, kt=kt, hg=hg, b=b, src=src):
                                for c4 in range(4):
                                    ins = e.matmul(ps[b][:].rearrange("p (h n) -> p h n", h=4), src[:, c4, kt * 128:(kt + 1) * 128], wv4[:, c4, hg * 4:(hg + 1) * 4, 1, :],
                                                   start=(c4 == 0), stop=(c4 == 3))
                                return ins
                            P.op('pe', mv, reads=['xwup'] + srd, writes=['ps%d' % b])
                            if hg == 0:
                                P.op('dve', lambda e, s_=s_, b=b: e.tensor_copy(out=vbb[s_][:, 0:512], in_=ps[b][:]), reads=['ps%d' % b], writes=[rn])
                            else:
                                P.op('act', lambda e, s_=s_, b=b: e.copy(out=vbb[s_][:, 512:1024], in_=ps[b][:]), reads=['ps%d' % b], writes=[rn])
                        dma('sp', Vd[kt * 128:(kt + 1) * 128, 256:1280], vbb[s_][:], reads=[rn], key=rn)
                P.flush()

        def attn_main(L, j, a):
            W = attn_w_in[j]
            with ExitStack() as st:
                pb = PrepBufs(st)
                eb = EpiBufs(st, 2)
                hT = sb(st, "mhT", [128, KC, 512], BF16)
                oT = sb(st, "moT", [128, KC, 512], BF16)
                ktb = [sb(st, "mkt%d" % i, [128, 4608], BF16) for i in range(2)]
                vtb = [sb(st, "mvt%d" % i, [128, 36, 128], BF16) for i in range(2)]
                krs = sb(st, "mkrs", [128, 4608], BF16)
                krp = sb(st, "mkrp", [128, 512], BF16)
                wq = [sb(st, "mwq%d" % i, [128, KC, 128], BF16) for i in range(4)]
                wqp = [sb(st, "mwqp%d" % i, [128, KC, 128], BF16) for i in range(2)]
                wo = [sb(st, "mwo%d" % i, [128, KC, 256], BF16) for i in range(2)]
                rA = sb(st, "mrA", [128, 2, 512], F32)
                rB = sb(st, "mrB", [128, 2, 512], F32)
                gq = sb(st, "mgq", [128, 1], F32)
                gqp = sb(st, "mgqp", [128, 1], F32)
                sq = [sb(st, "msq%d" % i, [128, 512], F32R) for i in range(2)]
                sqr = Ring('msq', 2)
                rstd = sb(st, "mrstd", [128, 512], F32)
                t1 = sb(st, "mt1", [128, 512], F32)
                t2 = sb(st, "mt2", [128, 512], F32)
                qT = [sb(st, "mqT%d" % i, [128, 512], BF16) for i in range(2)]
                qTr = Ring('mqT', 2)
                qr = [sb(st, "mqr%d" % i, [128, 512], BF16) for i in range(2)]
                qrr = Ring('mqr', 2)
                pT = [sb(st, "mpT%d" % i, [128, 512], BF16) for i in range(3)]
                pTr = Ring('mpT', 3)
                rden = sb(st, "mrden", [128, 512], F32)
                rwq, rwo, rkt, rvt, rwqp = Ring('mwq', 4), Ring('mwo', 2), Ring('mkt', 2), Ring('mvt', 2), Ring('mwqp', 2)
                dma('sp', gq[:], attn_q_norm[j].rearrange("(p o) -> p o", o=1), writes=['mgq'], key='c0', slow=True)
                gqp_r = load_perm_gain(gqp, attn_q_norm[j], 32, 128, 'mgqp')
                dma('sp', krs[:], a.KR_s, writes=['mkrs'], key='c1')
                dma('sp', krp[:], a.KR_p, writes=['mkrp'], key='c2')
                sbank = [0]
                obank = [0]

                def nsb():
                    b = sbank[0] % 4
                    sbank[0] += 1
                    return b

                def proj(wt, slot, b, reads):
                    def f(e):
                        for k in range(KC):
                            ins = e.matmul(ps[b][:], wt[slot][:, k, :], hT[:, k, :], start=(k == 0), stop=(k == KC - 1))
                        return ins
                    P.op('pe', f, reads=reads + ['mhT'], writes=['ps%d' % b])

                def attention(groups, q_ap, q_rd, kt_slot, kt_rd, vt_slot, vt_rd, scale, chunk, rope=None):
                    for (q0_, nq_, tiles_) in groups:
                        do_group(q0_, nq_, tiles_, q_ap, q_rd, kt_slot, kt_rd, vt_slot, vt_rd, scale, chunk, rope)

                def do_group(q0, nq, tiles, q_ap, q_rd, kt_slot, kt_rd, vt_slot, vt_rd, scale, chunk, rope):
                    if True:
                        ob = 4
                        db = 5
                        nt = len(tiles)

                        def score(idx):
                            kt = tiles[idx]
                            b = nsb()

                            def f(e):
                                ins = e.matmul(ps[b][:, 0:nq], ktb[kt_slot][:, kt * 128:(kt + 1) * 128], q_ap[:, q0:q0 + nq], start=True, stop=(rope is None))
                                if rope is not None:
                                    qrt, _, hp, krt, _ = rope
                                    ins = e.matmul(ps[b][:, 0:nq], krt[hp * 64:(hp + 1) * 64, kt * 128:(kt + 1) * 128], qrt[hp * 64:(hp + 1) * 64, q0:q0 + nq],
                                                   start=False, stop=True)
                                return ins
                            rds = [kt_rd, q_rd] + ([rope[1], rope[4]] if rope is not None else [])
                            P.op('pe', f, reads=rds, writes=['ps%d' % b])
                            return b
                        pend = [score(0)]
                        if nt > 1:
                            pend.append(score(1))
                        for idx in range(nt):
                            b = pend.pop(0)
                            if idx + 2 < nt:
                                pend.append(score(idx + 2))
                            s_, rnp = pTr.next()
                            P.op('act', lambda e, b=b, s_=s_: e.activation(out=pT[s_][:, 0:nq], in_=ps[b][:, 0:nq], func=AF.Exp, scale=scale),
                                 reads=['ps%d' % b], writes=[rnp])
                            kt = tiles[idx]

                            def pv(e, s_=s_, kt=kt, idx=idx):
                                e.matmul(ps[ob][:, 0:nq], vtb[vt_slot][:, kt, :], pT[s_][:, 0:nq], start=(idx == 0), stop=(idx == nt - 1))
                                return e.matmul(ps[db][:, 0:nq], ones_b[:], pT[s_][:, 0:nq], start=(idx == 0), stop=(idx == nt - 1))
                            P.op('pe', pv, reads=[rnp, vt_rd, 'ones_b'], writes=['ps%d' % ob, 'ps%d' % db])
                        P.op('dve', lambda e: e.reciprocal(out=rden[:, 0:nq], in_=ps[db][:, 0:nq]), reads=['ps%d' % db], writes=['mrden'])
                        P.op('dve', lambda e: e.tensor_tensor(out=oT[:, chunk, q0:q0 + nq], in0=ps[ob][:, 0:nq], in1=rden[:, 0:nq], op=ALU.mult),
                             reads=['ps%d' % ob, 'mrden'], writes=[('moT', chunk)])

                for u in range(NU):
                    samp = u < 4
                    prep_unit(pb, u, 0, lambda k: hT[:, k, :], 'mhT', nsb())
                    if samp:
                        dma('sp', rA[:], ropeA[:, :, u * 512:(u + 1) * 512].rearrange("t p n -> p t n"), writes=['mrA'], key='c3')
                        dma('sp', rB[:], ropeB[:, :, u * 512:(u + 1) * 512].rearrange("t p n -> p t n"), writes=['mrB'], key='c4')
                        groups = [(0, 512, list(range(36)))]
                        KT, VV, nk = a.KT_s, a.V_s, 4608
                        krt, kr_rd = krs, 'mkrs'
                    else:
                        groups = [(0, 256, [0, 1]), (256, 256, [2, 3])]
                        KT, VV, nk = a.KT_p, a.V_p, 512
                        krt, kr_rd = krp, 'mkrp'
                    kvl = {}
                    wql = {}
                    lstate = {'last_kind': None, 'kv': None}

                    def hinfo(hh):
                        isA = hh < 8
                        h = hh if isA else hh - 8
                        kind = (h // 4) if isA else 2 + h
                        return isA, h, kind

                    def issue_kv(hh, KT=KT, VV=VV, nk=nk, kvl=kvl, lstate=lstate):
                        isA, h, kind = hinfo(hh)
                        if kind != lstate['last_kind']:
                            ks, krn = rkt.next()
                            vs, vrn = rvt.next()
                            dma('sp', ktb[ks][:, 0:nk], KT[kind, :, 0:nk], writes=[krn], key=krn)
                            dma('sp', vtb[vs][:, 0:nk // 128, :], VV[0:nk, kind * 128:(kind + 1) * 128].rearrange("(t p) c -> p t c", p=128), writes=[vrn], key=vrn)
                            lstate['kv'] = (ks, krn, vs, vrn)
                            lstate['last_kind'] = kind
                        kvl[hh] = lstate['kv']

                    def issue_wq(hh, wql=wql):
                        isA, h, kind = hinfo(hh)
                        dd = {}
                        ws, wrn = rwq.next()
                        c0 = h * 128 if isA else 1536 + h * 192
                        dma('pool', wq[ws][:], W[:, c0:c0 + 128].rearrange("(k p) n -> p k n", p=128), writes=[wrn], key=wrn)
                        dd['wq'] = (ws, wrn)
                        if (not isA) and h % 2 == 0:
                            ws2, wrn2 = rwq.next()
                            for i2 in range(2):
                                cr = 1536 + (h + i2) * 192 + 128
                                dma('pool', wq[ws2][:, :, i2 * 64:(i2 + 1) * 64], W[:, cr:cr + 64].rearrange("(k p) n -> p k n", p=128),
                                    writes=[wrn2], key=wrn2)
                            dd['wq2'] = (ws2, wrn2)
                        wql[hh] = dd

                    qst = {'cur_qr': None}

                    def qprep(hh, samp=samp, wql=wql, qst=qst):
                        isA, h, kind = hinfo(hh)
                        ws, wrn = wql[hh]['wq']
                        proj(wq, ws, 6, [wrn])
                        qs, qrn = qTr.next()
                        if isA:
                            if samp:
                                wps, wprn = rwqp.next()
                                pr = swap_copy('dve', wqp[wps][:], wq[ws][:], 32, [wrn], wprn)
                                proj(wqp, wps, 7, pr)
                                P.op('dve', lambda e: e.scalar_tensor_tensor(out=t1[:], in0=ps[6][:], scalar=gq[:, 0:1], in1=rA[:, 0, :], op0=ALU.mult, op1=ALU.mult),
                                     reads=['ps6', 'mgq', 'mrA'], writes=['mt1'])
                            else:
                                P.op('dve', lambda e: e.tensor_scalar(out=t1[:], in0=ps[6][:], scalar1=gq[:, 0:1], scalar2=None, op0=ALU.mult),
                                     reads=['ps6', 'mgq'], writes=['mt1'])
                            rms_rstd([ps[6][:]], 128, sqr, sq, 6, rstd, 'mrstd', ['ps6'])
                            if samp:
                                P.op('dve', lambda e: e.scalar_tensor_tensor(out=t2[:], in0=ps[7][:], scalar=gqp[:, 0:1], in1=rA[:, 1, :], op0=ALU.mult, op1=ALU.mult),
                                     reads=['ps7', 'mrA'] + gqp_r, writes=['mt2'])
                                P.op('dve', lambda e: e.tensor_tensor(out=t1[:], in0=t1[:], in1=t2[:], op=ALU.add), reads=['mt1', 'mt2'], writes=['mt1'])
                            P.op('dve', lambda e, qs=qs: e.tensor_tensor(out=qT[qs][:], in0=t1[:], in1=rstd[:], op=ALU.mult), reads=['mt1', 'mrstd'], writes=[qrn])
                            return (qs, qrn, None)
                        P.op('act', lambda e, qs=qs: e.copy(out=qT[qs][:], in_=ps[6][:]), reads=['ps6'], writes=[qrn])
                        if h % 2 == 0:
                            ws2, wrn2 = wql[hh]['wq2']
                            proj(wq, ws2, 7, [wrn2])
                            rs_, rrn = qrr.next()
                            if samp:
                                wps, wprn = rwqp.next()
                                pr = swap_copy('dve', wqp[wps][:], wq[ws2][:], 16, [wrn2], wprn)
                                proj(wqp, wps, 6, pr)
                                P.op('dve', lambda e: e.tensor_tensor(out=t1[:], in0=ps[7][:], in1=rB[:, 0, :], op=ALU.mult), reads=['ps7', 'mrB'], writes=['mt1'])
                                P.op('dve', lambda e: e.tensor_tensor(out=t2[:], in0=ps[6][:], in1=rB[:, 1, :], op=ALU.mult), reads=['ps6', 'mrB'], writes=['mt2'])
                                P.op('dve', lambda e, rs_=rs_: e.tensor_tensor(out=qr[rs_][:], in0=t1[:], in1=t2[:], op=ALU.add), reads=['mt1', 'mt2'], writes=[rrn])
                            else:
                                P.op('act', lambda e, rs_=rs_: e.copy(out=qr[rs_][:], in_=ps[7][:]), reads=['ps7'], writes=[rrn])
                            qst['cur_qr'] = (rs_, rrn)
                        return (qs, qrn, qst['cur_qr'])

                    issue_wq(0)
                    issue_wq(1)
                    issue_kv(0)
                    qinfo = {0: qprep(0)}
                    for hh in range(16):
                        isA, h, kind = hinfo(hh)
                        if hh + 2 < 16:
                            issue_wq(hh + 2)
                        if hh + 1 < 16:
                            issue_kv(hh + 1)
                            qinfo[hh + 1] = qprep(hh + 1)
                        ks, krn, vs, vrn = kvl[hh]
                        qs, qrn, cq = qinfo[hh]
                        if isA:
                            attention(groups, qT[qs], qrn, ks, krn, vs, vrn, 128.0 ** -0.5, h)
                        else:
                            attention(groups, qT[qs], qrn, ks, krn, vs, vrn, 192.0 ** -0.5, 8 + h,
                                      rope=(qr[cq[0]], cq[1], h % 2, krt, kr_rd))
                    for mo2 in range(8):
                        so, rno = rwo.next()
                        dma('pool', wo[so][:], attn_w_out[j][:, mo2 * 256:(mo2 + 1) * 256].rearrange("(k p) n -> p k n", p=128), writes=[rno], key=rno)
                        for ml in range(2):
                            mo = mo2 * 2 + ml
                            b = nsb()
                            ld = epi_load(eb, mo, u)

                            def mmo(e, so=so, ml=ml, b=b):
                                for k in range(KC):
                                    ins = e.matmul(ps[b][:], wo[so][:, k, ml * 128:(ml + 1) * 128], oT[:, k, :], start=(k == 0), stop=(k == KC - 1))
                                return ins
                            P.op('pe', mmo, reads=[rno] + [('moT', c) for c in range(KC)], writes=['ps%d' % b])
                            epi_finish(eb, ld, mo, u, 2, b)
                P.flush()

        def block_final():
            with ExitStack() as st:
                xg = [sb(st, "fxg%d" % i, [128, KC, 128], F32) for i in range(2)]
                sq = [sb(st, "fsq%d" % i, [128, 128], F32) for i in range(2)]
                rstd = [sb(st, "frs%d" % i, [128, 128], F32) for i in range(2)]
                yn = [sb(st, "fyn%d" % i, [128, KC, 128], F32) for i in range(2)]
                yo = [sb(st, "fyo%d" % i, [128, D], F32) for i in range(2)]
                rq = Ring('fsq', 2)
                for t in range(NT // 128):
                    s = t % 2
                    u = t // 4
                    dma('sp', xg[s][:], xT[:, :, t * 128:(t + 1) * 128].rearrange("k p n -> p k n"),
                        reads=[('xT', k, u) for k in range(KC)], writes=['fxg%d' % s], key='fxg%d' % s)
                    for k in range(KC):
                        q, rn = rq.next()
                        P.op('act', lambda e, k=k, q=q, s=s: e.activation(out=sq[q][:], in_=xg[s][:, k, :], func=AF.Square), reads=['fxg%d' % s], writes=[rn])
                        P.op('pe', lambda e, k=k, q=q: e.matmul(ps[0][:, 0:128], ones_f[:], sq[q][:], start=(k == 0), stop=(k == KC - 1)),
                             reads=[rn, 'ones_f'], writes=['ps0'])
                    P.op('act', lambda e, s=s: e.activation(out=rstd[s][:], in_=ps[0][:, 0:128], func=AF.Sqrt, scale=1.0 / D, bias=epsb[:]),
                         reads=['ps0', 'epsb'], writes=['frs%d' % s])
                    P.op('dve', lambda e, s=s: e.reciprocal(out=rstd[s][:], in_=rstd[s][:]), reads=['frs%d' % s], writes=['frs%d' % s])
                    for k in range(KC):
                        P.op('dve', lambda e, k=k, s=s: e.scalar_tensor_tensor(out=yn[s][:, k, :], in0=xg[s][:, k, :], scalar=fnw[:, k:k + 1], in1=rstd[s][:],
                                                                               op0=ALU.mult, op1=ALU.mult),
                             reads=['fxg%d' % s, 'fnw', 'frs%d' % s], writes=[('fyn', s, k // 4)])
                    for g in range(4):
                        b = 1 + (t * 4 + g) % 7

                        def tr(e, s=s, g=g, b=b):
                            for i in range(4):
                                ins = e.transpose(ps[b][:, i * 128:(i + 1) * 128], yn[s][:, g * 4 + i, :], ident[:])
                            return ins
                        P.op('pe', tr, reads=[('fyn', s, g), 'ident'], writes=['ps%d' % b])
                        if g % 2 == 0:
                            P.op('dve', lambda e, s=s, g=g, b=b: e.tensor_copy(out=yo[s][:, g * 512:(g + 1) * 512], in_=ps[b][:]), reads=['ps%d' % b], writes=[('fyo', s, g)])
                        else:
                            P.op('act', lambda e, s=s, g=g, b=b: e.copy(out=yo[s][:, g * 512:(g + 1) * 512], in_=ps[b][:]), reads=['ps%d' % b], writes=[('fyo', s, g)])
                    dma('sp', y_out[t * 128:(t + 1) * 128, :], yo[s][:], reads=[('fyo', s, g) for g in range(4)], key='fyo%d' % s)
                P.flush()

        block_init()
        ada_done = [False] * (DEPTH + 1)
        for L in range(DEPTH):
            if not need_layer[L]:
                continue
            if not ada_done[L]:
                block_ada(L)
                ada_done[L] = True
            cur[0] = L % 2
            if want("mix%d" % L):
                if L % 2 == 1:
                    block_cmlp(L)
                else:
                    block_attn(L)
            if want("ffn%d" % L):
                block_ffn(L)
                if L + 1 < DEPTH and need_layer[L + 1]:
                    ada_done[L + 1] = True
        block_final()
    return nc


def rope_tables(hf):
    t = np.arange(NS) + hf * NS
    row = (t // 64).astype(np.float32)
    col = (t % 64).astype(np.float32)

    def tab(width):
        half = width // 2
        m = half // 2
        inv = (1.0 / (np.float32(10000.0) ** (np.arange(0, half, 2, dtype=np.float32) / np.float32(half)))).astype(np.float32)
        C = np.zeros((width, NS), np.float32)
        S = np.zeros((width, NS), np.float32)
        for p in range(width):
            pos = row if p < half else col
            q = p % half
            f = q % m
            ang = (pos * inv[f]).astype(np.float32)
            C[p] = np.cos(ang)
            S[p] = np.sin(ang) * (-1.0 if q < m else 1.0)
        return C, S
    CA, SA = tab(128)
    CB, SB = tab(64)
    ra = np.stack([CA, SA]).astype(np.float32)
    rb = np.stack([np.concatenate([CB, CB]), np.concatenate([SB, SB])]).astype(np.float32)
    return ra, rb


_CACHE = {}


def kernel(**inputs):
    inp = {k: np.ascontiguousarray(np.asarray(v)) for k, v in inputs.items()}
    import os
    cfg = {"stages": STAGES, "attn_parts": os.environ.get("ATT_PARTS", "123")}
    key = str(STAGES) + cfg["attn_parts"]
    if key not in _CACHE:
        _CACHE[key] = build_program(cfg)
    nc = _CACHE[key]
    def want(name):
        return STAGES is None or name in STAGES
    shared = {k: inp[k] for k in ("ada_b", "norm_mix", "norm_ffn", "attn_q_norm", "attn_k_norm", "attn_kv_norm",
                                  "cmlp_v_norm", "cmlp_w_s", "cmlp_b_s", "final_norm")}
    for L in range(DEPTH):
        if want("mix%d" % L) or want("ffn%d" % L):
            shared["ada_w%d" % L] = inp["ada_w"][L]
        if want("ffn%d" % L):
            shared["ffn_gate%d" % L] = inp["ffn_gate"][L]
            shared["ffn_up%d" % L] = inp["ffn_up"][L]
            shared["ffn_down%d" % L] = inp["ffn_down"][L]
    for j in range(2):
        if want("mix%d" % (2 * j)):
            shared["attn_w_in%d" % j] = inp["attn_w_in"][j]
            shared["attn_w_kv_up%d" % j] = inp["attn_w_kv_up"][j]
            shared["attn_w_out%d" % j] = inp["attn_w_out"][j]
        if want("mix%d" % (2 * j + 1)):
            shared["cmlp_w_in%d" % j] = inp["cmlp_w_in"][j]
            shared["cmlp_w_out%d" % j] = inp["cmlp_w_out"][j]
    ident = np.eye(128, dtype=np.float32)
    in_maps = []
    for c in range(8):
        b, hf = c // 2, c % 2
        xs = inp["x_sample"][b, hf * NS:(hf + 1) * NS]
        xp = inp["x_prompt"][2 * c:2 * c + 2].reshape(NPR, D)
        cond = np.stack([inp["c"][b], inp["c_ctx"]])
        condT = np.ascontiguousarray(cond.reshape(2, KC, 128).transpose(2, 1, 0))
        ra, rb = rope_tables(hf)
        m = dict(shared)
        m.update({
            "xin": np.ascontiguousarray(np.concatenate([xs, xp], axis=0)),
            "condT": condT,
            "cache_k": np.ascontiguousarray(inp["cache_gqa_k"][b].reshape(2, 512, 256)),
            "cache_v": np.ascontiguousarray(inp["cache_gqa_v"][b].reshape(2, 512, 256)),
            "cache_ckv": np.ascontiguousarray(inp["cache_mla_ckv"][b]),
            "cache_kr": np.ascontiguousarray(inp["cache_mla_krope"][b]),
            "ropeA": ra, "ropeB": rb, "ident_in": ident,
        })
        in_maps.append(m)
    res = run_bass_kernel_spmd(nc, in_maps, core_ids=list(range(8)))
    y_prompt = np.zeros((16, 256, D), np.float32)
    y_sample = np.zeros((4, 4096, D), np.float32)
    s_k = np.zeros((16, 2, 256, 2, 128), np.float32)
    s_v = np.zeros((16, 2, 256, 2, 128), np.float32)
    s_ckv = np.zeros((16, 2, 256, 512), np.float32)
    s_kr = np.zeros((16, 2, 256, 64), np.float32)
    for c in range(8):
        r = res.results[c]
        b, hf = c // 2, c % 2
        y = r["y_out"]
        y_sample[b, hf * NS:(hf + 1) * NS] = y[:NS]
        y_prompt[2 * c:2 * c + 2] = y[NS:].reshape(2, 256, D)
        for j in range(2):
            s_k[2 * c:2 * c + 2, j] = r["st_k"][j].reshape(2, 256, 2, 128)
            s_v[2 * c:2 * c + 2, j] = r["st_v"][j].reshape(2, 256, 2, 128)
            s_ckv[2 * c:2 * c + 2, j] = r["st_ckv"][j].reshape(2, 256, 512)
            s_kr[2 * c:2 * c + 2, j] = r["st_kr"][j].reshape(2, 256, 64)
    return (y_prompt, y_sample, s_k, s_v, s_ckv, s_kr)
```

```python
import numpy as np
from contextlib import ExitStack
import concourse.bass as bass
import concourse.mybir as mybir
from concourse.bass_utils import run_bass_kernel_spmd

F32 = mybir.dt.float32
BF16 = mybir.dt.bfloat16
F32R = mybir.dt.float32r
AF = mybir.ActivationFunctionType
ALU = mybir.AluOpType

D = 2048
KC = 16
NT = 2560
NS = 2048
NPR = 512
NU = 5
FF = 5632
JC = 44
EPS = 1e-6
ATTN_IN = 3648
DEPTH = 4

STAGES = None


class Prog:
    ENG = ('pe', 'act', 'dve', 'pool', 'sp')

    def __init__(self, nc, stack):
        self.nc = nc
        self.stack = stack
        self.sems = []
        self.sem_cnt = []
        self.esem = {}
        for e in self.ENG[:4]:
            self.esem[e] = self.new_sem("c_" + e)
        self.dma_sems = {}
        self.q = {e: [] for e in self.ENG}
        self.waited = {e: {} for e in self.ENG}
        self.last_w = {}
        self.readers = {}
        self.nops = 0

    def new_sem(self, name):
        h = self.stack.enter_context(self.nc.semaphore(name))
        self.sems.append(h)
        self.sem_cnt.append(0)
        return len(self.sems) - 1

    def dsem(self, key):
        if key not in self.dma_sems:
            self.dma_sems[key] = self.new_sem("d_%d" % len(self.dma_sems))
        return self.dma_sems[key]

    def op(self, eng, fn, reads=(), writes=(), dma=None, inc=None):
        psr = [r for r in reads if isinstance(r, str) and r.startswith('ps')]
        if psr:
            reads = [r for r in reads if r not in psr]
            writes = list(writes) + psr
        deps = []
        for r in reads:
            w = self.last_w.get(r)
            if w:
                deps.append(w)
        for r in writes:
            w = self.last_w.get(r)
            if w:
                deps.append(w)
            deps.extend(self.readers.get(r, ()))
        if dma is None:
            si = self.esem[eng]
            inc = 1
        else:
            si = self.dsem(dma)
            inc = 16 if inc is None else inc
        self.sem_cnt[si] += inc
        sig = (si, self.sem_cnt[si])
        need = {}
        for (s, v) in deps:
            if eng == 'pe' and s == self.esem['pe']:
                continue
            if self.waited[eng].get(s, 0) < v:
                need[s] = max(need.get(s, 0), v)
        for s, v in need.items():
            self.waited[eng][s] = v
        self.q[eng].append((list(need.items()), fn, si, inc))
        for r in reads:
            self.readers.setdefault(r, []).append(sig)
        for r in writes:
            self.last_w[r] = sig
            self.readers[r] = []
        self.nops += 1
        return sig

    def flush(self):
        nc = self.nc
        need = []
        for si in range(len(self.sems)):
            v = self.sem_cnt[si]
            if v > 0 and self.waited['sp'].get(si, 0) < v:
                need.append((si, v))
                self.waited['sp'][si] = v
        self.q['sp'].append((need, None, None, 0))
        q = self.q
        self.q = {e: [] for e in self.ENG}
        sems = self.sems

        def body(lst):
            def f(e):
                for (waits, fn, si, inc) in lst:
                    for (s, v) in waits:
                        e.wait_ge(sems[s], v)
                    if fn is not None:
                        ins = fn(e)
                        ins.then_inc(sems[si], inc)
            return f
        with nc.Block() as block:
            block.sync(body(q['sp']))
            block.tensor(body(q['pe']))
            block.scalar(body(q['act']))
            block.vector(body(q['dve']))
            block.gpsimd(body(q['pool']))
        self.last_w = {}
        self.readers = {}


class Ring:
    def __init__(self, name, n):
        self.name = name
        self.n = n
        self.i = 0

    def next(self):
        s = self.i % self.n
        self.i += 1
        return s, "%s%d" % (self.name, s)


def build_program(cfg):
    nc = bass.Bass("TRN2", target_bir_lowering=False)

    def din(name, shape):
        return nc.dram_tensor(name, list(shape), F32, kind="ExternalInput").ap()

    def dout(name, shape):
        return nc.dram_tensor(name, list(shape), F32, kind="ExternalOutput").ap()

    xin = din("xin", [NT, D])
    condT = din("condT", [128, KC, 2])
    cache_k = din("cache_k", [2, 512, 256])
    cache_v = din("cache_v", [2, 512, 256])
    cache_ckv = din("cache_ckv", [2, 512, 512])
    cache_kr = din("cache_kr", [2, 512, 64])
    stages = cfg.get("stages")

    def want(name):
        return stages is None or name in stages
    need_layer = [want("mix%d" % L) or want("ffn%d" % L) for L in range(DEPTH)]
    ada_w = [din("ada_w%d" % L, [D, 6 * D]) if need_layer[L] else None for L in range(DEPTH)]
    ada_b = din("ada_b", [DEPTH, 6 * D])
    norm_mix = din("norm_mix", [DEPTH, D])
    norm_ffn = din("norm_ffn", [DEPTH, D])
    ffn_gate = [din("ffn_gate%d" % L, [D, FF]) if want("ffn%d" % L) else None for L in range(DEPTH)]
    ffn_up = [din("ffn_up%d" % L, [D, FF]) if want("ffn%d" % L) else None for L in range(DEPTH)]
    ffn_down = [din("ffn_down%d" % L, [FF, D]) if want("ffn%d" % L) else None for L in range(DEPTH)]
    attn_w_in = [din("attn_w_in%d" % j, [D, ATTN_IN]) if want("mix%d" % (2 * j)) else None for j in range(2)]
    attn_q_norm = din("attn_q_norm", [2, 128])
    attn_k_norm = din("attn_k_norm", [2, 128])
    attn_kv_norm = din("attn_kv_norm", [2, 512])
    attn_w_kv_up = [din("attn_w_kv_up%d" % j, [512, 2048]) if want("mix%d" % (2 * j)) else None for j in range(2)]
    attn_w_out = [din("attn_w_out%d" % j, [D, D]) if want("mix%d" % (2 * j)) else None for j in range(2)]
    cmlp_w_in = [din("cmlp_w_in%d" % j, [D, 2 * D]) if want("mix%d" % (2 * j + 1)) else None for j in range(2)]
    cmlp_v_norm = din("cmlp_v_norm", [2, D])
    cmlp_w_s = din("cmlp_w_s", [2, 8, 128, 128])
    cmlp_b_s = din("cmlp_b_s", [2, 8, 128])
    cmlp_w_out = [din("cmlp_w_out%d" % j, [D, D]) if want("mix%d" % (2 * j + 1)) else None for j in range(2)]
    final_norm = din("final_norm", [D])
    ropeA = din("ropeA", [2, 128, NS])
    ropeB = din("ropeB", [2, 128, NS])
    ident_in = din("ident_in", [128, 128])

    y_out = dout("y_out", [NT, D])
    st_k = dout("st_k", [2, NPR, 256])
    st_v = dout("st_v", [2, NPR, 256])
    st_ckv = dout("st_ckv", [2, NPR, 512])
    st_kr = dout("st_kr", [2, NPR, 64])

    xT = nc.dram_tensor("xT_scratch", [KC, 128, NT], F32).ap()

    with ExitStack() as top:
        P = Prog(nc, top)
        ps = [top.enter_context(nc.psum_tensor("ps%d" % i, [128, 512], F32)) for i in range(8)]

        uniq = [0]

        def sb(stack, name, shape, dt):
            uniq[0] += 1
            return stack.enter_context(nc.sbuf_tensor("%s_%d" % (name, uniq[0]), list(shape), dt))

        ident = sb(top, "ident", [128, 128], F32)
        ones_f = sb(top, "ones_f", [128, 128], F32)
        ones_r = sb(top, "ones_r", [128, 128], F32R)
        ones_b = sb(top, "ones_b", [128, 128], BF16)
        epsb = sb(top, "epsb", [128, 1], F32)
        scond = sb(top, "scond", [128, KC, 2], BF16)
        modT2 = [sb(top, "modT%d" % i, [128, 96, 2], F32) for i in range(2)]
        coefA2 = [sb(top, "coefA%d" % i, [128, 2, KC, 2], F32) for i in range(2)]
        nrmw2 = [sb(top, "nrmw%d" % i, [128, 2, KC], F32) for i in range(2)]
        cur = [0]
        fnw = sb(top, "fnw", [128, KC], F32)

        def dma(eng, out, in_, reads=(), writes=(), key=None, slow=False):
            if slow:
                return P.op(eng, lambda e: e.dma_start(out=out, in_=in_, allow_slow_non_contiguous=True), reads=reads, writes=writes, dma=key)
            return P.op(eng, lambda e: e.dma_start(out=out, in_=in_), reads=reads, writes=writes, dma=key)

        def mod_ap(split, k, c):
            return modT2[cur[0]][:, split * 16 + k, c:c + 1]

        def unit_cond(u):
            return 0 if u < 4 else 1

        def block_init():
            with ExitStack() as st:
                xt = [sb(st, "xt%d" % i, [128, D], F32) for i in range(2)]
                xo = [sb(st, "xo%d" % i, [128, KC, 128], F32) for i in range(2)]
                cnd = sb(st, "cnd", [128, KC, 2], F32)
                dma('sp', ident[:], ident_in[:, :], writes=['ident'], key='c0')
                P.op('dve', lambda e: e.memset(ones_f[:], 1.0), writes=['ones_f'])
                P.op('dve', lambda e: e.tensor_copy(out=ones_r[:], in_=ones_f[:]), reads=['ones_f'], writes=['ones_r'])
                P.op('dve', lambda e: e.memset(ones_b[:], 1.0), writes=['ones_b'])
                P.op('dve', lambda e: e.memset(epsb[:], EPS), writes=['epsb'])
                dma('sp', cnd[:], condT[:, :, :], writes=['cnd'], key='c1')
                P.op('act', lambda e: e.activation(out=scond[:], in_=cnd[:], func=AF.Silu), reads=['cnd'], writes=['scond'])
                dma('sp', fnw[:], final_norm.rearrange("(k p) -> p k", p=128), writes=['fnw'], key='c2', slow=True)
                for t in range(NT // 128):
                    s = t % 2
                    dma('sp', xt[s][:], xin[t * 128:(t + 1) * 128, :], writes=['xt%d' % s], key='xt%d' % s)
                    for g in range(4):
                        b = (t * 4 + g) % 8

                        def tr(e, s=s, g=g, b=b):
                            for i in range(4):
                                k = g * 4 + i
                                ins = e.transpose(ps[b][:, i * 128:(i + 1) * 128], xt[s][:, k * 128:(k + 1) * 128], ident[:])
                            return ins
                        P.op('pe', tr, reads=['xt%d' % s, 'ident'], writes=['ps%d' % b])
                        eng = 'dve' if g % 2 == 0 else 'act'
                        if eng == 'dve':
                            P.op('dve', lambda e, s=s, g=g, b=b: e.tensor_copy(out=xo[s][:, g * 4:(g + 1) * 4, :], in_=ps[b][:].rearrange("p (a n) -> p a n", a=4)),
                                 reads=['ps%d' % b], writes=[('xo', s, g)])
                        else:
                            P.op('act', lambda e, s=s, g=g, b=b: e.copy(out=xo[s][:, g * 4:(g + 1) * 4, :], in_=ps[b][:].rearrange("p (a n) -> p a n", a=4)),
                                 reads=['ps%d' % b], writes=[('xo', s, g)])
                    u = t // 4
                    dma('sp', xT[:, :, t * 128:(t + 1) * 128].rearrange("k p n -> p k n"), xo[s][:],
                        reads=[('xo', s, g) for g in range(4)], writes=[('xTt', t)], key='xo%d' % s)
                P.flush()

        def ada_steps(L, st, bank):
            par = L % 2
            modT, coefA, nrmw = modT2[par], coefA2[par], nrmw2[par]
            wa = [sb(st, "wa%d" % i, [128, KC, 256], BF16) for i in range(2)]
            adab = sb(st, "adab", [128, 96], F32)
            NSL = 48
            names = ['adawa0', 'adawa1']

            def load(s_):
                dma('pool', wa[s_ % 2][:], ada_w[L][:, s_ * 256:(s_ + 1) * 256].rearrange("(k p) n -> p k n", p=128),
                    writes=[names[s_ % 2]], key=names[s_ % 2])

            def first():
                dma('sp', adab[:], ada_b[L].rearrange("(m p) -> p m", p=128), writes=['adab'], key='adac0', slow=True)
                dma('sp', nrmw[:, 0, :], norm_mix[L].rearrange("(k p) -> p k", p=128), writes=['nrmwA%d' % par], key='adac1', slow=True)
                dma('sp', nrmw[:, 1, :], norm_ffn[L].rearrange("(k p) -> p k", p=128), writes=['nrmwB%d' % par], key='adac2', slow=True)
                load(0)

            def step(s_):
                def f():
                    if s_ + 1 < NSL:
                        load(s_ + 1)

                    def mm(e):
                        for mi in range(2):
                            m = s_ * 2 + mi
                            for k in range(KC):
                                ins = e.matmul(ps[bank][:, m * 2:m * 2 + 2], wa[s_ % 2][:, k, mi * 128:(mi + 1) * 128], scond[:, k, :],
                                               start=(k == 0), stop=(k == KC - 1))
                        return ins
                    P.op('pe', mm, reads=[names[s_ % 2], 'scond'], writes=['ps%d' % bank])
                return f

            def last():
                P.op('dve', lambda e: e.tensor_tensor(out=modT[:], in0=ps[bank][:, 0:192].rearrange("p (m c) -> p m c", c=2),
                                                      in1=adab[:].unsqueeze(2).to_broadcast([128, 96, 2]), op=ALU.add),
                     reads=['ps%d' % bank, 'adab'], writes=['modT%d' % par])
                for sub in range(2):
                    sc_split = 1 if sub == 0 else 4
                    for c in range(2):
                        P.op('dve', lambda e, sub=sub, c=c, sc_split=sc_split: e.scalar_tensor_tensor(
                            out=coefA[:, sub, :, c], in0=modT[:, sc_split * 16:(sc_split + 1) * 16, c], scalar=1.0,
                            in1=nrmw[:, sub, :], op0=ALU.add, op1=ALU.mult),
                            reads=['modT%d' % par, 'nrmwA%d' % par, 'nrmwB%d' % par], writes=['coefA%d' % par])
            return [first] + [step(s_) for s_ in range(NSL)] + [last]

        def block_ada(L):
            with ExitStack() as st:
                for f in ada_steps(L, st, 0):
                    f()
                P.flush()

        class PrepBufs:
            def __init__(self, st, nh=1):
                self.xg = sb(st, "xg", [128, KC, 512], F32)
                self.sq = [sb(st, "sq%d" % i, [128, 512], F32R) for i in range(2)]
                self.rstd = sb(st, "rstd", [128, 512], F32)
                self.tmp = [sb(st, "ptmp%d" % i, [128, 512], F32) for i in range(2)]
                self.sqr = Ring('sq', 2)
                self.tmr = Ring('ptmp', 2)

        def prep_unit(pb, u, sub, hT_ap, hname, psb, next_u=None):
            c = unit_cond(u)

            def load_x(uu):
                for g in range(4):
                    dma('sp', pb.xg[:, g * 4:(g + 1) * 4, :], xT[g * 4:(g + 1) * 4, :, uu * 512:(uu + 1) * 512].rearrange("k p n -> p k n"),
                        reads=[('xT', k, uu) for k in range(g * 4, g * 4 + 4)], writes=[('xg', g)], key='xg%d' % g)
            if getattr(pb, 'preloaded', None) != u:
                load_x(u)
            pb.preloaded = None
            for k in range(KC):
                s, rn = pb.sqr.next()
                P.op('act', lambda e, k=k, s=s: e.activation(out=pb.sq[s][:], in_=pb.xg[:, k, :], func=AF.Square),
                     reads=[('xg', k // 4)], writes=[rn])
                P.op('pe', lambda e, k=k, s=s: e.matmul(ps[psb][:], ones_r[:], pb.sq[s][:], start=(k == 0), stop=(k == KC - 1)),
                     reads=[rn, 'ones_r'], writes=['ps%d' % psb])
            P.op('act', lambda e: e.activation(out=pb.rstd[:], in_=ps[psb][:], func=AF.Sqrt, scale=1.0 / D, bias=epsb[:]),
                 reads=['ps%d' % psb, 'epsb'], writes=['rstd'])
            P.op('dve', lambda e: e.reciprocal(out=pb.rstd[:], in_=pb.rstd[:]), reads=['rstd'], writes=['rstd'])
            sh_split = 0 if sub == 0 else 3
            for k in range(KC):
                s, rn = pb.tmr.next()
                P.op('dve', lambda e, k=k, s=s: e.scalar_tensor_tensor(out=pb.tmp[s][:], in0=pb.xg[:, k, :], scalar=coefA2[cur[0]][:, sub, k, c:c + 1],
                                                                       in1=pb.rstd[:], op0=ALU.mult, op1=ALU.mult),
                     reads=[('xg', k // 4), 'coefA%d' % cur[0], 'rstd'], writes=[rn])
                P.op('act', lambda e, k=k, s=s: e.activation(out=hT_ap(k), in_=pb.tmp[s][:], func=AF.Identity,
                                                             bias=mod_ap(sh_split, k, c), scale=1.0),
                     reads=[rn, 'modT%d' % cur[0]], writes=[hname])
            if next_u is not None:
                load_x(next_u)
                pb.preloaded = next_u

        class EpiBufs:
            def __init__(self, st, n=3):
                self.xi = [sb(st, "exi%d" % i, [128, 512], F32) for i in range(n)]
                self.xo = [sb(st, "exo%d" % i, [128, 512], F32) for i in range(n)]
                self.ri = Ring('exi', n)
                self.ro = Ring('exo', n)

        def epi_load(eb, m, u):
            s, rn = eb.ri.next()
            dma('sp', eb.xi[s][:], xT[m, :, u * 512:(u + 1) * 512], reads=[('xT', m, u)], writes=[rn], key=rn)
            return s, rn

        def epi_finish(eb, ld, m, u, gsplit, psb):
            s, rn = ld
            so, rno = eb.ro.next()
            c = unit_cond(u)
            P.op('dve', lambda e: e.scalar_tensor_tensor(out=eb.xo[so][:], in0=ps[psb][:], scalar=mod_ap(gsplit, m, c), in1=eb.xi[s][:],
                                                         op0=ALU.mult, op1=ALU.add),
                 reads=['ps%d' % psb, 'modT%d' % cur[0], rn], writes=[rno])
            dma('sp', xT[m, :, u * 512:(u + 1) * 512], eb.xo[so][:], reads=[rno], writes=[('xT', m, u)], key=rno)

        def block_ffn(L):
            for units in [(0, 1), (2, 3), (4,)]:
                ffn_pass(L, units, ada_next=(units == (0, 1) and L + 1 < DEPTH and need_layer[L + 1]))

        def ffn_pass(L, units, ada_next=False):
            nu = len(units)
            HJ = JC // 2
            with ExitStack() as st:
                pb = PrepBufs(st)
                eb = EpiBufs(st, 3)
                hT = sb(st, "hT", [128, KC, nu * 512], BF16)
                act = sb(st, "actb", [128, HJ, nu * 512], BF16)
                wg = [sb(st, "wg%d" % i, [128, KC, 256], BF16) for i in range(2)]
                wu = [sb(st, "wu%d" % i, [128, KC, 256], BF16) for i in range(2)]
                mcn = 1 if ada_next else 2
                wd = [sb(st, "wd%d" % i, [128, HJ, 128 * mcn], BF16) for i in range(2)]
                sl = [sb(st, "sl%d" % i, [128, 512], F32) for i in range(2)]
                rg, ru, rd, rs = Ring('wg', 2), Ring('wu', 2), Ring('wd', 2), Ring('sl', 2)
                for i, u in enumerate(units):
                    prep_unit(pb, u, 1, lambda k, i=i: hT[:, k, i * 512:(i + 1) * 512], ('hT', i), 0, next_u=(units[i + 1] if i + 1 < nu else None))
                bank = 0
                asteps = ada_steps(L + 1, st, 7) if ada_next else []

                def ada_tick(asteps=asteps):
                    if asteps:
                        asteps.pop(0)()
                for half in range(2):
                    bank = ffn_half(L, units, half, bank, hT, act, wg, wu, wd, sl, rg, ru, rd, rs, eb, ada_tick, mcn)
                while asteps:
                    ada_tick()
                P.flush()

        def ffn_half(L, units, half, bank, hT, act, wg, wu, wd, sl, rg, ru, rd, rs, eb, ada_tick, mcn):
                nu = len(units)
                HJ = JC // 2
                for jj in range(HJ // 2):
                    j0 = half * HJ + jj * 2
                    sg, rng_ = rg.next()
                    su, rnu = ru.next()
                    dma('pool', wg[sg][:], ffn_gate[L][:, j0 * 128:(j0 + 2) * 128].rearrange("(k p) n -> p k n", p=128), writes=[rng_], key=rng_)
                    dma('pool', wu[su][:], ffn_up[L][:, j0 * 128:(j0 + 2) * 128].rearrange("(k p) n -> p k n", p=128), writes=[rnu], key=rnu)
                    ada_tick()
                    for jl in range(2):
                        for i in range(nu):
                            bg, bu = bank % 6, (bank + 1) % 6
                            bank += 2

                            def mm(e, w, slot, b, jl=jl, i=i):
                                for k in range(KC):
                                    ins = e.matmul(ps[b][:], w[slot][:, k, jl * 128:(jl + 1) * 128], hT[:, k, i * 512:(i + 1) * 512],
                                                   start=(k == 0), stop=(k == KC - 1))
                                return ins
                            P.op('pe', lambda e, mm=mm, sg=sg, bg=bg: mm(e, wg, sg, bg), reads=[rng_, ('hT', i)], writes=['ps%d' % bg])
                            P.op('pe', lambda e, mm=mm, su=su, bu=bu: mm(e, wu, su, bu), reads=[rnu, ('hT', i)], writes=['ps%d' % bu])
                            ss, rns = rs.next()
                            P.op('act', lambda e, ss=ss, bg=bg: e.activation(out=sl[ss][:], in_=ps[bg][:], func=AF.Silu),
                                 reads=['ps%d' % bg], writes=[rns])
                            jrel = jj * 2 + jl
                            P.op('dve', lambda e, ss=ss, bu=bu, jrel=jrel, i=i: e.tensor_tensor(out=act[:, jrel, i * 512:(i + 1) * 512], in0=ps[bu][:], in1=sl[ss][:], op=ALU.mult),
                                 reads=['ps%d' % bu, rns], writes=[('act', jrel, i)])
                items = [(m, i, u) for m in range(KC) for i, u in enumerate(units)]
                lds = {0: epi_load(eb, items[0][0], items[0][2])}
                slab = {}
                for t, (m, i, u) in enumerate(items):
                    if i == 0 and m % mcn == 0:
                        sd, rnd = rd.next()
                        dma('pool', wd[sd][:], ffn_down[L][half * HJ * 128:(half + 1) * HJ * 128, m * 128:(m + mcn) * 128].rearrange("(j p) n -> p j n", p=128),
                            writes=[rnd], key=rnd)
                        slab['cur'] = (sd, rnd)
                    if i == 0:
                        ada_tick()
                    sd, rnd = slab['cur']
                    ml = m % mcn
                    b = bank % 7
                    bank += 1
                    if t + 1 < len(items):
                        lds[t + 1] = epi_load(eb, items[t + 1][0], items[t + 1][2])

                    def mm2(e, sd=sd, b=b, i=i, ml=ml):
                        for j in range(HJ):
                            ins = e.matmul(ps[b][:], wd[sd][:, j, ml * 128:(ml + 1) * 128], act[:, j, i * 512:(i + 1) * 512], start=(j == 0), stop=(j == HJ - 1))
                        return ins
                    P.op('pe', mm2, reads=[rnd] + [('act', j, i) for j in range(HJ)], writes=['ps%d' % b])
                    epi_finish(eb, lds.pop(t), m, u, 5, b)
                return bank


        def block_cmlp(L):
            j = L // 2
            with ExitStack() as st:
                pb = PrepBufs(st)
                eb = EpiBufs(st, 2)
                hT = sb(st, "chT", [128, KC, 512], BF16)
                vtm = sb(st, "vtm", [128, 4, D], F32)
                vhat = sb(st, "vhat", [128, 4, D], BF16)
                gated = sb(st, "gated", [128, KC, 512], BF16)
                wv = [sb(st, "cwv%d" % i, [128, KC, 256], BF16) for i in range(2)]
                wu = [sb(st, "cwu%d" % i, [128, KC, 256], BF16) for i in range(2)]
                wo = [sb(st, "cwo%d" % i, [128, KC, 256], BF16) for i in range(2)]
                wsT = sb(st, "wsT", [128, 8, 128], BF16)
                wsl = [sb(st, "wsl%d" % i, [128, 4, 128], F32) for i in range(2)]
                bsb = sb(st, "bsb", [128, 8, 128], F32)
                vn = sb(st, "cvn", [128, KC], F32)
                ssq = sb(st, "cssq", [128, 4], F32)
                crs = sb(st, "crs", [128, 4], F32)
                usb = [sb(st, "usb%d" % i, [128, 512], F32) for i in range(2)]
                svb = [sb(st, "svb%d" % i, [128, 512], F32) for i in range(2)]
                rv, ru, ro, rus, rsv = Ring('cwv', 2), Ring('cwu', 2), Ring('cwo', 2), Ring('usb', 2), Ring('svb', 2)
                dma('sp', vn[:], cmlp_v_norm[j].rearrange("(k p) -> p k", p=128), writes=['cvn'], key='c0', slow=True)
                dma('sp', bsb[:], cmlp_b_s[j].partition_broadcast(128), writes=['bsb'], key='c1')
                for hh in range(2):
                    dma('sp', wsl[hh][:], cmlp_w_s[j, hh * 4:(hh + 1) * 4].rearrange("g p q -> p g q"), writes=['wsl%d' % hh], key='wsl%d' % hh)

                    def trw(e, hh=hh):
                        for i in range(4):
                            ins = e.transpose(ps[hh][:, i * 128:(i + 1) * 128], wsl[hh][:, i, :], ident[:])
                        return ins
                    P.op('pe', trw, reads=['wsl%d' % hh, 'ident'], writes=['ps%d' % hh])
                    P.op('dve', lambda e, hh=hh: e.tensor_copy(out=wsT[:, hh * 4:(hh + 1) * 4, :], in_=ps[hh][:].rearrange("p (g q) -> p g q", g=4)),
                         reads=['ps%d' % hh], writes=['wsT'])
                bank = 2
                for u in range(NU):
                    prep_unit(pb, u, 0, lambda k: hT[:, k, :], 'chT', bank % 8, next_u=(u + 1 if u + 1 < NU else None))
                    bank += 1
                    for n in range(8):
                        sv_, rnv = rv.next()
                        dma('pool', wv[sv_][:], cmlp_w_in[j][:, D + n * 256:D + (n + 1) * 256].rearrange("(k p) n -> p k n", p=128), writes=[rnv], key=rnv)
                        for tc in range(4):
                            b = bank % 8
                            bank += 1

                            def mmv(e, sv_=sv_, tc=tc, b=b):
                                for k in range(KC):
                                    ins = e.matmul(ps[b][:, 0:256], hT[:, k, tc * 128:(tc + 1) * 128], wv[sv_][:, k, :], start=(k == 0), stop=(k == KC - 1))
                                return ins
                            P.op('pe', mmv, reads=[rnv, 'chT'], writes=['ps%d' % b])
                            P.op('act', lambda e, tc=tc, n=n, b=b: e.activation(out=vtm[:, tc, n * 256:(n + 1) * 256], in_=ps[b][:, 0:256], func=AF.Gelu_apprx_tanh),
                                 reads=['ps%d' % b], writes=[('vtm', tc, n)])
                    for tc in range(4):
                        P.op('act', lambda e, tc=tc: e.activation(out=vhat[:, tc, :], in_=vtm[:, tc, :], func=AF.Square, accum_out=ssq[:, tc:tc + 1]),
                             reads=[('vtm', tc, n) for n in range(8)], writes=[('vhat', tc), ('ssq', tc)])
                    P.op('act', lambda e: e.activation(out=crs[:], in_=ssq[:], func=AF.Sqrt, scale=1.0 / D, bias=epsb[:]),
                         reads=[('ssq', tc) for tc in range(4)] + ['epsb'], writes=['crs'])
                    P.op('dve', lambda e: e.reciprocal(out=crs[:], in_=crs[:]), reads=['crs'], writes=['crs'])
                    for tc in range(4):
                        P.op('dve', lambda e, tc=tc: e.tensor_scalar(out=vhat[:, tc, :], in0=vtm[:, tc, :], scalar1=crs[:, tc:tc + 1], scalar2=None, op0=ALU.mult),
                             reads=[('vtm', tc, n) for n in range(8)] + ['crs'], writes=[('vhat', tc)])
                    for mm_ in range(8):
                        su, rnu = ru.next()
                        dma('pool', wu[su][:], cmlp_w_in[j][:, mm_ * 256:(mm_ + 1) * 256].rearrange("(k p) n -> p k n", p=128), writes=[rnu], key=rnu)
                        for ml in range(2):
                            m = mm_ * 2 + ml
                            g = m // 2
                            ba, bb = bank % 8, (bank + 1) % 8
                            bank += 2

                            def mmu(e, su=su, ml=ml, ba=ba):
                                for k in range(KC):
                                    ins = e.matmul(ps[ba][:], wu[su][:, k, ml * 128:(ml + 1) * 128], hT[:, k, :], start=(k == 0), stop=(k == KC - 1))
                                return ins
                            P.op('pe', mmu, reads=[rnu, 'chT'], writes=['ps%d' % ba])
                            s1, rn1 = rus.next()
                            P.op('act', lambda e, s1=s1, ba=ba: e.activation(out=usb[s1][:], in_=ps[ba][:], func=AF.Gelu_apprx_tanh), reads=['ps%d' % ba], writes=[rn1])

                            def mms(e, m=m, g=g, bb=bb):
                                for tc in range(4):
                                    ins = e.matmul(ps[bb][:, tc * 128:(tc + 1) * 128], vhat[:, tc, m * 128:(m + 1) * 128], wsT[:, g, :], start=True, stop=True)
                                return ins
                            P.op('pe', mms, reads=[('vhat', tc) for tc in range(4)] + ['wsT'], writes=['ps%d' % bb])
                            s2, rn2 = rsv.next()
                            P.op('dve', lambda e, s2=s2, bb=bb, m=m, g=g: e.scalar_tensor_tensor(
                                out=svb[s2][:].rearrange("p (t q) -> p t q", t=4), in0=ps[bb][:].rearrange("p (t q) -> p t q", t=4), scalar=vn[:, m:m + 1],
                                in1=bsb[:, g, :].unsqueeze(1).to_broadcast([128, 4, 128]), op0=ALU.mult, op1=ALU.add),
                                reads=['ps%d' % bb, 'cvn', 'bsb'], writes=[rn2])
                            P.op('dve', lambda e, s1=s1, s2=s2, m=m: e.tensor_tensor(out=gated[:, m, :], in0=usb[s1][:], in1=svb[s2][:], op=ALU.mult),
                                 reads=[rn1, rn2], writes=[('gated', m)])
                    for mo2 in range(8):
                        so, rno = ro.next()
                        dma('pool', wo[so][:], cmlp_w_out[j][:, mo2 * 256:(mo2 + 1) * 256].rearrange("(k p) n -> p k n", p=128), writes=[rno], key=rno)
                        for ml in range(2):
                            mo = mo2 * 2 + ml
                            b = bank % 8
                            bank += 1
                            ld = epi_load(eb, mo, u)

                            def mmo(e, so=so, ml=ml, b=b):
                                for k in range(KC):
                                    ins = e.matmul(ps[b][:], wo[so][:, k, ml * 128:(ml + 1) * 128], gated[:, k, :], start=(k == 0), stop=(k == KC - 1))
                                return ins
                            P.op('pe', mmo, reads=[rno] + [('gated', m) for m in range(KC)], writes=['ps%d' % b])
                            epi_finish(eb, ld, mo, u, 2, b)
                P.flush()


        def swap_copy(eng, dst, src, blk, reads, writes):
            dv = dst.rearrange("p k (a two c) -> p k a two c", two=2, c=blk)
            sv = src.rearrange("p k (a two c) -> p k a two c", two=2, c=blk)
            P.op(eng, lambda e: e.tensor_copy(out=dv[:, :, :, 0, :], in_=sv[:, :, :, 1, :]), reads=reads, writes=[writes + '_a'])
            P.op(eng, lambda e: e.tensor_copy(out=dv[:, :, :, 1, :], in_=sv[:, :, :, 0, :]), reads=reads, writes=[writes + '_b'])
            return [writes + '_a', writes + '_b']

        def load_perm_gain(dst, src_vec, blk, width, key):
            nb_ = width // blk
            for b_ in range(nb_):
                pb_ = b_ ^ 1
                dma('sp', dst[b_ * blk:(b_ + 1) * blk, 0:1], src_vec[pb_ * blk:(pb_ + 1) * blk].rearrange("(p o) -> p o", o=1),
                    writes=[(key, b_)], key=key, slow=True)
            return [(key, b_) for b_ in range(nb_)]

        class AttnScratch:
            pass

        def attn_scratch(j):
            a = AttnScratch()

            def dt_(name, shape):
                return nc.dram_tensor("%s_%d" % (name, j), list(shape), BF16)
            a.sendF1 = dt_("sendF1", [512, NS])
            a.recvF1 = dt_("recvF1", [1024, NS])
            a.sendF2 = dt_("sendF2", [384, NS])
            a.recvF2 = dt_("recvF2", [768, NS])

            def sF(blk):
                if blk < 4:
                    return a.sendF1.ap()[blk * 128:(blk + 1) * 128, :]
                return a.sendF2.ap()[(blk - 4) * 128:(blk - 3) * 128, :]

            def rF(r, blk):
                if blk < 4:
                    return a.recvF1.ap()[r * 512 + blk * 128:r * 512 + (blk + 1) * 128, :]
                return a.recvF2.ap()[r * 384 + (blk - 4) * 128:r * 384 + (blk - 3) * 128, :]
            a.sF = sF
            a.rF = rF
            a.sendV = dt_("sendV", [NS, 256])
            a.recvV = dt_("recvV", [2 * NS, 256])
            a.KT_s = dt_("KT_s", [10, 128, 4608]).ap()
            a.KR_s = dt_("KR_s", [128, 4608]).ap()
            a.V_s = dt_("V_s", [4608, 1280]).ap()
            a.KT_p = dt_("KT_p", [10, 128, 512]).ap()
            a.KR_p = dt_("KR_p", [128, 512]).ap()
            a.V_p = dt_("V_p", [512, 1280]).ap()
            a.CK_p = dt_("CK_p", [4, 128, 512]).ap()
            return a

        def rms_rstd(srcs, n_feat, sqring, sqbufs, bank, rstd_tile, rname, src_reads):
            n = len(srcs)
            for i, (src, rd) in enumerate(zip(srcs, src_reads)):
                s_, rn = sqring.next()
                P.op('act', lambda e, src=src, s_=s_: e.activation(out=sqbufs[s_][:], in_=src, func=AF.Square), reads=[rd], writes=[rn])
                P.op('pe', lambda e, s_=s_, i=i: e.matmul(ps[bank][:], ones_r[:], sqbufs[s_][:], start=(i == 0), stop=(i == n - 1)),
                     reads=[rn, 'ones_r'], writes=['ps%d' % bank])
            P.op('act', lambda e: e.activation(out=rstd_tile[:], in_=ps[bank][:], func=AF.Sqrt, scale=1.0 / n_feat, bias=epsb[:]),
                 reads=['ps%d' % bank, 'epsb'], writes=[rname])
            P.op('dve', lambda e: e.reciprocal(out=rstd_tile[:], in_=rstd_tile[:]), reads=[rname], writes=[rname])

        def block_attn(L):
            j = L // 2
            a = attn_scratch(j)
            parts = cfg.get("attn_parts", "123")
            if "1" in parts:
                attn_kv_pass(L, j, a)
            if "2" in parts:
                attn_exchange(L, j, a)
            if "3" in parts:
                attn_main(L, j, a)

        def attn_kv_pass(L, j, a):
            W = attn_w_in[j]
            with ExitStack() as st:
                pb = PrepBufs(st)
                hT = sb(st, "ahT", [128, KC, 512], BF16)
                wk = sb(st, "awk", [128, KC, 256], BF16)
                wkp = sb(st, "awkp", [128, KC, 256], BF16)
                wv = sb(st, "awv", [128, KC, 256], BF16)
                wc = sb(st, "awc", [128, KC, 512], BF16)
                wr = sb(st, "awr", [128, KC, 128], BF16)
                wrp = sb(st, "awrp", [128, KC, 128], BF16)
                gk = sb(st, "agk", [128, 1], F32)
                gkp = sb(st, "agkp", [128, 1], F32)
                gkv = sb(st, "agkv", [128, 4], F32)
                rA = sb(st, "arA", [128, 2, 512], F32)
                rB = sb(st, "arB", [128, 2, 512], F32)
                sq = [sb(st, "asq%d" % i, [128, 512], F32R) for i in range(2)]
                sqr = Ring('asq', 2)
                rstd = sb(st, "arstd", [128, 512], F32)
                t1 = sb(st, "at1", [128, 512], F32)
                t2 = sb(st, "at2", [128, 512], F32)
                kf = sb(st, "akf", [128, 512], F32)
                craw = sb(st, "acraw", [128, 4, 512], F32)
                kb = [sb(st, "akb%d" % i, [128, 512], BF16) for i in range(2)]
                kbr = Ring('akb', 2)
                vb = sb(st, "avb", [128, 4, 256], BF16)
                stk = sb(st, "astk", [128, 4, 256], F32)
                stv = sb(st, "astv", [128, 4, 256], F32)
                stc = sb(st, "astc", [128, 4, 512], F32)
                strr = sb(st, "astr", [128, 4, 64], F32)
                dma('pool', wk[:], W[:, 1024:1280].rearrange("(k p) n -> p k n", p=128), writes=['awk'], key='w0')
                dma('pool', wv[:], W[:, 1280:1536].rearrange("(k p) n -> p k n", p=128), writes=['awv'], key='w1')
                dma('pool', wc[:], W[:, 3072:3584].rearrange("(k p) n -> p k n", p=128), writes=['awc'], key='w2')
                dma('pool', wr[:, :, 0:64], W[:, 3584:3648].rearrange("(k p) n -> p k n", p=128), writes=['awr_a'], key='w3')
                dma('pool', wr[:, :, 64:128], W[:, 3584:3648].rearrange("(k p) n -> p k n", p=128), writes=['awr_b'], key='w3')
                wkp_r = swap_copy('dve', wkp[:], wk[:], 32, ['awk'], 'awkp')
                wrp_r = swap_copy('dve', wrp[:], wr[:], 16, ['awr_a', 'awr_b'], 'awrp')
                dma('sp', gk[:], attn_k_norm[j].rearrange("(p o) -> p o", o=1), writes=['agk'], key='c0', slow=True)
                gkp_r = load_perm_gain(gkp, attn_k_norm[j], 32, 128, 'agkp')
                dma('sp', gkv[:], attn_kv_norm[j].rearrange("(c p) -> p c", p=128), writes=['agkv'], key='c1', slow=True)
                bank = [0]

                def nb():
                    b = bank[0] % 8
                    bank[0] += 1
                    return b

                def proj(wt, c0, b, reads):
                    def f(e):
                        for k in range(KC):
                            ins = e.matmul(ps[b][:], wt[:, k, c0:c0 + 128], hT[:, k, :], start=(k == 0), stop=(k == KC - 1))
                        return ins
                    P.op('pe', f, reads=reads + ['ahT'], writes=['ps%d' % b])

                def transposes_out(src, dst_ap_fn, b, reads, wname, ncol=128):
                    def f(e):
                        for tt in range(4):
                            ins = e.transpose(ps[b][:, tt * 128:(tt + 1) * 128], src[:, tt * 128:(tt + 1) * 128], ident[:])
                        return ins
                    P.op('pe', f, reads=reads + ['ident'], writes=['ps%d' % b])
                    P.op('dve', lambda e: e.tensor_copy(out=dst_ap_fn(), in_=ps[b][:].rearrange("p (t c) -> p t c", t=4)[:, :, 0:ncol]),
                         reads=['ps%d' % b], writes=[wname])

                for u in range(NU):
                    samp = u < 4
                    prep_unit(pb, u, 0, lambda k: hT[:, k, :], 'ahT', nb(), next_u=(u + 1 if u + 1 < NU else None))
                    if samp:
                        dma('sp', rA[:], ropeA[:, :, u * 512:(u + 1) * 512].rearrange("t p n -> p t n"), writes=['arA'], key='c2')
                        dma('sp', rB[:], ropeB[:, :, u * 512:(u + 1) * 512].rearrange("t p n -> p t n"), writes=['arB'], key='c3')
                    for h in range(2):
                        b0 = nb()
                        proj(wk, h * 128, b0, ['awk'])
                        if samp:
                            b1 = nb()
                            proj(wkp, h * 128, b1, wkp_r)
                        b2 = nb()
                        rms_rstd([ps[b0][:]], 128, sqr, sq, b2, rstd, 'arstd', ['ps%d' % b0])
                        s_, rnk = kbr.next()
                        if samp:
                            P.op('dve', lambda e, b0=b0: e.scalar_tensor_tensor(out=t1[:], in0=ps[b0][:], scalar=gk[:, 0:1], in1=rA[:, 0, :], op0=ALU.mult, op1=ALU.mult),
                                 reads=['ps%d' % b0, 'agk', 'arA'], writes=['at1'])
                            P.op('dve', lambda e, b1=b1: e.scalar_tensor_tensor(out=t2[:], in0=ps[b1][:], scalar=gkp[:, 0:1], in1=rA[:, 1, :], op0=ALU.mult, op1=ALU.mult),
                                 reads=['ps%d' % b1, 'arA'] + gkp_r, writes=['at2'])
                            P.op('dve', lambda e: e.tensor_tensor(out=t1[:], in0=t1[:], in1=t2[:], op=ALU.add), reads=['at1', 'at2'], writes=['at1'])
                            P.op('dve', lambda e, s_=s_: e.tensor_tensor(out=kb[s_][:], in0=t1[:], in1=rstd[:], op=ALU.mult), reads=['at1', 'arstd'], writes=[rnk])
                            dma('sp', a.sF(h)[:, u * 512:(u + 1) * 512], kb[s_][:], reads=[rnk], key=rnk)
                        else:
                            P.op('dve', lambda e, b0=b0: e.scalar_tensor_tensor(out=kf[:], in0=ps[b0][:], scalar=gk[:, 0:1], in1=rstd[:], op0=ALU.mult, op1=ALU.mult),
                                 reads=['ps%d' % b0, 'agk', 'arstd'], writes=['akf'])
                            P.op('act', lambda e, s_=s_: e.copy(out=kb[s_][:], in_=kf[:]), reads=['akf'], writes=[rnk])
                            dma('sp', a.KT_p[h], kb[s_][:], reads=[rnk], key=rnk)
                            transposes_out(kf, lambda h=h: stk[:, :, h * 128:(h + 1) * 128], nb(), ['akf'], ('astk', h))
                    if not samp:
                        dma('sp', st_k[j].rearrange("(t p) c -> p t c", p=128), stk[:], reads=[('astk', 0), ('astk', 1)], key='so0')
                    for tc in range(4):
                        b = nb()

                        def mmv(e, tc=tc, b=b):
                            for k in range(KC):
                                ins = e.matmul(ps[b][:, 0:256], hT[:, k, tc * 128:(tc + 1) * 128], wv[:, k, :], start=(k == 0), stop=(k == KC - 1))
                            return ins
                        P.op('pe', mmv, reads=['awv', 'ahT'], writes=['ps%d' % b])
                        if not samp:
                            P.op('act', lambda e, tc=tc, b=b: e.copy(out=stv[:, tc, :], in_=ps[b][:, 0:256]), reads=['ps%d' % b], writes=[('astv', tc)])
                        P.op('dve', lambda e, tc=tc, b=b: e.tensor_copy(out=vb[:, tc, :], in_=ps[b][:, 0:256]), reads=['ps%d' % b], writes=[('avb', tc)])
                    if samp:
                        dma('sp', a.sendV.ap()[u * 512:(u + 1) * 512, :].rearrange("(t p) c -> p t c", p=128), vb[:], reads=[('avb', tc) for tc in range(4)], key='so1')
                    else:
                        dma('sp', st_v[j].rearrange("(t p) c -> p t c", p=128), stv[:], reads=[('astv', tc) for tc in range(4)], key='so2')
                        dma('sp', a.V_p[:, 0:256].rearrange("(t p) c -> p t c", p=128), vb[:], reads=[('avb', tc) for tc in range(4)], key='so1')
                    for c4 in range(4):
                        b = nb()
                        proj(wc, c4 * 128, b, ['awc'])
                        P.op('act', lambda e, c4=c4, b=b: e.copy(out=craw[:, c4, :], in_=ps[b][:]), reads=['ps%d' % b], writes=[('acraw', c4)])
                    rms_rstd([craw[:, c4, :] for c4 in range(4)], 512, sqr, sq, nb(), rstd, 'arstd', [('acraw', c4) for c4 in range(4)])
                    for c4 in range(4):
                        s_, rnk = kbr.next()
                        if samp:
                            P.op('dve', lambda e, c4=c4, s_=s_: e.scalar_tensor_tensor(out=kb[s_][:], in0=craw[:, c4, :], scalar=gkv[:, c4:c4 + 1], in1=rstd[:], op0=ALU.mult, op1=ALU.mult),
                                 reads=[('acraw', c4), 'agkv', 'arstd'], writes=[rnk])
                            dma('sp', a.sF(2 + c4)[:, u * 512:(u + 1) * 512], kb[s_][:], reads=[rnk], key=rnk)
                        else:
                            P.op('dve', lambda e, c4=c4: e.scalar_tensor_tensor(out=kf[:], in0=craw[:, c4, :], scalar=gkv[:, c4:c4 + 1], in1=rstd[:], op0=ALU.mult, op1=ALU.mult),
                                 reads=[('acraw', c4), 'agkv', 'arstd'], writes=['akf'])
                            P.op('act', lambda e, s_=s_: e.copy(out=kb[s_][:], in_=kf[:]), reads=['akf'], writes=[rnk])
                            dma('sp', a.CK_p[c4], kb[s_][:], reads=[rnk], key=rnk)
                            transposes_out(kf, lambda c4=c4: stc[:, :, c4 * 128:(c4 + 1) * 128], nb(), ['akf'], ('astc', c4))
                    if not samp:
                        dma('sp', st_ckv[j].rearrange("(t p) c -> p t c", p=128), stc[:], reads=[('astc', c4) for c4 in range(4)], key='so3')
                    b0 = nb()
                    proj(wr, 0, b0, ['awr_a', 'awr_b'])
                    s_, rnk = kbr.next()
                    if samp:
                        b1 = nb()
                        proj(wrp, 0, b1, wrp_r)
                        P.op('dve', lambda e, b0=b0: e.tensor_tensor(out=t1[:], in0=ps[b0][:], in1=rB[:, 0, :], op=ALU.mult), reads=['ps%d' % b0, 'arB'], writes=['at1'])
                        P.op('dve', lambda e, b1=b1: e.tensor_tensor(out=t2[:], in0=ps[b1][:], in1=rB[:, 1, :], op=ALU.mult), reads=['ps%d' % b1, 'arB'], writes=['at2'])
                        P.op('dve', lambda e, s_=s_: e.tensor_tensor(out=kb[s_][:], in0=t1[:], in1=t2[:], op=ALU.add), reads=['at1', 'at2'], writes=[rnk])
                        dma('sp', a.sF(6)[:, u * 512:(u + 1) * 512], kb[s_][:], reads=[rnk], key=rnk)
                    else:
                        P.op('act', lambda e, b0=b0: e.copy(out=kf[:], in_=ps[b0][:]), reads=['ps%d' % b0], writes=['akf'])
                        P.op('dve', lambda e, s_=s_: e.tensor_copy(out=kb[s_][:], in_=kf[:]), reads=['akf'], writes=[rnk])
                        dma('sp', a.KR_p, kb[s_][:], reads=[rnk], key=rnk)
                        transposes_out(kf, lambda: strr[:], nb(), ['akf'], 'astr', ncol=64)
                        dma('sp', st_kr[j].rearrange("(t p) c -> p t c", p=128), strr[:], reads=['astr'], key='so4')
                P.flush()

        def attn_exchange(L, j, a):
            with ExitStack() as st:
                ckvT = sb(st, "xckvT", [128, 4, 4608], BF16)
                ckp = sb(st, "xckp", [128, 4, 512], BF16)
                wup = sb(st, "xwup", [128, 4, 2048], BF16)
                lt = [sb(st, "xlt%d" % i, [128, 512], F32) for i in range(3)]
                kcs = sb(st, "xkcs", [128, 2, 512], BF16)
                krc = sb(st, "xkrc", [128, 512], BF16)
                vcs = sb(st, "xvcs", [128, 4, 256], BF16)
                knb = [sb(st, "xknb%d" % i, [128, 4608], BF16) for i in range(2)]
                knr = Ring('xknb', 2)
                vbb = [sb(st, "xvbb%d" % i, [128, 1024], BF16) for i in range(2)]
                vbr = Ring('xvbb', 2)
                cc1, cc2 = 'cc1_%d' % j, 'cc2_%d' % j
                groups = [[0, 1], [2, 3], [4, 5], [6, 7]]
                P.op('pool', lambda e: e.collective_compute("AllGather", ALU.bypass, replica_groups=groups,
                                                            ins=[a.sendF1.ap().opt()], outs=[a.recvF1.ap().opt()]),
                     writes=['recvF1'], dma=cc1, inc=1)
                P.op('pool', lambda e: e.collective_compute("AllGather", ALU.bypass, replica_groups=groups,
                                                            ins=[a.sendF2.ap().opt()], outs=[a.recvF2.ap().opt()]),
                     writes=['recvF2'], dma=cc1 + 'b', inc=1)
                P.op('pool', lambda e: e.collective_compute("AllGather", ALU.bypass, replica_groups=groups,
                                                            ins=[a.sendV.ap().opt()], outs=[a.recvV.ap().opt()]),
                     writes=['recvV'], dma=cc2, inc=1)
                dma('pool', wup[:], attn_w_kv_up[j].rearrange("(c p) n -> p c n", p=128), writes=['xwup'], key='w0')
                rV = a.recvV.ap()
                for r in range(2):
                    for h in range(2):
                        dma('sp', a.KT_s[h, :, r * NS:(r + 1) * NS], a.rF(r, h), reads=['recvF1', 'recvF2'], key='d0')
                    dma('sp', a.KR_s[:, r * NS:(r + 1) * NS], a.rF(r, 6), reads=['recvF1', 'recvF2'], key='d0')
                    for c4 in range(4):
                        dma('sp', ckvT[:, c4, r * NS:(r + 1) * NS], a.rF(r, 2 + c4), reads=['recvF1', 'recvF2'],
                            writes=[('xckvT', r, c4)], key='d1')
                    dma('sp', a.V_s[r * NS:(r + 1) * NS, 0:256], rV[r * NS:(r + 1) * NS, :], reads=['recvV'], key='d0')
                dma('sp', ckp[:], a.CK_p.rearrange("c p n -> p c n"), writes=['xckp'], key='d2')
                bank = [0]

                def nb():
                    b = bank[0] % 8
                    bank[0] += 1
                    return b
                for tt in range(4):
                    dma('sp', lt[0][:, 0:256], cache_k[j, tt * 128:(tt + 1) * 128, :], writes=['xlt0'], key='xlt0')
                    b = nb()

                    def trk(e, b=b):
                        for h in range(2):
                            ins = e.transpose(ps[b][:, h * 128:(h + 1) * 128], lt[0][:, h * 128:(h + 1) * 128], ident[:])
                        return ins
                    P.op('pe', trk, reads=['xlt0', 'ident'], writes=['ps%d' % b])
                    P.op('dve', lambda e, tt=tt, b=b: e.tensor_copy(out=kcs[:, :, tt * 128:(tt + 1) * 128], in_=ps[b][:, 0:256].rearrange("p (h c) -> p h c", h=2)),
                         reads=['ps%d' % b], writes=[('xkcs', tt)])
                    dma('sp', lt[1][:], cache_ckv[j, tt * 128:(tt + 1) * 128, :], writes=['xlt1'], key='xlt1')
                    b = nb()

                    def trc(e, b=b):
                        for c4 in range(4):
                            ins = e.transpose(ps[b][:, c4 * 128:(c4 + 1) * 128], lt[1][:, c4 * 128:(c4 + 1) * 128], ident[:])
                        return ins
                    P.op('pe', trc, reads=['xlt1', 'ident'], writes=['ps%d' % b])
                    P.op('dve', lambda e, tt=tt, b=b: e.tensor_copy(out=ckvT[:, :, 2 * NS + tt * 128:2 * NS + (tt + 1) * 128], in_=ps[b][:].rearrange("p (c n) -> p c n", c=4)),
                         reads=['ps%d' % b], writes=[('xckvTc', tt)])
                    P.op('sp', lambda e, tt=tt: e.dma_start(out=lt[2][:, 0:64], in_=cache_kr[j, tt * 128:(tt + 1) * 128, :]), writes=['xlt2'], dma='xlt2')
                    P.op('sp', lambda e, tt=tt: e.dma_start(out=lt[2][:, 64:128], in_=cache_kr[j, tt * 128:(tt + 1) * 128, :]), reads=['xlt2'], writes=['xlt2b'], dma='xlt2')
                    b = nb()
                    P.op('pe', lambda e, b=b: e.transpose(ps[b][:, 0:128], lt[2][:, 0:128], ident[:]), reads=['xlt2b', 'ident'], writes=['ps%d' % b, 'xlt2'])
                    P.op('dve', lambda e, tt=tt, b=b: e.tensor_copy(out=krc[:, tt * 128:(tt + 1) * 128], in_=ps[b][:, 0:128]), reads=['ps%d' % b], writes=[('xkrc', tt)])
                for h in range(2):
                    dma('sp', a.KT_s[h, :, 2 * NS:2 * NS + 512], kcs[:, h, :], reads=[('xkcs', tt) for tt in range(4)], key='d3')
                dma('sp', a.KR_s[:, 2 * NS:2 * NS + 512], krc[:], reads=[('xkrc', tt) for tt in range(4)], key='d3')
                dma('pool', vcs[:], cache_v[j].rearrange("(t p) c -> p t c", p=128), writes=['xvcs'], key='w1')
                dma('sp', a.V_s[2 * NS:2 * NS + 512, 0:256].rearrange("(t p) c -> p t c", p=128), vcs[:], reads=['xvcs'], key='d3')
                srd_s = [('xckvT', r, c4) for r in range(2) for c4 in range(4)] + [('xckvTc', tt) for tt in range(4)]
                for (src, nk, KTd, Vd, srd) in ((ckvT, 4608, a.KT_s, a.V_s, srd_s), (ckp, 512, a.KT_p, a.V_p, ['xckp'])):
                    for h in range(8):
                        s_, rn = knr.next()
                        for kn in range(nk // 512):
                            b = nb()

                            def mk(e, h=h, kn=kn, b=b, src=src):
                                for c4 in range(4):
                                    ins = e.matmul(ps[b][:], wup[:, c4, h * 256:h * 256 + 128], src[:, c4, kn * 512:(kn + 1) * 512], start=(c4 == 0), stop=(c4 == 3))
                                return ins
                            P.op('pe', mk, reads=['xwup'] + srd, writes=['ps%d' % b])
                            if kn % 2 == 0:
                                P.op('dve', lambda e, s_=s_, kn=kn, b=b: e.tensor_copy(out=knb[s_][:, kn * 512:(kn + 1) * 512], in_=ps[b][:]), reads=['ps%d' % b], writes=[rn])
                            else:
                                P.op('act', lambda e, s_=s_, kn=kn, b=b: e.copy(out=knb[s_][:, kn * 512:(kn + 1) * 512], in_=ps[b][:]), reads=['ps%d' % b], writes=[rn])
                        dma('sp', KTd[2 + h, :, 0:nk], knb[s_][:, 0:nk], reads=[rn], key=rn)
                    wv4 = wup[:].rearrange("p c (h t n) -> p c h t n", h=8, t=2)
                    for kt in range(nk // 128):
                        s_, rn = vbr.next()
                        for hg in range(2):
                            b = nb()

                            def mv(e, kt=kt, hg=hg, b=b, src=src):
                                for c4 in range(4):
                                    ins = e.matmul(ps[b][:].rearrange("p (h n) -> p h n", h=4), src[:, c4, kt * 128:(kt + 1) * 128], wv4[:, c4, hg * 4:(hg + 1) * 4, 1, :],
                                                   start=(c4 == 0), stop=(c4 == 3))
                                return ins
                            P.op('pe', mv, reads=['xwup'] + srd, writes=['ps%d' % b])
                            if hg == 0:
                                P.op('dve', lambda e, s_=s_, b=b: e.tensor_copy(out=vbb[s_][:, 0:512], in_=ps[b][:]), reads=['ps%d' % b], writes=[rn])
                            else:
                                P.op('act', lambda e, s_=s_, b=b: e.copy(out=vbb[s_][:, 512:1024], in_=ps[b][:]), reads=['ps%d' % b], writes=[rn])
                        dma('sp', Vd[kt * 128:(kt + 1) * 128, 256:1280], vbb[s_][:], reads=[rn], key=rn)
                P.flush()

        def attn_main(L, j, a):
            W = attn_w_in[j]
            with ExitStack() as st:
                pb = PrepBufs(st)
                eb = EpiBufs(st, 2)
                hT = sb(st, "mhT", [128, KC, 512], BF16)
                oT = sb(st, "moT", [128, KC, 512], BF16)
                ktb = [sb(st, "mkt%d" % i, [128, 4608], BF16) for i in range(2)]
                vtb = [sb(st, "mvt%d" % i, [128, 36, 128], BF16) for i in range(2)]
                krs = sb(st, "mkrs", [128, 4608], BF16)
                krp = sb(st, "mkrp", [128, 512], BF16)
                wq = [sb(st, "mwq%d" % i, [128, KC, 128], BF16) for i in range(4)]
                wqp = [sb(st, "mwqp%d" % i, [128, KC, 128], BF16) for i in range(2)]
                wo = [sb(st, "mwo%d" % i, [128, KC, 256], BF16) for i in range(2)]
                rA = sb(st, "mrA", [128, 2, 512], F32)
                rB = sb(st, "mrB", [128, 2, 512], F32)
                gq = sb(st, "mgq", [128, 1], F32)
                gqp = sb(st, "mgqp", [128, 1], F32)
                sq = [sb(st, "msq%d" % i, [128, 512], F32R) for i in range(2)]
                sqr = Ring('msq', 2)
                rstd = sb(st, "mrstd", [128, 512], F32)
                t1 = sb(st, "mt1", [128, 512], F32)
                t2 = sb(st, "mt2", [128, 512], F32)
                qT = [sb(st, "mqT%d" % i, [128, 512], BF16) for i in range(2)]
                qTr = Ring('mqT', 2)
                qr = [sb(st, "mqr%d" % i, [128, 512], BF16) for i in range(2)]
                qrr = Ring('mqr', 2)
                pT = [sb(st, "mpT%d" % i, [128, 512], BF16) for i in range(3)]
                pTr = Ring('mpT', 3)
                rden = sb(st, "mrden", [128, 512], F32)
                rwq, rwo, rkt, rvt, rwqp = Ring('mwq', 4), Ring('mwo', 2), Ring('mkt', 2), Ring('mvt', 2), Ring('mwqp', 2)
                dma('sp', gq[:], attn_q_norm[j].rearrange("(p o) -> p o", o=1), writes=['mgq'], key='c0', slow=True)
                gqp_r = load_perm_gain(gqp, attn_q_norm[j], 32, 128, 'mgqp')
                dma('sp', krs[:], a.KR_s, writes=['mkrs'], key='c1')
                dma('sp', krp[:], a.KR_p, writes=['mkrp'], key='c2')
                sbank = [0]
                obank = [0]

                def nsb():
                    b = sbank[0] % 4
                    sbank[0] += 1
                    return b

                def proj(wt, slot, b, reads):
                    def f(e):
                        for k in range(KC):
                            ins = e.matmul(ps[b][:], wt[slot][:, k, :], hT[:, k, :], start=(k == 0), stop=(k == KC - 1))
                        return ins
                    P.op('pe', f, reads=reads + ['mhT'], writes=['ps%d' % b])

                def attention(groups, q_ap, q_rd, kt_slot, kt_rd, vt_slot, vt_rd, scale, chunk, rope=None):
                    for (q0_, nq_, tiles_) in groups:
                        do_group(q0_, nq_, tiles_, q_ap, q_rd, kt_slot, kt_rd, vt_slot, vt_rd, scale, chunk, rope)

                def do_group(q0, nq, tiles, q_ap, q_rd, kt_slot, kt_rd, vt_slot, vt_rd, scale, chunk, rope):
                    if True:
                        ob = 4
                        db = 5
                        nt = len(tiles)

                        def score(idx):
                            kt = tiles[idx]
                            b = nsb()

                            def f(e):
                                ins = e.matmul(ps[b][:, 0:nq], ktb[kt_slot][:, kt * 128:(kt + 1) * 128], q_ap[:, q0:q0 + nq], start=True, stop=(rope is None))
                                if rope is not None:
                                    qrt, _, hp, krt, _ = rope
                                    ins = e.matmul(ps[b][:, 0:nq], krt[hp * 64:(hp + 1) * 64, kt * 128:(kt + 1) * 128], qrt[hp * 64:(hp + 1) * 64, q0:q0 + nq],
                                                   start=False, stop=True)
                                return ins
                            rds = [kt_rd, q_rd] + ([rope[1], rope[4]] if rope is not None else [])
                            P.op('pe', f, reads=rds, writes=['ps%d' % b])
                            return b
                        pend = [score(0)]
                        if nt > 1:
                            pend.append(score(1))
                        for idx in range(nt):
                            b = pend.pop(0)
                            if idx + 2 < nt:
                                pend.append(score(idx + 2))
                            s_, rnp = pTr.next()
                            P.op('act', lambda e, b=b, s_=s_: e.activation(out=pT[s_][:, 0:nq], in_=ps[b][:, 0:nq], func=AF.Exp, scale=scale),
                                 reads=['ps%d' % b], writes=[rnp])
                            kt = tiles[idx]

                            def pv(e, s_=s_, kt=kt, idx=idx):
                                e.matmul(ps[ob][:, 0:nq], vtb[vt_slot][:, kt, :], pT[s_][:, 0:nq], start=(idx == 0), stop=(idx == nt - 1))
                                return e.matmul(ps[db][:, 0:nq], ones_b[:], pT[s_][:, 0:nq], start=(idx == 0), stop=(idx == nt - 1))
                            P.op('pe', pv, reads=[rnp, vt_rd, 'ones_b'], writes=['ps%d' % ob, 'ps%d' % db])
                        P.op('dve', lambda e: e.reciprocal(out=rden[:, 0:nq], in_=ps[db][:, 0:nq]), reads=['ps%d' % db], writes=['mrden'])
                        P.op('dve', lambda e: e.tensor_tensor(out=oT[:, chunk, q0:q0 + nq], in0=ps[ob][:, 0:nq], in1=rden[:, 0:nq], op=ALU.mult),
                             reads=['ps%d' % ob, 'mrden'], writes=[('moT', chunk)])

                for u in range(NU):
                    samp = u < 4
                    prep_unit(pb, u, 0, lambda k: hT[:, k, :], 'mhT', nsb(), next_u=(u + 1 if u + 1 < NU else None))
                    if samp:
                        dma('sp', rA[:], ropeA[:, :, u * 512:(u + 1) * 512].rearrange("t p n -> p t n"), writes=['mrA'], key='c3')
                        dma('sp', rB[:], ropeB[:, :, u * 512:(u + 1) * 512].rearrange("t p n -> p t n"), writes=['mrB'], key='c4')
                        groups = [(0, 512, list(range(36)))]
                        KT, VV, nk = a.KT_s, a.V_s, 4608
                        krt, kr_rd = krs, 'mkrs'
                    else:
                        groups = [(0, 256, [0, 1]), (256, 256, [2, 3])]
                        KT, VV, nk = a.KT_p, a.V_p, 512
                        krt, kr_rd = krp, 'mkrp'
                    kvl = {}
                    wql = {}
                    lstate = {'last_kind': None, 'kv': None}

                    def hinfo(hh):
                        isA = hh < 8
                        h = hh if isA else hh - 8
                        kind = (h // 4) if isA else 2 + h
                        return isA, h, kind

                    def issue_kv(hh, KT=KT, VV=VV, nk=nk, kvl=kvl, lstate=lstate):
                        isA, h, kind = hinfo(hh)
                        if kind != lstate['last_kind']:
                            ks, krn = rkt.next()
                            vs, vrn = rvt.next()
                            dma('sp', ktb[ks][:, 0:nk], KT[kind, :, 0:nk], writes=[krn], key=krn)
                            dma('sp', vtb[vs][:, 0:nk // 128, :], VV[0:nk, kind * 128:(kind + 1) * 128].rearrange("(t p) c -> p t c", p=128), writes=[vrn], key=vrn)
                            lstate['kv'] = (ks, krn, vs, vrn)
                            lstate['last_kind'] = kind
                        kvl[hh] = lstate['kv']

                    def issue_wq(hh, wql=wql):
                        isA, h, kind = hinfo(hh)
                        dd = {}
                        ws, wrn = rwq.next()
                        c0 = h * 128 if isA else 1536 + h * 192
                        dma('pool', wq[ws][:], W[:, c0:c0 + 128].rearrange("(k p) n -> p k n", p=128), writes=[wrn], key=wrn)
                        dd['wq'] = (ws, wrn)
                        if (not isA) and h % 2 == 0:
                            ws2, wrn2 = rwq.next()
                            for i2 in range(2):
                                cr = 1536 + (h + i2) * 192 + 128
                                dma('pool', wq[ws2][:, :, i2 * 64:(i2 + 1) * 64], W[:, cr:cr + 64].rearrange("(k p) n -> p k n", p=128),
                                    writes=[wrn2], key=wrn2)
                            dd['wq2'] = (ws2, wrn2)
                        wql[hh] = dd

                    qst = {'cur_qr': None}

                    def qprep(hh, samp=samp, wql=wql, qst=qst):
                        isA, h, kind = hinfo(hh)
                        ws, wrn = wql[hh]['wq']
                        proj(wq, ws, 6, [wrn])
                        qs, qrn = qTr.next()
                        if isA:
                            if samp:
                                wps, wprn = rwqp.next()
                                pr = swap_copy('dve', wqp[wps][:], wq[ws][:], 32, [wrn], wprn)
                                proj(wqp, wps, 7, pr)
                                P.op('dve', lambda e: e.scalar_tensor_tensor(out=t1[:], in0=ps[6][:], scalar=gq[:, 0:1], in1=rA[:, 0, :], op0=ALU.mult, op1=ALU.mult),
                                     reads=['ps6', 'mgq', 'mrA'], writes=['mt1'])
                            else:
                                P.op('dve', lambda e: e.tensor_scalar(out=t1[:], in0=ps[6][:], scalar1=gq[:, 0:1], scalar2=None, op0=ALU.mult),
                                     reads=['ps6', 'mgq'], writes=['mt1'])
                            rms_rstd([ps[6][:]], 128, sqr, sq, 6, rstd, 'mrstd', ['ps6'])
                            if samp:
                                P.op('dve', lambda e: e.scalar_tensor_tensor(out=t2[:], in0=ps[7][:], scalar=gqp[:, 0:1], in1=rA[:, 1, :], op0=ALU.mult, op1=ALU.mult),
                                     reads=['ps7', 'mrA'] + gqp_r, writes=['mt2'])
                                P.op('dve', lambda e: e.tensor_tensor(out=t1[:], in0=t1[:], in1=t2[:], op=ALU.add), reads=['mt1', 'mt2'], writes=['mt1'])
                            P.op('dve', lambda e, qs=qs: e.tensor_tensor(out=qT[qs][:], in0=t1[:], in1=rstd[:], op=ALU.mult), reads=['mt1', 'mrstd'], writes=[qrn])
                            return (qs, qrn, None)
                        P.op('act', lambda e, qs=qs: e.copy(out=qT[qs][:], in_=ps[6][:]), reads=['ps6'], writes=[qrn])
                        if h % 2 == 0:
                            ws2, wrn2 = wql[hh]['wq2']
                            proj(wq, ws2, 7, [wrn2])
                            rs_, rrn = qrr.next()
                            if samp:
                                wps, wprn = rwqp.next()
                                pr = swap_copy('dve', wqp[wps][:], wq[ws2][:], 16, [wrn2], wprn)
                                proj(wqp, wps, 6, pr)
                                P.op('dve', lambda e: e.tensor_tensor(out=t1[:], in0=ps[7][:], in1=rB[:, 0, :], op=ALU.mult), reads=['ps7', 'mrB'], writes=['mt1'])
                                P.op('dve', lambda e: e.tensor_tensor(out=t2[:], in0=ps[6][:], in1=rB[:, 1, :], op=ALU.mult), reads=['ps6', 'mrB'], writes=['mt2'])
                                P.op('dve', lambda e, rs_=rs_: e.tensor_tensor(out=qr[rs_][:], in0=t1[:], in1=t2[:], op=ALU.add), reads=['mt1', 'mt2'], writes=[rrn])
                            else:
                                P.op('act', lambda e, rs_=rs_: e.copy(out=qr[rs_][:], in_=ps[7][:]), reads=['ps7'], writes=[rrn])
                            qst['cur_qr'] = (rs_, rrn)
                        return (qs, qrn, qst['cur_qr'])

                    issue_wq(0)
                    issue_wq(1)
                    issue_kv(0)
                    qinfo = {0: qprep(0)}
                    for hh in range(16):
                        isA, h, kind = hinfo(hh)
                        if hh + 2 < 16:
                            issue_wq(hh + 2)
                        if hh + 1 < 16:
                            issue_kv(hh + 1)
                            qinfo[hh + 1] = qprep(hh + 1)
                        ks, krn, vs, vrn = kvl[hh]
                        qs, qrn, cq = qinfo[hh]
                        if isA:
                            attention(groups, qT[qs], qrn, ks, krn, vs, vrn, 128.0 ** -0.5, h)
                        else:
                            attention(groups, qT[qs], qrn, ks, krn, vs, vrn, 192.0 ** -0.5, 8 + h,
                                      rope=(qr[cq[0]], cq[1], h % 2, krt, kr_rd))
                    for mo2 in range(8):
                        so, rno = rwo.next()
                        dma('pool', wo[so][:], attn_w_out[j][:, mo2 * 256:(mo2 + 1) * 256].rearrange("(k p) n -> p k n", p=128), writes=[rno], key=rno)
                        for ml in range(2):
                            mo = mo2 * 2 + ml
                            b = nsb()
                            ld = epi_load(eb, mo, u)

                            def mmo(e, so=so, ml=ml, b=b):
                                for k in range(KC):
                                    ins = e.matmul(ps[b][:], wo[so][:, k, ml * 128:(ml + 1) * 128], oT[:, k, :], start=(k == 0), stop=(k == KC - 1))
                                return ins
                            P.op('pe', mmo, reads=[rno] + [('moT', c) for c in range(KC)], writes=['ps%d' % b])
                            epi_finish(eb, ld, mo, u, 2, b)
                P.flush()

        def block_final():
            with ExitStack() as st:
                xg = [sb(st, "fxg%d" % i, [128, KC, 128], F32) for i in range(2)]
                sq = [sb(st, "fsq%d" % i, [128, 128], F32) for i in range(2)]
                rstd = [sb(st, "frs%d" % i, [128, 128], F32) for i in range(2)]
                yn = [sb(st, "fyn%d" % i, [128, KC, 128], F32) for i in range(2)]
                yo = [sb(st, "fyo%d" % i, [128, D], F32) for i in range(2)]
                rq = Ring('fsq', 2)
                for t in range(NT // 128):
                    s = t % 2
                    u = t // 4
                    dma('sp', xg[s][:], xT[:, :, t * 128:(t + 1) * 128].rearrange("k p n -> p k n"),
                        reads=[('xT', k, u) for k in range(KC)], writes=['fxg%d' % s], key='fxg%d' % s)
                    for k in range(KC):
                        q, rn = rq.next()
                        P.op('act', lambda e, k=k, q=q, s=s: e.activation(out=sq[q][:], in_=xg[s][:, k, :], func=AF.Square), reads=['fxg%d' % s], writes=[rn])
                        P.op('pe', lambda e, k=k, q=q: e.matmul(ps[0][:, 0:128], ones_f[:], sq[q][:], start=(k == 0), stop=(k == KC - 1)),
                             reads=[rn, 'ones_f'], writes=['ps0'])
                    P.op('act', lambda e, s=s: e.activation(out=rstd[s][:], in_=ps[0][:, 0:128], func=AF.Sqrt, scale=1.0 / D, bias=epsb[:]),
                         reads=['ps0', 'epsb'], writes=['frs%d' % s])
                    P.op('dve', lambda e, s=s: e.reciprocal(out=rstd[s][:], in_=rstd[s][:]), reads=['frs%d' % s], writes=['frs%d' % s])
                    for k in range(KC):
                        P.op('dve', lambda e, k=k, s=s: e.scalar_tensor_tensor(out=yn[s][:, k, :], in0=xg[s][:, k, :], scalar=fnw[:, k:k + 1], in1=rstd[s][:],
                                                                               op0=ALU.mult, op1=ALU.mult),
                             reads=['fxg%d' % s, 'fnw', 'frs%d' % s], writes=[('fyn', s, k // 4)])
                    for g in range(4):
                        b = 1 + (t * 4 + g) % 7

                        def tr(e, s=s, g=g, b=b):
                            for i in range(4):
                                ins = e.transpose(ps[b][:, i * 128:(i + 1) * 128], yn[s][:, g * 4 + i, :], ident[:])
                            return ins
                        P.op('pe', tr, reads=[('fyn', s, g), 'ident'], writes=['ps%d' % b])
                        if g % 2 == 0:
                            P.op('dve', lambda e, s=s, g=g, b=b: e.tensor_copy(out=yo[s][:, g * 512:(g + 1) * 512], in_=ps[b][:]), reads=['ps%d' % b], writes=[('fyo', s, g)])
                        else:
                            P.op('act', lambda e, s=s, g=g, b=b: e.copy(out=yo[s][:, g * 512:(g + 1) * 512], in_=ps[b][:]), reads=['ps%d' % b], writes=[('fyo', s, g)])
                    dma('sp', y_out[t * 128:(t + 1) * 128, :], yo[s][:], reads=[('fyo', s, g) for g in range(4)], key='fyo%d' % s)
                P.flush()

        block_init()
        ada_done = [False] * (DEPTH + 1)
        for L in range(DEPTH):
            if not need_layer[L]:
                continue
            if not ada_done[L]:
                block_ada(L)
                ada_done[L] = True
            cur[0] = L % 2
            if want("mix%d" % L):
                if L % 2 == 1:
                    block_cmlp(L)
                else:
                    block_attn(L)
            if want("ffn%d" % L):
                block_ffn(L)
                if L + 1 < DEPTH and need_layer[L + 1]:
                    ada_done[L + 1] = True
        block_final()
    return nc


def rope_tables(hf):
    t = np.arange(NS) + hf * NS
    row = (t // 64).astype(np.float32)
    col = (t % 64).astype(np.float32)

    def tab(width):
        half = width // 2
        m = half // 2
        inv = (1.0 / (np.float32(10000.0) ** (np.arange(0, half, 2, dtype=np.float32) / np.float32(half)))).astype(np.float32)
        C = np.zeros((width, NS), np.float32)
        S = np.zeros((width, NS), np.float32)
        for p in range(width):
            pos = row if p < half else col
            q = p % half
            f = q % m
            ang = (pos * inv[f]).astype(np.float32)
            C[p] = np.cos(ang)
            S[p] = np.sin(ang) * (-1.0 if q < m else 1.0)
        return C, S
    CA, SA = tab(128)
    CB, SB = tab(64)
    ra = np.stack([CA, SA]).astype(np.float32)
    rb = np.stack([np.concatenate([CB, CB]), np.concatenate([SB, SB])]).astype(np.float32)
    return ra, rb


_CACHE = {}


def kernel(**inputs):
    inp = {k: np.ascontiguousarray(np.asarray(v)) for k, v in inputs.items()}
    import os
    cfg = {"stages": STAGES, "attn_parts": os.environ.get("ATT_PARTS", "123")}
    key = str(STAGES) + cfg["attn_parts"]
    if key not in _CACHE:
        _CACHE[key] = build_program(cfg)
    nc = _CACHE[key]
    def want(name):
        return STAGES is None or name in STAGES
    shared = {k: inp[k] for k in ("ada_b", "norm_mix", "norm_ffn", "attn_q_norm", "attn_k_norm", "attn_kv_norm",
                                  "cmlp_v_norm", "cmlp_w_s", "cmlp_b_s", "final_norm")}
    for L in range(DEPTH):
        if want("mix%d" % L) or want("ffn%d" % L):
            shared["ada_w%d" % L] = inp["ada_w"][L]
        if want("ffn%d" % L):
            shared["ffn_gate%d" % L] = inp["ffn_gate"][L]
            shared["ffn_up%d" % L] = inp["ffn_up"][L]
            shared["ffn_down%d" % L] = inp["ffn_down"][L]
    for j in range(2):
        if want("mix%d" % (2 * j)):
            shared["attn_w_in%d" % j] = inp["attn_w_in"][j]
            shared["attn_w_kv_up%d" % j] = inp["attn_w_kv_up"][j]
            shared["attn_w_out%d" % j] = inp["attn_w_out"][j]
        if want("mix%d" % (2 * j + 1)):
            shared["cmlp_w_in%d" % j] = inp["cmlp_w_in"][j]
            shared["cmlp_w_out%d" % j] = inp["cmlp_w_out"][j]
    ident = np.eye(128, dtype=np.float32)
    in_maps = []
    for c in range(8):
        b, hf = c // 2, c % 2
        xs = inp["x_sample"][b, hf * NS:(hf + 1) * NS]
        xp = inp["x_prompt"][2 * c:2 * c + 2].reshape(NPR, D)
        cond = np.stack([inp["c"][b], inp["c_ctx"]])
        condT = np.ascontiguousarray(cond.reshape(2, KC, 128).transpose(2, 1, 0))
        ra, rb = rope_tables(hf)
        m = dict(shared)
        m.update({
            "xin": np.ascontiguousarray(np.concatenate([xs, xp], axis=0)),
            "condT": condT,
            "cache_k": np.ascontiguousarray(inp["cache_gqa_k"][b].reshape(2, 512, 256)),
            "cache_v": np.ascontiguousarray(inp["cache_gqa_v"][b].reshape(2, 512, 256)),
            "cache_ckv": np.ascontiguousarray(inp["cache_mla_ckv"][b]),
            "cache_kr": np.ascontiguousarray(inp["cache_mla_krope"][b]),
            "ropeA": ra, "ropeB": rb, "ident_in": ident,
        })
        in_maps.append(m)
    res = run_bass_kernel_spmd(nc, in_maps, core_ids=list(range(8)))
    y_prompt = np.zeros((16, 256, D), np.float32)
    y_sample = np.zeros((4, 4096, D), np.float32)
    s_k = np.zeros((16, 2, 256, 2, 128), np.float32)
    s_v = np.zeros((16, 2, 256, 2, 128), np.float32)
    s_ckv = np.zeros((16, 2, 256, 512), np.float32)
    s_kr = np.zeros((16, 2, 256, 64), np.float32)
    for c in range(8):
        r = res.results[c]
        b, hf = c // 2, c % 2
        y = r["y_out"]
        y_sample[b, hf * NS:(hf + 1) * NS] = y[:NS]
        y_prompt[2 * c:2 * c + 2] = y[NS:].reshape(2, 256, D)
        for j in range(2):
            s_k[2 * c:2 * c + 2, j] = r["st_k"][j].reshape(2, 256, 2, 128)
            s_v[2 * c:2 * c + 2, j] = r["st_v"][j].reshape(2, 256, 2, 128)
            s_ckv[2 * c:2 * c + 2, j] = r["st_ckv"][j].reshape(2, 256, 512)
            s_kr[2 * c:2 * c + 2, j] = r["st_kr"][j].reshape(2, 256, 64)
    return (y_prompt, y_sample, s_k, s_v, s_ckv, s_kr)
```

```python
import numpy as np
from contextlib import ExitStack
import concourse.bass as bass
import concourse.mybir as mybir
from concourse.bass_utils import run_bass_kernel_spmd

F32 = mybir.dt.float32
BF16 = mybir.dt.bfloat16
F32R = mybir.dt.float32r
AF = mybir.ActivationFunctionType
ALU = mybir.AluOpType

D = 2048
KC = 16
NT = 2560
NS = 2048
NPR = 512
NU = 5
FF = 5632
JC = 44
EPS = 1e-6
ATTN_IN = 3648
DEPTH = 4

STAGES = None


class Prog:
    ENG = ('pe', 'act', 'dve', 'pool', 'sp')

    def __init__(self, nc, stack):
        self.nc = nc
        self.stack = stack
        self.sems = []
        self.sem_cnt = []
        self.esem = {}
        for e in self.ENG[:4]:
            self.esem[e] = self.new_sem("c_" + e)
        self.dma_sems = {}
        self.q = {e: [] for e in self.ENG}
        self.waited = {e: {} for e in self.ENG}
        self.last_w = {}
        self.readers = {}
        self.nops = 0

    def new_sem(self, name):
        h = self.stack.enter_context(self.nc.semaphore(name))
        self.sems.append(h)
        self.sem_cnt.append(0)
        return len(self.sems) - 1

    def dsem(self, key):
        if key not in self.dma_sems:
            self.dma_sems[key] = self.new_sem("d_%d" % len(self.dma_sems))
        return self.dma_sems[key]

    def op(self, eng, fn, reads=(), writes=(), dma=None, inc=None):
        psr = [r for r in reads if isinstance(r, str) and r.startswith('ps')]
        if psr:
            reads = [r for r in reads if r not in psr]
            writes = list(writes) + psr
        deps = []
        for r in reads:
            w = self.last_w.get(r)
            if w:
                deps.append(w)
        for r in writes:
            w = self.last_w.get(r)
            if w:
                deps.append(w)
            deps.extend(self.readers.get(r, ()))
        if dma is None:
            si = self.esem[eng]
            inc = 1
        else:
            si = self.dsem(dma)
            inc = 16 if inc is None else inc
        self.sem_cnt[si] += inc
        sig = (si, self.sem_cnt[si])
        need = {}
        for (s, v) in deps:
            if eng == 'pe' and s == self.esem['pe']:
                continue
            if self.waited[eng].get(s, 0) < v:
                need[s] = max(need.get(s, 0), v)
        for s, v in need.items():
            self.waited[eng][s] = v
        self.q[eng].append((list(need.items()), fn, si, inc))
        for r in reads:
            self.readers.setdefault(r, []).append(sig)
        for r in writes:
            self.last_w[r] = sig
            self.readers[r] = []
        self.nops += 1
        return sig

    def flush(self):
        nc = self.nc
        need = []
        for si in range(len(self.sems)):
            v = self.sem_cnt[si]
            if v > 0 and self.waited['sp'].get(si, 0) < v:
                need.append((si, v))
                self.waited['sp'][si] = v
        self.q['sp'].append((need, None, None, 0))
        q = self.q
        self.q = {e: [] for e in self.ENG}
        sems = self.sems

        def body(lst):
            def f(e):
                for (waits, fn, si, inc) in lst:
                    for (s, v) in waits:
                        e.wait_ge(sems[s], v)
                    if fn is not None:
                        ins = fn(e)
                        ins.then_inc(sems[si], inc)
            return f
        with nc.Block() as block:
            block.sync(body(q['sp']))
            block.tensor(body(q['pe']))
            block.scalar(body(q['act']))
            block.vector(body(q['dve']))
            block.gpsimd(body(q['pool']))
        self.last_w = {}
        self.readers = {}


class Ring:
    def __init__(self, name, n):
        self.name = name
        self.n = n
        self.i = 0

    def next(self):
        s = self.i % self.n
        self.i += 1
        return s, "%s%d" % (self.name, s)


def build_program(cfg):
    nc = bass.Bass("TRN2", target_bir_lowering=False)

    def din(name, shape):
        return nc.dram_tensor(name, list(shape), F32, kind="ExternalInput").ap()

    def dout(name, shape):
        return nc.dram_tensor(name, list(shape), F32, kind="ExternalOutput").ap()

    xin = din("xin", [NT, D])
    condT = din("condT", [128, KC, 2])
    cache_k = din("cache_k", [2, 512, 256])
    cache_v = din("cache_v", [2, 512, 256])
    cache_ckv = din("cache_ckv", [2, 512, 512])
    cache_kr = din("cache_kr", [2, 512, 64])
    stages = cfg.get("stages")

    def want(name):
        return stages is None or name in stages
    need_layer = [want("mix%d" % L) or want("ffn%d" % L) for L in range(DEPTH)]
    ada_w = [din("ada_w%d" % L, [D, 6 * D]) if need_layer[L] else None for L in range(DEPTH)]
    ada_b = din("ada_b", [DEPTH, 6 * D])
    norm_mix = din("norm_mix", [DEPTH, D])
    norm_ffn = din("norm_ffn", [DEPTH, D])
    ffn_gate = [din("ffn_gate%d" % L, [D, FF]) if want("ffn%d" % L) else None for L in range(DEPTH)]
    ffn_up = [din("ffn_up%d" % L, [D, FF]) if want("ffn%d" % L) else None for L in range(DEPTH)]
    ffn_down = [din("ffn_down%d" % L, [FF, D]) if want("ffn%d" % L) else None for L in range(DEPTH)]
    attn_w_in = [din("attn_w_in%d" % j, [D, ATTN_IN]) if want("mix%d" % (2 * j)) else None for j in range(2)]
    attn_q_norm = din("attn_q_norm", [2, 128])
    attn_k_norm = din("attn_k_norm", [2, 128])
    attn_kv_norm = din("attn_kv_norm", [2, 512])
    attn_w_kv_up = [din("attn_w_kv_up%d" % j, [512, 2048]) if want("mix%d" % (2 * j)) else None for j in range(2)]
    attn_w_out = [din("attn_w_out%d" % j, [D, D]) if want("mix%d" % (2 * j)) else None for j in range(2)]
    cmlp_w_in = [din("cmlp_w_in%d" % j, [D, 2 * D]) if want("mix%d" % (2 * j + 1)) else None for j in range(2)]
    cmlp_v_norm = din("cmlp_v_norm", [2, D])
    cmlp_w_s = din("cmlp_w_s", [2, 8, 128, 128])
    cmlp_b_s = din("cmlp_b_s", [2, 8, 128])
    cmlp_w_out = [din("cmlp_w_out%d" % j, [D, D]) if want("mix%d" % (2 * j + 1)) else None for j in range(2)]
    final_norm = din("final_norm", [D])
    ropeA = din("ropeA", [2, 128, NS])
    ropeB = din("ropeB", [2, 128, NS])
    ident_in = din("ident_in", [128, 128])

    y_out = dout("y_out", [NT, D])
    st_k = dout("st_k", [2, NPR, 256])
    st_v = dout("st_v", [2, NPR, 256])
    st_ckv = dout("st_ckv", [2, NPR, 512])
    st_kr = dout("st_kr", [2, NPR, 64])

    xT = nc.dram_tensor("xT_scratch", [KC, 128, NT], F32).ap()

    with ExitStack() as top:
        P = Prog(nc, top)
        ps = [top.enter_context(nc.psum_tensor("ps%d" % i, [128, 512], F32)) for i in range(8)]

        uniq = [0]

        def sb(stack, name, shape, dt):
            uniq[0] += 1
            return stack.enter_context(nc.sbuf_tensor("%s_%d" % (name, uniq[0]), list(shape), dt))

        ident = sb(top, "ident", [128, 128], F32)
        ones_f = sb(top, "ones_f", [128, 128], F32)
        ones_r = sb(top, "ones_r", [128, 128], F32R)
        ones_b = sb(top, "ones_b", [128, 128], BF16)
        epsb = sb(top, "epsb", [128, 1], F32)
        scond = sb(top, "scond", [128, KC, 2], BF16)
        modT2 = [sb(top, "modT%d" % i, [128, 96, 2], F32) for i in range(2)]
        coefA2 = [sb(top, "coefA%d" % i, [128, 2, KC, 2], F32) for i in range(2)]
        nrmw2 = [sb(top, "nrmw%d" % i, [128, 2, KC], F32) for i in range(2)]
        cur = [0]
        fnw = sb(top, "fnw", [128, KC], F32)

        def dma(eng, out, in_, reads=(), writes=(), key=None, slow=False):
            if slow:
                return P.op(eng, lambda e: e.dma_start(out=out, in_=in_, allow_slow_non_contiguous=True), reads=reads, writes=writes, dma=key)
            return P.op(eng, lambda e: e.dma_start(out=out, in_=in_), reads=reads, writes=writes, dma=key)

        def mod_ap(split, k, c):
            return modT2[cur[0]][:, split * 16 + k, c:c + 1]

        def unit_cond(u):
            return 0 if u < 4 else 1

        def block_init():
            with ExitStack() as st:
                xt = [sb(st, "xt%d" % i, [128, D], F32) for i in range(2)]
                xo = [sb(st, "xo%d" % i, [128, KC, 128], F32) for i in range(2)]
                cnd = sb(st, "cnd", [128, KC, 2], F32)
                dma('sp', ident[:], ident_in[:, :], writes=['ident'], key='c0')
                P.op('dve', lambda e: e.memset(ones_f[:], 1.0), writes=['ones_f'])
                P.op('dve', lambda e: e.tensor_copy(out=ones_r[:], in_=ones_f[:]), reads=['ones_f'], writes=['ones_r'])
                P.op('dve', lambda e: e.memset(ones_b[:], 1.0), writes=['ones_b'])
                P.op('dve', lambda e: e.memset(epsb[:], EPS), writes=['epsb'])
                dma('sp', cnd[:], condT[:, :, :], writes=['cnd'], key='c1')
                P.op('act', lambda e: e.activation(out=scond[:], in_=cnd[:], func=AF.Silu), reads=['cnd'], writes=['scond'])
                dma('sp', fnw[:], final_norm.rearrange("(k p) -> p k", p=128), writes=['fnw'], key='c2', slow=True)
                for t in range(NT // 128):
                    s = t % 2
                    dma('sp', xt[s][:], xin[t * 128:(t + 1) * 128, :], writes=['xt%d' % s], key='xt%d' % s)
                    for g in range(4):
                        b = (t * 4 + g) % 8

                        def tr(e, s=s, g=g, b=b):
                            for i in range(4):
                                k = g * 4 + i
                                ins = e.transpose(ps[b][:, i * 128:(i + 1) * 128], xt[s][:, k * 128:(k + 1) * 128], ident[:])
                            return ins
                        P.op('pe', tr, reads=['xt%d' % s, 'ident'], writes=['ps%d' % b])
                        eng = 'dve' if g % 2 == 0 else 'act'
                        if eng == 'dve':
                            P.op('dve', lambda e, s=s, g=g, b=b: e.tensor_copy(out=xo[s][:, g * 4:(g + 1) * 4, :], in_=ps[b][:].rearrange("p (a n) -> p a n", a=4)),
                                 reads=['ps%d' % b], writes=[('xo', s, g)])
                        else:
                            P.op('act', lambda e, s=s, g=g, b=b: e.copy(out=xo[s][:, g * 4:(g + 1) * 4, :], in_=ps[b][:].rearrange("p (a n) -> p a n", a=4)),
                                 reads=['ps%d' % b], writes=[('xo', s, g)])
                    u = t // 4
                    dma('sp', xT[:, :, t * 128:(t + 1) * 128].rearrange("k p n -> p k n"), xo[s][:],
                        reads=[('xo', s, g) for g in range(4)], writes=[('xTt', t)], key='xo%d' % s)
                P.flush()

        def ada_steps(L, st, bank):
            par = L % 2
            modT, coefA, nrmw = modT2[par], coefA2[par], nrmw2[par]
            wa = [sb(st, "wa%d" % i, [128, KC, 256], BF16) for i in range(2)]
            adab = sb(st, "adab", [128, 96], F32)
            NSL = 48
            names = ['adawa0', 'adawa1']

            def load(s_):
                dma('pool', wa[s_ % 2][:], ada_w[L][:, s_ * 256:(s_ + 1) * 256].rearrange("(k p) n -> p k n", p=128),
                    writes=[names[s_ % 2]], key=names[s_ % 2])

            def first():
                dma('sp', adab[:], ada_b[L].rearrange("(m p) -> p m", p=128), writes=['adab'], key='adac0', slow=True)
                dma('sp', nrmw[:, 0, :], norm_mix[L].rearrange("(k p) -> p k", p=128), writes=['nrmwA%d' % par], key='adac1', slow=True)
                dma('sp', nrmw[:, 1, :], norm_ffn[L].rearrange("(k p) -> p k", p=128), writes=['nrmwB%d' % par], key='adac2', slow=True)
                load(0)

            def step(s_):
                def f():
                    if s_ + 1 < NSL:
                        load(s_ + 1)

                    def mm(e):
                        for mi in range(2):
                            m = s_ * 2 + mi
                            for k in range(KC):
                                ins = e.matmul(ps[bank][:, m * 2:m * 2 + 2], wa[s_ % 2][:, k, mi * 128:(mi + 1) * 128], scond[:, k, :],
                                               start=(k == 0), stop=(k == KC - 1))
                        return ins
                    P.op('pe', mm, reads=[names[s_ % 2], 'scond'], writes=['ps%d' % bank])
                return f

            def last():
                P.op('dve', lambda e: e.tensor_tensor(out=modT[:], in0=ps[bank][:, 0:192].rearrange("p (m c) -> p m c", c=2),
                                                      in1=adab[:].unsqueeze(2).to_broadcast([128, 96, 2]), op=ALU.add),
                     reads=['ps%d' % bank, 'adab'], writes=['modT%d' % par])
                for sub in range(2):
                    sc_split = 1 if sub == 0 else 4
                    for c in range(2):
                        P.op('dve', lambda e, sub=sub, c=c, sc_split=sc_split: e.scalar_tensor_tensor(
                            out=coefA[:, sub, :, c], in0=modT[:, sc_split * 16:(sc_split + 1) * 16, c], scalar=1.0,
                            in1=nrmw[:, sub, :], op0=ALU.add, op1=ALU.mult),
                            reads=['modT%d' % par, 'nrmwA%d' % par, 'nrmwB%d' % par], writes=['coefA%d' % par])
            return [first] + [step(s_) for s_ in range(NSL)] + [last]

        def block_ada(L):
            with ExitStack() as st:
                for f in ada_steps(L, st, 0):
                    f()
                P.flush()

        class PrepBufs:
            def __init__(self, st, nh=1):
                self.xg = sb(st, "xg", [128, KC, 512], F32)
                self.sq = [sb(st, "sq%d" % i, [128, 512], F32R) for i in range(2)]
                self.rstd = sb(st, "rstd", [128, 512], F32)
                self.tmp = [sb(st, "ptmp%d" % i, [128, 512], F32) for i in range(2)]
                self.sqr = Ring('sq', 2)
                self.tmr = Ring('ptmp', 2)

        def prep_unit(pb, u, sub, hT_ap, hname, psb, next_u=None):
            c = unit_cond(u)

            def load_x(uu):
                for g in range(4):
                    dma('sp', pb.xg[:, g * 4:(g + 1) * 4, :], xT[g * 4:(g + 1) * 4, :, uu * 512:(uu + 1) * 512].rearrange("k p n -> p k n"),
                        reads=[('xT', k, uu) for k in range(g * 4, g * 4 + 4)], writes=[('xg', g)], key='xg%d' % g)
            if getattr(pb, 'preloaded', None) != u:
                load_x(u)
            pb.preloaded = None
            for k in range(KC):
                s, rn = pb.sqr.next()
                P.op('act', lambda e, k=k, s=s: e.activation(out=pb.sq[s][:], in_=pb.xg[:, k, :], func=AF.Square),
                     reads=[('xg', k // 4)], writes=[rn])
                P.op('pe', lambda e, k=k, s=s: e.matmul(ps[psb][:], ones_r[:], pb.sq[s][:], start=(k == 0), stop=(k == KC - 1)),
                     reads=[rn, 'ones_r'], writes=['ps%d' % psb])
            P.op('act', lambda e: e.activation(out=pb.rstd[:], in_=ps[psb][:], func=AF.Sqrt, scale=1.0 / D, bias=epsb[:]),
                 reads=['ps%d' % psb, 'epsb'], writes=['rstd'])
            P.op('dve', lambda e: e.reciprocal(out=pb.rstd[:], in_=pb.rstd[:]), reads=['rstd'], writes=['rstd'])
            sh_split = 0 if sub == 0 else 3
            for k in range(KC):
                s, rn = pb.tmr.next()
                P.op('dve', lambda e, k=k, s=s: e.scalar_tensor_tensor(out=pb.tmp[s][:], in0=pb.xg[:, k, :], scalar=coefA2[cur[0]][:, sub, k, c:c + 1],
                                                                       in1=pb.rstd[:], op0=ALU.mult, op1=ALU.mult),
                     reads=[('xg', k // 4), 'coefA%d' % cur[0], 'rstd'], writes=[rn])
                P.op('act', lambda e, k=k, s=s: e.activation(out=hT_ap(k), in_=pb.tmp[s][:], func=AF.Identity,
                                                             bias=mod_ap(sh_split, k, c), scale=1.0),
                     reads=[rn, 'modT%d' % cur[0]], writes=[hname])
            if next_u is not None:
                load_x(next_u)
                pb.preloaded = next_u

        class EpiBufs:
            def __init__(self, st, n=3):
                self.xi = [sb(st, "exi%d" % i, [128, 512], F32) for i in range(n)]
                self.xo = [sb(st, "exo%d" % i, [128, 512], F32) for i in range(n)]
                self.ri = Ring('exi', n)
                self.ro = Ring('exo', n)

        def epi_load(eb, m, u):
            s, rn = eb.ri.next()
            dma('sp', eb.xi[s][:], xT[m, :, u * 512:(u + 1) * 512], reads=[('xT', m, u)], writes=[rn], key=rn)
            return s, rn

        def epi_finish(eb, ld, m, u, gsplit, psb):
            s, rn = ld
            so, rno = eb.ro.next()
            c = unit_cond(u)
            P.op('dve', lambda e: e.scalar_tensor_tensor(out=eb.xo[so][:], in0=ps[psb][:], scalar=mod_ap(gsplit, m, c), in1=eb.xi[s][:],
                                                         op0=ALU.mult, op1=ALU.add),
                 reads=['ps%d' % psb, 'modT%d' % cur[0], rn], writes=[rno])
            dma('sp', xT[m, :, u * 512:(u + 1) * 512], eb.xo[so][:], reads=[rno], writes=[('xT', m, u)], key=rno)

        def block_ffn(L):
            for units in [(0, 1), (2, 3), (4,)]:
                ffn_pass(L, units, ada_next=(units == (0, 1) and L + 1 < DEPTH and need_layer[L + 1]))

        def ffn_pass(L, units, ada_next=False):
            nu = len(units)
            HJ = JC // 2
            with ExitStack() as st:
                pb = PrepBufs(st)
                eb = EpiBufs(st, 3)
                hT = sb(st, "hT", [128, KC, nu * 512], BF16)
                act = sb(st, "actb", [128, HJ, nu * 512], BF16)
                wg = [sb(st, "wg%d" % i, [128, KC, 256], BF16) for i in range(2)]
                wu = [sb(st, "wu%d" % i, [128, KC, 256], BF16) for i in range(2)]
                mcn = 1 if ada_next else 2
                wd = [sb(st, "wd%d" % i, [128, HJ, 128 * mcn], BF16) for i in range(2)]
                sl = [sb(st, "sl%d" % i, [128, 512], F32) for i in range(2)]
                rg, ru, rd, rs = Ring('wg', 2), Ring('wu', 2), Ring('wd', 2), Ring('sl', 2)
                for i, u in enumerate(units):
                    prep_unit(pb, u, 1, lambda k, i=i: hT[:, k, i * 512:(i + 1) * 512], ('hT', i), 0, next_u=(units[i + 1] if i + 1 < nu else None))
                bank = 0
                asteps = ada_steps(L + 1, st, 7) if ada_next else []

                def ada_tick(asteps=asteps):
                    if asteps:
                        asteps.pop(0)()
                for half in range(2):
                    bank = ffn_half(L, units, half, bank, hT, act, wg, wu, wd, sl, rg, ru, rd, rs, eb, ada_tick, mcn)
                while asteps:
                    ada_tick()
                P.flush()

        def ffn_half(L, units, half, bank, hT, act, wg, wu, wd, sl, rg, ru, rd, rs, eb, ada_tick, mcn):
                nu = len(units)
                HJ = JC // 2
                for jj in range(HJ // 2):
                    j0 = half * HJ + jj * 2
                    sg, rng_ = rg.next()
                    su, rnu = ru.next()
                    dma('pool', wg[sg][:], ffn_gate[L][:, j0 * 128:(j0 + 2) * 128].rearrange("(k p) n -> p k n", p=128), writes=[rng_], key=rng_)
                    dma('pool', wu[su][:], ffn_up[L][:, j0 * 128:(j0 + 2) * 128].rearrange("(k p) n -> p k n", p=128), writes=[rnu], key=rnu)
                    ada_tick()
                    for jl in range(2):
                        for i in range(nu):
                            bg, bu = bank % 6, (bank + 1) % 6
                            bank += 2

                            def mm(e, w, slot, b, jl=jl, i=i):
                                for k in range(KC):
                                    ins = e.matmul(ps[b][:], w[slot][:, k, jl * 128:(jl + 1) * 128], hT[:, k, i * 512:(i + 1) * 512],
                                                   start=(k == 0), stop=(k == KC - 1))
                                return ins
                            P.op('pe', lambda e, mm=mm, sg=sg, bg=bg: mm(e, wg, sg, bg), reads=[rng_, ('hT', i)], writes=['ps%d' % bg])
                            P.op('pe', lambda e, mm=mm, su=su, bu=bu: mm(e, wu, su, bu), reads=[rnu, ('hT', i)], writes=['ps%d' % bu])
                            ss, rns = rs.next()
                            P.op('act', lambda e, ss=ss, bg=bg: e.activation(out=sl[ss][:], in_=ps[bg][:], func=AF.Silu),
                                 reads=['ps%d' % bg], writes=[rns])
                            jrel = jj * 2 + jl
                            P.op('dve', lambda e, ss=ss, bu=bu, jrel=jrel, i=i: e.tensor_tensor(out=act[:, jrel, i * 512:(i + 1) * 512], in0=ps[bu][:], in1=sl[ss][:], op=ALU.mult),
                                 reads=['ps%d' % bu, rns], writes=[('act', jrel, i)])
                items = [(m, i, u) for m in range(KC) for i, u in enumerate(units)]
                lds = {0: epi_load(eb, items[0][0], items[0][2])}
                slab = {}
                for t, (m, i, u) in enumerate(items):
                    if i == 0 and m % mcn == 0:
                        sd, rnd = rd.next()
                        dma('pool', wd[sd][:], ffn_down[L][half * HJ * 128:(half + 1) * HJ * 128, m * 128:(m + mcn) * 128].rearrange("(j p) n -> p j n", p=128),
                            writes=[rnd], key=rnd)
                        slab['cur'] = (sd, rnd)
                    if i == 0:
                        ada_tick()
                    sd, rnd = slab['cur']
                    ml = m % mcn
                    b = bank % 7
                    bank += 1
                    if t + 1 < len(items):
                        lds[t + 1] = epi_load(eb, items[t + 1][0], items[t + 1][2])

                    def mm2(e, sd=sd, b=b, i=i, ml=ml):
                        for j in range(HJ):
                            ins = e.matmul(ps[b][:], wd[sd][:, j, ml * 128:(ml + 1) * 128], act[:, j, i * 512:(i + 1) * 512], start=(j == 0), stop=(j == HJ - 1))
                        return ins
                    P.op('pe', mm2, reads=[rnd] + [('act', j, i) for j in range(HJ)], writes=['ps%d' % b])
                    epi_finish(eb, lds.pop(t), m, u, 5, b)
                return bank


        def block_cmlp(L):
            j = L // 2
            with ExitStack() as st:
                pb = PrepBufs(st)
                eb = EpiBufs(st, 2)
                hT = sb(st, "chT", [128, KC, 512], BF16)
                vtm = sb(st, "vtm", [128, 4, D], F32)
                vhat = sb(st, "vhat", [128, 4, D], BF16)
                gated = sb(st, "gated", [128, KC, 512], BF16)
                wv = [sb(st, "cwv%d" % i, [128, KC, 256], BF16) for i in range(2)]
                wu = [sb(st, "cwu%d" % i, [128, KC, 256], BF16) for i in range(2)]
                wo = [sb(st, "cwo%d" % i, [128, KC, 256], BF16) for i in range(2)]
                wsT = sb(st, "wsT", [128, 8, 128], BF16)
                wsl = [sb(st, "wsl%d" % i, [128, 4, 128], F32) for i in range(2)]
                bsb = sb(st, "bsb", [128, 8, 128], F32)
                vn = sb(st, "cvn", [128, KC], F32)
                ssq = sb(st, "cssq", [128, 4], F32)
                crs = sb(st, "crs", [128, 4], F32)
                usb = [sb(st, "usb%d" % i, [128, 512], F32) for i in range(2)]
                svb = [sb(st, "svb%d" % i, [128, 512], F32) for i in range(2)]
                rv, ru, ro, rus, rsv = Ring('cwv', 2), Ring('cwu', 2), Ring('cwo', 2), Ring('usb', 2), Ring('svb', 2)
                dma('sp', vn[:], cmlp_v_norm[j].rearrange("(k p) -> p k", p=128), writes=['cvn'], key='c0', slow=True)
                dma('sp', bsb[:], cmlp_b_s[j].partition_broadcast(128), writes=['bsb'], key='c1')
                for hh in range(2):
                    dma('sp', wsl[hh][:], cmlp_w_s[j, hh * 4:(hh + 1) * 4].rearrange("g p q -> p g q"), writes=['wsl%d' % hh], key='wsl%d' % hh)

                    def trw(e, hh=hh):
                        for i in range(4):
                            ins = e.transpose(ps[hh][:, i * 128:(i + 1) * 128], wsl[hh][:, i, :], ident[:])
                        return ins
                    P.op('pe', trw, reads=['wsl%d' % hh, 'ident'], writes=['ps%d' % hh])
                    P.op('dve', lambda e, hh=hh: e.tensor_copy(out=wsT[:, hh * 4:(hh + 1) * 4, :], in_=ps[hh][:].rearrange("p (g q) -> p g q", g=4)),
                         reads=['ps%d' % hh], writes=['wsT'])
                bank = 2
                for u in range(NU):
                    prep_unit(pb, u, 0, lambda k: hT[:, k, :], 'chT', bank % 8, next_u=(u + 1 if u + 1 < NU else None))
                    bank += 1
                    for n in range(8):
                        sv_, rnv = rv.next()
                        dma('pool', wv[sv_][:], cmlp_w_in[j][:, D + n * 256:D + (n + 1) * 256].rearrange("(k p) n -> p k n", p=128), writes=[rnv], key=rnv)
                        for tc in range(4):
                            b = bank % 8
                            bank += 1

                            def mmv(e, sv_=sv_, tc=tc, b=b):
                                for k in range(KC):
                                    ins = e.matmul(ps[b][:, 0:256], hT[:, k, tc * 128:(tc + 1) * 128], wv[sv_][:, k, :], start=(k == 0), stop=(k == KC - 1))
                                return ins
                            P.op('pe', mmv, reads=[rnv, 'chT'], writes=['ps%d' % b])
                            P.op('act', lambda e, tc=tc, n=n, b=b: e.activation(out=vtm[:, tc, n * 256:(n + 1) * 256], in_=ps[b][:, 0:256], func=AF.Gelu_apprx_tanh),
                                 reads=['ps%d' % b], writes=[('vtm', tc, n)])
                    for tc in range(4):
                        P.op('act', lambda e, tc=tc: e.activation(out=vhat[:, tc, :], in_=vtm[:, tc, :], func=AF.Square, accum_out=ssq[:, tc:tc + 1]),
                             reads=[('vtm', tc, n) for n in range(8)], writes=[('vhat', tc), ('ssq', tc)])
                    P.op('act', lambda e: e.activation(out=crs[:], in_=ssq[:], func=AF.Sqrt, scale=1.0 / D, bias=epsb[:]),
                         reads=[('ssq', tc) for tc in range(4)] + ['epsb'], writes=['crs'])
                    P.op('dve', lambda e: e.reciprocal(out=crs[:], in_=crs[:]), reads=['crs'], writes=['crs'])
                    for tc in range(4):
                        P.op('dve', lambda e, tc=tc: e.tensor_scalar(out=vhat[:, tc, :], in0=vtm[:, tc, :], scalar1=crs[:, tc:tc + 1], scalar2=None, op0=ALU.mult),
                             reads=[('vtm', tc, n) for n in range(8)] + ['crs'], writes=[('vhat', tc)])
                    for mm_ in range(8):
                        su, rnu = ru.next()
                        dma('pool', wu[su][:], cmlp_w_in[j][:, mm_ * 256:(mm_ + 1) * 256].rearrange("(k p) n -> p k n", p=128), writes=[rnu], key=rnu)
                        for ml in range(2):
                            m = mm_ * 2 + ml
                            g = m // 2
                            ba, bb = bank % 8, (bank + 1) % 8
                            bank += 2

                            def mmu(e, su=su, ml=ml, ba=ba):
                                for k in range(KC):
                                    ins = e.matmul(ps[ba][:], wu[su][:, k, ml * 128:(ml + 1) * 128], hT[:, k, :], start=(k == 0), stop=(k == KC - 1))
                                return ins
                            P.op('pe', mmu, reads=[rnu, 'chT'], writes=['ps%d' % ba])
                            s1, rn1 = rus.next()
                            P.op('act', lambda e, s1=s1, ba=ba: e.activation(out=usb[s1][:], in_=ps[ba][:], func=AF.Gelu_apprx_tanh), reads=['ps%d' % ba], writes=[rn1])

                            def mms(e, m=m, g=g, bb=bb):
                                for tc in range(4):
                                    ins = e.matmul(ps[bb][:, tc * 128:(tc + 1) * 128], vhat[:, tc, m * 128:(m + 1) * 128], wsT[:, g, :], start=True, stop=True)
                                return ins
                            P.op('pe', mms, reads=[('vhat', tc) for tc in range(4)] + ['wsT'], writes=['ps%d' % bb])
                            s2, rn2 = rsv.next()
                            P.op('dve', lambda e, s2=s2, bb=bb, m=m, g=g: e.scalar_tensor_tensor(
                                out=svb[s2][:].rearrange("p (t q) -> p t q", t=4), in0=ps[bb][:].rearrange("p (t q) -> p t q", t=4), scalar=vn[:, m:m + 1],
                                in1=bsb[:, g, :].unsqueeze(1).to_broadcast([128, 4, 128]), op0=ALU.mult, op1=ALU.add),
                                reads=['ps%d' % bb, 'cvn', 'bsb'], writes=[rn2])
                            P.op('dve', lambda e, s1=s1, s2=s2, m=m: e.tensor_tensor(out=gated[:, m, :], in0=usb[s1][:], in1=svb[s2][:], op=ALU.mult),
                                 reads=[rn1, rn2], writes=[('gated', m)])
                    pend_ld = epi_load(eb, 0, u)
                    for mo2 in range(8):
                        so, rno = ro.next()
                        dma('pool', wo[so][:], cmlp_w_out[j][:, mo2 * 256:(mo2 + 1) * 256].rearrange("(k p) n -> p k n", p=128), writes=[rno], key=rno)
                        for ml in range(2):
                            mo = mo2 * 2 + ml
                            b = bank % 8
                            bank += 1
                            ld = pend_ld
                            if mo + 1 < KC:
                                pend_ld = epi_load(eb, mo + 1, u)

                            def mmo(e, so=so, ml=ml, b=b):
                                for k in range(KC):
                                    ins = e.matmul(ps[b][:], wo[so][:, k, ml * 128:(ml + 1) * 128], gated[:, k, :], start=(k == 0), stop=(k == KC - 1))
                                return ins
                            P.op('pe', mmo, reads=[rno] + [('gated', m) for m in range(KC)], writes=['ps%d' % b])
                            epi_finish(eb, ld, mo, u, 2, b)
                P.flush()


        def swap_copy(eng, dst, src, blk, reads, writes):
            dv = dst.rearrange("p k (a two c) -> p k a two c", two=2, c=blk)
            sv = src.rearrange("p k (a two c) -> p k a two c", two=2, c=blk)
            P.op(eng, lambda e: e.tensor_copy(out=dv[:, :, :, 0, :], in_=sv[:, :, :, 1, :]), reads=reads, writes=[writes + '_a'])
            P.op(eng, lambda e: e.tensor_copy(out=dv[:, :, :, 1, :], in_=sv[:, :, :, 0, :]), reads=reads, writes=[writes + '_b'])
            return [writes + '_a', writes + '_b']

        def load_perm_gain(dst, src_vec, blk, width, key):
            nb_ = width // blk
            for b_ in range(nb_):
                pb_ = b_ ^ 1
                dma('sp', dst[b_ * blk:(b_ + 1) * blk, 0:1], src_vec[pb_ * blk:(pb_ + 1) * blk].rearrange("(p o) -> p o", o=1),
                    writes=[(key, b_)], key=key, slow=True)
            return [(key, b_) for b_ in range(nb_)]

        class AttnScratch:
            pass

        def attn_scratch(j):
            a = AttnScratch()

            def dt_(name, shape):
                return nc.dram_tensor("%s_%d" % (name, j), list(shape), BF16)
            a.sendF1 = dt_("sendF1", [512, NS])
            a.recvF1 = dt_("recvF1", [1024, NS])
            a.sendF2 = dt_("sendF2", [384, NS])
            a.recvF2 = dt_("recvF2", [768, NS])

            def sF(blk):
                if blk < 4:
                    return a.sendF1.ap()[blk * 128:(blk + 1) * 128, :]
                return a.sendF2.ap()[(blk - 4) * 128:(blk - 3) * 128, :]

            def rF(r, blk):
                if blk < 4:
                    return a.recvF1.ap()[r * 512 + blk * 128:r * 512 + (blk + 1) * 128, :]
                return a.recvF2.ap()[r * 384 + (blk - 4) * 128:r * 384 + (blk - 3) * 128, :]
            a.sF = sF
            a.rF = rF
            a.sendV = dt_("sendV", [NS, 256])
            a.recvV = dt_("recvV", [2 * NS, 256])
            a.KT_s = dt_("KT_s", [10, 128, 4608]).ap()
            a.KR_s = dt_("KR_s", [128, 4608]).ap()
            a.V_s = dt_("V_s", [4608, 1280]).ap()
            a.KT_p = dt_("KT_p", [10, 128, 512]).ap()
            a.KR_p = dt_("KR_p", [128, 512]).ap()
            a.V_p = dt_("V_p", [512, 1280]).ap()
            a.CK_p = dt_("CK_p", [4, 128, 512]).ap()
            return a

        def rms_rstd(srcs, n_feat, sqring, sqbufs, bank, rstd_tile, rname, src_reads):
            n = len(srcs)
            for i, (src, rd) in enumerate(zip(srcs, src_reads)):
                s_, rn = sqring.next()
                P.op('act', lambda e, src=src, s_=s_: e.activation(out=sqbufs[s_][:], in_=src, func=AF.Square), reads=[rd], writes=[rn])
                P.op('pe', lambda e, s_=s_, i=i: e.matmul(ps[bank][:], ones_r[:], sqbufs[s_][:], start=(i == 0), stop=(i == n - 1)),
                     reads=[rn, 'ones_r'], writes=['ps%d' % bank])
            P.op('act', lambda e: e.activation(out=rstd_tile[:], in_=ps[bank][:], func=AF.Sqrt, scale=1.0 / n_feat, bias=epsb[:]),
                 reads=['ps%d' % bank, 'epsb'], writes=[rname])
            P.op('dve', lambda e: e.reciprocal(out=rstd_tile[:], in_=rstd_tile[:]), reads=[rname], writes=[rname])

        def block_attn(L):
            j = L // 2
            a = attn_scratch(j)
            parts = cfg.get("attn_parts", "123")
            if "1" in parts:
                attn_kv_pass(L, j, a)
            if "2" in parts:
                attn_exchange(L, j, a)
            if "3" in parts:
                attn_main(L, j, a)

        def attn_kv_pass(L, j, a):
            W = attn_w_in[j]
            with ExitStack() as st:
                pb = PrepBufs(st)
                hT = sb(st, "ahT", [128, KC, 512], BF16)
                wk = sb(st, "awk", [128, KC, 256], BF16)
                wkp = sb(st, "awkp", [128, KC, 256], BF16)
                wv = sb(st, "awv", [128, KC, 256], BF16)
                wc = sb(st, "awc", [128, KC, 512], BF16)
                wr = sb(st, "awr", [128, KC, 128], BF16)
                wrp = sb(st, "awrp", [128, KC, 128], BF16)
                gk = sb(st, "agk", [128, 1], F32)
                gkp = sb(st, "agkp", [128, 1], F32)
                gkv = sb(st, "agkv", [128, 4], F32)
                rA = sb(st, "arA", [128, 2, 512], F32)
                rB = sb(st, "arB", [128, 2, 512], F32)
                sq = [sb(st, "asq%d" % i, [128, 512], F32R) for i in range(2)]
                sqr = Ring('asq', 2)
                rstd = sb(st, "arstd", [128, 512], F32)
                t1 = sb(st, "at1", [128, 512], F32)
                t2 = sb(st, "at2", [128, 512], F32)
                kf = sb(st, "akf", [128, 512], F32)
                craw = sb(st, "acraw", [128, 4, 512], F32)
                kb = [sb(st, "akb%d" % i, [128, 512], BF16) for i in range(2)]
                kbr = Ring('akb', 2)
                vb = sb(st, "avb", [128, 4, 256], BF16)
                stk = sb(st, "astk", [128, 4, 256], F32)
                stv = sb(st, "astv", [128, 4, 256], F32)
                stc = sb(st, "astc", [128, 4, 512], F32)
                strr = sb(st, "astr", [128, 4, 64], F32)
                dma('pool', wk[:], W[:, 1024:1280].rearrange("(k p) n -> p k n", p=128), writes=['awk'], key='w0')
                dma('pool', wv[:], W[:, 1280:1536].rearrange("(k p) n -> p k n", p=128), writes=['awv'], key='w1')
                dma('pool', wc[:], W[:, 3072:3584].rearrange("(k p) n -> p k n", p=128), writes=['awc'], key='w2')
                dma('pool', wr[:, :, 0:64], W[:, 3584:3648].rearrange("(k p) n -> p k n", p=128), writes=['awr_a'], key='w3')
                dma('pool', wr[:, :, 64:128], W[:, 3584:3648].rearrange("(k p) n -> p k n", p=128), writes=['awr_b'], key='w3')
                wkp_r = swap_copy('dve', wkp[:], wk[:], 32, ['awk'], 'awkp')
                wrp_r = swap_copy('dve', wrp[:], wr[:], 16, ['awr_a', 'awr_b'], 'awrp')
                dma('sp', gk[:], attn_k_norm[j].rearrange("(p o) -> p o", o=1), writes=['agk'], key='c0', slow=True)
                gkp_r = load_perm_gain(gkp, attn_k_norm[j], 32, 128, 'agkp')
                dma('sp', gkv[:], attn_kv_norm[j].rearrange("(c p) -> p c", p=128), writes=['agkv'], key='c1', slow=True)
                bank = [0]

                def nb():
                    b = bank[0] % 8
                    bank[0] += 1
                    return b

                def proj(wt, c0, b, reads):
                    def f(e):
                        for k in range(KC):
                            ins = e.matmul(ps[b][:], wt[:, k, c0:c0 + 128], hT[:, k, :], start=(k == 0), stop=(k == KC - 1))
                        return ins
                    P.op('pe', f, reads=reads + ['ahT'], writes=['ps%d' % b])

                def transposes_out(src, dst_ap_fn, b, reads, wname, ncol=128):
                    def f(e):
                        for tt in range(4):
                            ins = e.transpose(ps[b][:, tt * 128:(tt + 1) * 128], src[:, tt * 128:(tt + 1) * 128], ident[:])
                        return ins
                    P.op('pe', f, reads=reads + ['ident'], writes=['ps%d' % b])
                    P.op('dve', lambda e: e.tensor_copy(out=dst_ap_fn(), in_=ps[b][:].rearrange("p (t c) -> p t c", t=4)[:, :, 0:ncol]),
                         reads=['ps%d' % b], writes=[wname])

                for u in range(NU):
                    samp = u < 4
                    prep_unit(pb, u, 0, lambda k: hT[:, k, :], 'ahT', nb(), next_u=(u + 1 if u + 1 < NU else None))
                    if samp:
                        dma('sp', rA[:], ropeA[:, :, u * 512:(u + 1) * 512].rearrange("t p n -> p t n"), writes=['arA'], key='c2')
                        dma('sp', rB[:], ropeB[:, :, u * 512:(u + 1) * 512].rearrange("t p n -> p t n"), writes=['arB'], key='c3')
                    for h in range(2):
                        b0 = nb()
                        proj(wk, h * 128, b0, ['awk'])
                        if samp:
                            b1 = nb()
                            proj(wkp, h * 128, b1, wkp_r)
                        b2 = nb()
                        rms_rstd([ps[b0][:]], 128, sqr, sq, b2, rstd, 'arstd', ['ps%d' % b0])
                        s_, rnk = kbr.next()
                        if samp:
                            P.op('dve', lambda e, b0=b0: e.scalar_tensor_tensor(out=t1[:], in0=ps[b0][:], scalar=gk[:, 0:1], in1=rA[:, 0, :], op0=ALU.mult, op1=ALU.mult),
                                 reads=['ps%d' % b0, 'agk', 'arA'], writes=['at1'])
                            P.op('dve', lambda e, b1=b1: e.scalar_tensor_tensor(out=t2[:], in0=ps[b1][:], scalar=gkp[:, 0:1], in1=rA[:, 1, :], op0=ALU.mult, op1=ALU.mult),
                                 reads=['ps%d' % b1, 'arA'] + gkp_r, writes=['at2'])
                            P.op('dve', lambda e: e.tensor_tensor(out=t1[:], in0=t1[:], in1=t2[:], op=ALU.add), reads=['at1', 'at2'], writes=['at1'])
                            P.op('dve', lambda e, s_=s_: e.tensor_tensor(out=kb[s_][:], in0=t1[:], in1=rstd[:], op=ALU.mult), reads=['at1', 'arstd'], writes=[rnk])
                            dma('sp', a.sF(h)[:, u * 512:(u + 1) * 512], kb[s_][:], reads=[rnk], key=rnk)
                        else:
                            P.op('dve', lambda e, b0=b0: e.scalar_tensor_tensor(out=kf[:], in0=ps[b0][:], scalar=gk[:, 0:1], in1=rstd[:], op0=ALU.mult, op1=ALU.mult),
                                 reads=['ps%d' % b0, 'agk', 'arstd'], writes=['akf'])
                            P.op('act', lambda e, s_=s_: e.copy(out=kb[s_][:], in_=kf[:]), reads=['akf'], writes=[rnk])
                            dma('sp', a.KT_p[h], kb[s_][:], reads=[rnk], key=rnk)
                            transposes_out(kf, lambda h=h: stk[:, :, h * 128:(h + 1) * 128], nb(), ['akf'], ('astk', h))
                    if not samp:
                        dma('sp', st_k[j].rearrange("(t p) c -> p t c", p=128), stk[:], reads=[('astk', 0), ('astk', 1)], key='so0')
                    for tc in range(4):
                        b = nb()

                        def mmv(e, tc=tc, b=b):
                            for k in range(KC):
                                ins = e.matmul(ps[b][:, 0:256], hT[:, k, tc * 128:(tc + 1) * 128], wv[:, k, :], start=(k == 0), stop=(k == KC - 1))
                            return ins
                        P.op('pe', mmv, reads=['awv', 'ahT'], writes=['ps%d' % b])
                        if not samp:
                            P.op('act', lambda e, tc=tc, b=b: e.copy(out=stv[:, tc, :], in_=ps[b][:, 0:256]), reads=['ps%d' % b], writes=[('astv', tc)])
                        P.op('dve', lambda e, tc=tc, b=b: e.tensor_copy(out=vb[:, tc, :], in_=ps[b][:, 0:256]), reads=['ps%d' % b], writes=[('avb', tc)])
                    if samp:
                        dma('sp', a.sendV.ap()[u * 512:(u + 1) * 512, :].rearrange("(t p) c -> p t c", p=128), vb[:], reads=[('avb', tc) for tc in range(4)], key='so1')
                    else:
                        dma('sp', st_v[j].rearrange("(t p) c -> p t c", p=128), stv[:], reads=[('astv', tc) for tc in range(4)], key='so2')
                        dma('sp', a.V_p[:, 0:256].rearrange("(t p) c -> p t c", p=128), vb[:], reads=[('avb', tc) for tc in range(4)], key='so1')
                    for c4 in range(4):
                        b = nb()
                        proj(wc, c4 * 128, b, ['awc'])
                        P.op('act', lambda e, c4=c4, b=b: e.copy(out=craw[:, c4, :], in_=ps[b][:]), reads=['ps%d' % b], writes=[('acraw', c4)])
                    rms_rstd([craw[:, c4, :] for c4 in range(4)], 512, sqr, sq, nb(), rstd, 'arstd', [('acraw', c4) for c4 in range(4)])
                    for c4 in range(4):
                        s_, rnk = kbr.next()
                        if samp:
                            P.op('dve', lambda e, c4=c4, s_=s_: e.scalar_tensor_tensor(out=kb[s_][:], in0=craw[:, c4, :], scalar=gkv[:, c4:c4 + 1], in1=rstd[:], op0=ALU.mult, op1=ALU.mult),
                                 reads=[('acraw', c4), 'agkv', 'arstd'], writes=[rnk])
                            dma('sp', a.sF(2 + c4)[:, u * 512:(u + 1) * 512], kb[s_][:], reads=[rnk], key=rnk)
                        else:
                            P.op('dve', lambda e, c4=c4: e.scalar_tensor_tensor(out=kf[:], in0=craw[:, c4, :], scalar=gkv[:, c4:c4 + 1], in1=rstd[:], op0=ALU.mult, op1=ALU.mult),
                                 reads=[('acraw', c4), 'agkv', 'arstd'], writes=['akf'])
                            P.op('act', lambda e, s_=s_: e.copy(out=kb[s_][:], in_=kf[:]), reads=['akf'], writes=[rnk])
                            dma('sp', a.CK_p[c4], kb[s_][:], reads=[rnk], key=rnk)
                            transposes_out(kf, lambda c4=c4: stc[:, :, c4 * 128:(c4 + 1) * 128], nb(), ['akf'], ('astc', c4))
                    if not samp:
                        dma('sp', st_ckv[j].rearrange("(t p) c -> p t c", p=128), stc[:], reads=[('astc', c4) for c4 in range(4)], key='so3')
                    b0 = nb()
                    proj(wr, 0, b0, ['awr_a', 'awr_b'])
                    s_, rnk = kbr.next()
                    if samp:
                        b1 = nb()
                        proj(wrp, 0, b1, wrp_r)
                        P.op('dve', lambda e, b0=b0: e.tensor_tensor(out=t1[:], in0=ps[b0][:], in1=rB[:, 0, :], op=ALU.mult), reads=['ps%d' % b0, 'arB'], writes=['at1'])
                        P.op('dve', lambda e, b1=b1: e.tensor_tensor(out=t2[:], in0=ps[b1][:], in1=rB[:, 1, :], op=ALU.mult), reads=['ps%d' % b1, 'arB'], writes=['at2'])
                        P.op('dve', lambda e, s_=s_: e.tensor_tensor(out=kb[s_][:], in0=t1[:], in1=t2[:], op=ALU.add), reads=['at1', 'at2'], writes=[rnk])
                        dma('sp', a.sF(6)[:, u * 512:(u + 1) * 512], kb[s_][:], reads=[rnk], key=rnk)
                    else:
                        P.op('act', lambda e, b0=b0: e.copy(out=kf[:], in_=ps[b0][:]), reads=['ps%d' % b0], writes=['akf'])
                        P.op('dve', lambda e, s_=s_: e.tensor_copy(out=kb[s_][:], in_=kf[:]), reads=['akf'], writes=[rnk])
                        dma('sp', a.KR_p, kb[s_][:], reads=[rnk], key=rnk)
                        transposes_out(kf, lambda: strr[:], nb(), ['akf'], 'astr', ncol=64)
                        dma('sp', st_kr[j].rearrange("(t p) c -> p t c", p=128), strr[:], reads=['astr'], key='so4')
                P.flush()

        def attn_exchange(L, j, a):
            with ExitStack() as st:
                ckvT = sb(st, "xckvT", [128, 4, 4608], BF16)
                ckp = sb(st, "xckp", [128, 4, 512], BF16)
                wup = sb(st, "xwup", [128, 4, 2048], BF16)
                lt = [sb(st, "xlt%d" % i, [128, 512], F32) for i in range(3)]
                kcs = sb(st, "xkcs", [128, 2, 512], BF16)
                krc = sb(st, "xkrc", [128, 512], BF16)
                vcs = sb(st, "xvcs", [128, 4, 256], BF16)
                knb = [sb(st, "xknb%d" % i, [128, 4608], BF16) for i in range(2)]
                knr = Ring('xknb', 2)
                vbb = [sb(st, "xvbb%d" % i, [128, 1024], BF16) for i in range(2)]
                vbr = Ring('xvbb', 2)
                cc1, cc2 = 'cc1_%d' % j, 'cc2_%d' % j
                groups = [[0, 1], [2, 3], [4, 5], [6, 7]]
                P.op('pool', lambda e: e.collective_compute("AllGather", ALU.bypass, replica_groups=groups,
                                                            ins=[a.sendF1.ap().opt()], outs=[a.recvF1.ap().opt()]),
                     writes=['recvF1'], dma=cc1, inc=1)
                P.op('pool', lambda e: e.collective_compute("AllGather", ALU.bypass, replica_groups=groups,
                                                            ins=[a.sendF2.ap().opt()], outs=[a.recvF2.ap().opt()]),
                     writes=['recvF2'], dma=cc1 + 'b', inc=1)
                P.op('pool', lambda e: e.collective_compute("AllGather", ALU.bypass, replica_groups=groups,
                                                            ins=[a.sendV.ap().opt()], outs=[a.recvV.ap().opt()]),
                     writes=['recvV'], dma=cc2, inc=1)
                dma('pool', wup[:], attn_w_kv_up[j].rearrange("(c p) n -> p c n", p=128), writes=['xwup'], key='w0')
                rV = a.recvV.ap()
                for r in range(2):
                    for h in range(2):
                        dma('sp', a.KT_s[h, :, r * NS:(r + 1) * NS], a.rF(r, h), reads=['recvF1', 'recvF2'], key='d0')
                    dma('sp', a.KR_s[:, r * NS:(r + 1) * NS], a.rF(r, 6), reads=['recvF1', 'recvF2'], key='d0')
                    for c4 in range(4):
                        dma('sp', ckvT[:, c4, r * NS:(r + 1) * NS], a.rF(r, 2 + c4), reads=['recvF1', 'recvF2'],
                            writes=[('xckvT', r, c4)], key='d1')
                    dma('sp', a.V_s[r * NS:(r + 1) * NS, 0:256], rV[r * NS:(r + 1) * NS, :], reads=['recvV'], key='d0')
                dma('sp', ckp[:], a.CK_p.rearrange("c p n -> p c n"), writes=['xckp'], key='d2')
                bank = [0]

                def nb():
                    b = bank[0] % 8
                    bank[0] += 1
                    return b
                for tt in range(4):
                    dma('sp', lt[0][:, 0:256], cache_k[j, tt * 128:(tt + 1) * 128, :], writes=['xlt0'], key='xlt0')
                    b = nb()

                    def trk(e, b=b):
                        for h in range(2):
                            ins = e.transpose(ps[b][:, h * 128:(h + 1) * 128], lt[0][:, h * 128:(h + 1) * 128], ident[:])
                        return ins
                    P.op('pe', trk, reads=['xlt0', 'ident'], writes=['ps%d' % b])
                    P.op('dve', lambda e, tt=tt, b=b: e.tensor_copy(out=kcs[:, :, tt * 128:(tt + 1) * 128], in_=ps[b][:, 0:256].rearrange("p (h c) -> p h c", h=2)),
                         reads=['ps%d' % b], writes=[('xkcs', tt)])
                    dma('sp', lt[1][:], cache_ckv[j, tt * 128:(tt + 1) * 128, :], writes=['xlt1'], key='xlt1')
                    b = nb()

                    def trc(e, b=b):
                        for c4 in range(4):
                            ins = e.transpose(ps[b][:, c4 * 128:(c4 + 1) * 128], lt[1][:, c4 * 128:(c4 + 1) * 128], ident[:])
                        return ins
                    P.op('pe', trc, reads=['xlt1', 'ident'], writes=['ps%d' % b])
                    P.op('dve', lambda e, tt=tt, b=b: e.tensor_copy(out=ckvT[:, :, 2 * NS + tt * 128:2 * NS + (tt + 1) * 128], in_=ps[b][:].rearrange("p (c n) -> p c n", c=4)),
                         reads=['ps%d' % b], writes=[('xckvTc', tt)])
                    P.op('sp', lambda e, tt=tt: e.dma_start(out=lt[2][:, 0:64], in_=cache_kr[j, tt * 128:(tt + 1) * 128, :]), writes=['xlt2'], dma='xlt2')
                    P.op('sp', lambda e, tt=tt: e.dma_start(out=lt[2][:, 64:128], in_=cache_kr[j, tt * 128:(tt + 1) * 128, :]), reads=['xlt2'], writes=['xlt2b'], dma='xlt2')
                    b = nb()
                    P.op('pe', lambda e, b=b: e.transpose(ps[b][:, 0:128], lt[2][:, 0:128], ident[:]), reads=['xlt2b', 'ident'], writes=['ps%d' % b, 'xlt2'])
                    P.op('dve', lambda e, tt=tt, b=b: e.tensor_copy(out=krc[:, tt * 128:(tt + 1) * 128], in_=ps[b][:, 0:128]), reads=['ps%d' % b], writes=[('xkrc', tt)])
                for h in range(2):
                    dma('sp', a.KT_s[h, :, 2 * NS:2 * NS + 512], kcs[:, h, :], reads=[('xkcs', tt) for tt in range(4)], key='d3')
                dma('sp', a.KR_s[:, 2 * NS:2 * NS + 512], krc[:], reads=[('xkrc', tt) for tt in range(4)], key='d3')
                dma('pool', vcs[:], cache_v[j].rearrange("(t p) c -> p t c", p=128), writes=['xvcs'], key='w1')
                dma('sp', a.V_s[2 * NS:2 * NS + 512, 0:256].rearrange("(t p) c -> p t c", p=128), vcs[:], reads=['xvcs'], key='d3')
                srd_s = [('xckvT', r, c4) for r in range(2) for c4 in range(4)] + [('xckvTc', tt) for tt in range(4)]
                for (src, nk, KTd, Vd, srd) in ((ckvT, 4608, a.KT_s, a.V_s, srd_s), (ckp, 512, a.KT_p, a.V_p, ['xckp'])):
                    for h in range(8):
                        s_, rn = knr.next()
                        for kn in range(nk // 512):
                            b = nb()

                            def mk(e, h=h, kn=kn, b=b, src=src):
                                for c4 in range(4):
                                    ins = e.matmul(ps[b][:], wup[:, c4, h * 256:h * 256 + 128], src[:, c4, kn * 512:(kn + 1) * 512], start=(c4 == 0), stop=(c4 == 3))
                                return ins
                            P.op('pe', mk, reads=['xwup'] + srd, writes=['ps%d' % b])
                            if kn % 2 == 0:
                                P.op('dve', lambda e, s_=s_, kn=kn, b=b: e.tensor_copy(out=knb[s_][:, kn * 512:(kn + 1) * 512], in_=ps[b][:]), reads=['ps%d' % b], writes=[rn])
                            else:
                                P.op('act', lambda e, s_=s_, kn=kn, b=b: e.copy(out=knb[s_][:, kn * 512:(kn + 1) * 512], in_=ps[b][:]), reads=['ps%d' % b], writes=[rn])
                        dma('sp', KTd[2 + h, :, 0:nk], knb[s_][:, 0:nk], reads=[rn], key=rn)
                    wv4 = wup[:].rearrange("p c (h t n) -> p c h t n", h=8, t=2)
                    for kt in range(nk // 128):
                        s_, rn = vbr.next()
                        for hg in range(2):
                            b = nb()

                            def mv(e, kt=kt, hg=hg, b=b, src=src):
                                for c4 in range(4):
                                    ins = e.matmul(ps[b][:].rearrange("p (h n) -> p h n", h=4), src[:, c4, kt * 128:(kt + 1) * 128], wv4[:, c4, hg * 4:(hg + 1) * 4, 1, :],
                                                   start=(c4 == 0), stop=(c4 == 3))
                                return ins
                            P.op('pe', mv, reads=['xwup'] + srd, writes=['ps%d' % b])
                            if hg == 0:
                                P.op('dve', lambda e, s_=s_, b=b: e.tensor_copy(out=vbb[s_][:, 0:512], in_=ps[b][:]), reads=['ps%d' % b], writes=[rn])
                            else:
                                P.op('act', lambda e, s_=s_, b=b: e.copy(out=vbb[s_][:, 512:1024], in_=ps[b][:]), reads=['ps%d' % b], writes=[rn])
                        dma('sp', Vd[kt * 128:(kt + 1) * 128, 256:1280], vbb[s_][:], reads=[rn], key=rn)
                P.flush()

        def attn_main(L, j, a):
            W = attn_w_in[j]
            with ExitStack() as st:
                pb = PrepBufs(st)
                eb = EpiBufs(st, 2)
                hT = sb(st, "mhT", [128, KC, 512], BF16)
                oT = sb(st, "moT", [128, KC, 512], BF16)
                ktb = [sb(st, "mkt%d" % i, [128, 4608], BF16) for i in range(2)]
                vtb = [sb(st, "mvt%d" % i, [128, 36, 128], BF16) for i in range(2)]
                krs = sb(st, "mkrs", [128, 4608], BF16)
                krp = sb(st, "mkrp", [128, 512], BF16)
                wq = [sb(st, "mwq%d" % i, [128, KC, 128], BF16) for i in range(4)]
                wqp = [sb(st, "mwqp%d" % i, [128, KC, 128], BF16) for i in range(2)]
                wo = [sb(st, "mwo%d" % i, [128, KC, 256], BF16) for i in range(2)]
                rA = sb(st, "mrA", [128, 2, 512], F32)
                rB = sb(st, "mrB", [128, 2, 512], F32)
                gq = sb(st, "mgq", [128, 1], F32)
                gqp = sb(st, "mgqp", [128, 1], F32)
                sq = [sb(st, "msq%d" % i, [128, 512], F32R) for i in range(2)]
                sqr = Ring('msq', 2)
                rstd = sb(st, "mrstd", [128, 512], F32)
                t1 = sb(st, "mt1", [128, 512], F32)
                t2 = sb(st, "mt2", [128, 512], F32)
                qT = [sb(st, "mqT%d" % i, [128, 512], BF16) for i in range(2)]
                qTr = Ring('mqT', 2)
                qr = [sb(st, "mqr%d" % i, [128, 512], BF16) for i in range(2)]
                qrr = Ring('mqr', 2)
                pT = [sb(st, "mpT%d" % i, [128, 512], BF16) for i in range(3)]
                pTr = Ring('mpT', 3)
                rden = sb(st, "mrden", [128, 512], F32)
                rwq, rwo, rkt, rvt, rwqp = Ring('mwq', 4), Ring('mwo', 2), Ring('mkt', 2), Ring('mvt', 2), Ring('mwqp', 2)
                dma('sp', gq[:], attn_q_norm[j].rearrange("(p o) -> p o", o=1), writes=['mgq'], key='c0', slow=True)
                gqp_r = load_perm_gain(gqp, attn_q_norm[j], 32, 128, 'mgqp')
                dma('sp', krs[:], a.KR_s, writes=['mkrs'], key='c1')
                dma('sp', krp[:], a.KR_p, writes=['mkrp'], key='c2')
                sbank = [0]
                obank = [0]

                def nsb():
                    b = sbank[0] % 4
                    sbank[0] += 1
                    return b

                def proj(wt, slot, b, reads):
                    def f(e):
                        for k in range(KC):
                            ins = e.matmul(ps[b][:], wt[slot][:, k, :], hT[:, k, :], start=(k == 0), stop=(k == KC - 1))
                        return ins
                    P.op('pe', f, reads=reads + ['mhT'], writes=['ps%d' % b])

                def attention(groups, q_ap, q_rd, kt_slot, kt_rd, vt_slot, vt_rd, scale, chunk, rope=None):
                    for (q0_, nq_, tiles_) in groups:
                        do_group(q0_, nq_, tiles_, q_ap, q_rd, kt_slot, kt_rd, vt_slot, vt_rd, scale, chunk, rope)

                def do_group(q0, nq, tiles, q_ap, q_rd, kt_slot, kt_rd, vt_slot, vt_rd, scale, chunk, rope):
                    if True:
                        ob = 4
                        db = 5
                        nt = len(tiles)

                        def score(idx):
                            kt = tiles[idx]
                            b = nsb()

                            def f(e):
                                ins = e.matmul(ps[b][:, 0:nq], ktb[kt_slot][:, kt * 128:(kt + 1) * 128], q_ap[:, q0:q0 + nq], start=True, stop=(rope is None))
                                if rope is not None:
                                    qrt, _, hp, krt, _ = rope
                                    ins = e.matmul(ps[b][:, 0:nq], krt[hp * 64:(hp + 1) * 64, kt * 128:(kt + 1) * 128], qrt[hp * 64:(hp + 1) * 64, q0:q0 + nq],
                                                   start=False, stop=True)
                                return ins
                            rds = [kt_rd, q_rd] + ([rope[1], rope[4]] if rope is not None else [])
                            P.op('pe', f, reads=rds, writes=['ps%d' % b])
                            return b
                        pend = [score(0)]
                        if nt > 1:
                            pend.append(score(1))
                        for idx in range(nt):
                            b = pend.pop(0)
                            if idx + 2 < nt:
                                pend.append(score(idx + 2))
                            s_, rnp = pTr.next()
                            P.op('act', lambda e, b=b, s_=s_: e.activation(out=pT[s_][:, 0:nq], in_=ps[b][:, 0:nq], func=AF.Exp, scale=scale),
                                 reads=['ps%d' % b], writes=[rnp])
                            kt = tiles[idx]

                            def pv(e, s_=s_, kt=kt, idx=idx):
                                e.matmul(ps[ob][:, 0:nq], vtb[vt_slot][:, kt, :], pT[s_][:, 0:nq], start=(idx == 0), stop=(idx == nt - 1))
                                return e.matmul(ps[db][:, 0:nq], ones_b[:], pT[s_][:, 0:nq], start=(idx == 0), stop=(idx == nt - 1))
                            P.op('pe', pv, reads=[rnp, vt_rd, 'ones_b'], writes=['ps%d' % ob, 'ps%d' % db])
                        P.op('dve', lambda e: e.reciprocal(out=rden[:, 0:nq], in_=ps[db][:, 0:nq]), reads=['ps%d' % db], writes=['mrden'])
                        P.op('dve', lambda e: e.tensor_tensor(out=oT[:, chunk, q0:q0 + nq], in0=ps[ob][:, 0:nq], in1=rden[:, 0:nq], op=ALU.mult),
                             reads=['ps%d' % ob, 'mrden'], writes=[('moT', chunk)])

                for u in range(NU):
                    samp = u < 4
                    prep_unit(pb, u, 0, lambda k: hT[:, k, :], 'mhT', nsb(), next_u=(u + 1 if u + 1 < NU else None))
                    if samp:
                        dma('sp', rA[:], ropeA[:, :, u * 512:(u + 1) * 512].rearrange("t p n -> p t n"), writes=['mrA'], key='c3')
                        dma('sp', rB[:], ropeB[:, :, u * 512:(u + 1) * 512].rearrange("t p n -> p t n"), writes=['mrB'], key='c4')
                        groups = [(0, 512, list(range(36)))]
                        KT, VV, nk = a.KT_s, a.V_s, 4608
                        krt, kr_rd = krs, 'mkrs'
                    else:
                        groups = [(0, 256, [0, 1]), (256, 256, [2, 3])]
                        KT, VV, nk = a.KT_p, a.V_p, 512
                        krt, kr_rd = krp, 'mkrp'
                    kvl = {}
                    wql = {}
                    lstate = {'last_kind': None, 'kv': None}

                    def hinfo(hh):
                        isA = hh < 8
                        h = hh if isA else hh - 8
                        kind = (h // 4) if isA else 2 + h
                        return isA, h, kind

                    def issue_kv(hh, KT=KT, VV=VV, nk=nk, kvl=kvl, lstate=lstate):
                        isA, h, kind = hinfo(hh)
                        if kind != lstate['last_kind']:
                            ks, krn = rkt.next()
                            vs, vrn = rvt.next()
                            dma('sp', ktb[ks][:, 0:nk], KT[kind, :, 0:nk], writes=[krn], key=krn)
                            dma('sp', vtb[vs][:, 0:nk // 128, :], VV[0:nk, kind * 128:(kind + 1) * 128].rearrange("(t p) c -> p t c", p=128), writes=[vrn], key=vrn)
                            lstate['kv'] = (ks, krn, vs, vrn)
                            lstate['last_kind'] = kind
                        kvl[hh] = lstate['kv']

                    def issue_wq(hh, wql=wql):
                        isA, h, kind = hinfo(hh)
                        dd = {}
                        ws, wrn = rwq.next()
                        c0 = h * 128 if isA else 1536 + h * 192
                        dma('pool', wq[ws][:], W[:, c0:c0 + 128].rearrange("(k p) n -> p k n", p=128), writes=[wrn], key=wrn)
                        dd['wq'] = (ws, wrn)
                        if (not isA) and h % 2 == 0:
                            ws2, wrn2 = rwq.next()
                            for i2 in range(2):
                                cr = 1536 + (h + i2) * 192 + 128
                                dma('pool', wq[ws2][:, :, i2 * 64:(i2 + 1) * 64], W[:, cr:cr + 64].rearrange("(k p) n -> p k n", p=128),
                                    writes=[wrn2], key=wrn2)
                            dd['wq2'] = (ws2, wrn2)
                        wql[hh] = dd

                    qst = {'cur_qr': None}

                    def qprep(hh, samp=samp, wql=wql, qst=qst):
                        isA, h, kind = hinfo(hh)
                        ws, wrn = wql[hh]['wq']
                        proj(wq, ws, 6, [wrn])
                        qs, qrn = qTr.next()
                        if isA:
                            if samp:
                                wps, wprn = rwqp.next()
                                pr = swap_copy('dve', wqp[wps][:], wq[ws][:], 32, [wrn], wprn)
                                proj(wqp, wps, 7, pr)
                                P.op('dve', lambda e: e.scalar_tensor_tensor(out=t1[:], in0=ps[6][:], scalar=gq[:, 0:1], in1=rA[:, 0, :], op0=ALU.mult, op1=ALU.mult),
                                     reads=['ps6', 'mgq', 'mrA'], writes=['mt1'])
                            else:
                                P.op('dve', lambda e: e.tensor_scalar(out=t1[:], in0=ps[6][:], scalar1=gq[:, 0:1], scalar2=None, op0=ALU.mult),
                                     reads=['ps6', 'mgq'], writes=['mt1'])
                            rms_rstd([ps[6][:]], 128, sqr, sq, 6, rstd, 'mrstd', ['ps6'])
                            if samp:
                                P.op('dve', lambda e: e.scalar_tensor_tensor(out=t2[:], in0=ps[7][:], scalar=gqp[:, 0:1], in1=rA[:, 1, :], op0=ALU.mult, op1=ALU.mult),
                                     reads=['ps7', 'mrA'] + gqp_r, writes=['mt2'])
                                P.op('dve', lambda e: e.tensor_tensor(out=t1[:], in0=t1[:], in1=t2[:], op=ALU.add), reads=['mt1', 'mt2'], writes=['mt1'])
                            P.op('dve', lambda e, qs=qs: e.tensor_tensor(out=qT[qs][:], in0=t1[:], in1=rstd[:], op=ALU.mult), reads=['mt1', 'mrstd'], writes=[qrn])
                            return (qs, qrn, None)
                        P.op('act', lambda e, qs=qs: e.copy(out=qT[qs][:], in_=ps[6][:]), reads=['ps6'], writes=[qrn])
                        if h % 2 == 0:
                            ws2, wrn2 = wql[hh]['wq2']
                            proj(wq, ws2, 7, [wrn2])
                            rs_, rrn = qrr.next()
                            if samp:
                                wps, wprn = rwqp.next()
                                pr = swap_copy('dve', wqp[wps][:], wq[ws2][:], 16, [wrn2], wprn)
                                proj(wqp, wps, 6, pr)
                                P.op('dve', lambda e: e.tensor_tensor(out=t1[:], in0=ps[7][:], in1=rB[:, 0, :], op=ALU.mult), reads=['ps7', 'mrB'], writes=['mt1'])
                                P.op('dve', lambda e: e.tensor_tensor(out=t2[:], in0=ps[6][:], in1=rB[:, 1, :], op=ALU.mult), reads=['ps6', 'mrB'], writes=['mt2'])
                                P.op('dve', lambda e, rs_=rs_: e.tensor_tensor(out=qr[rs_][:], in0=t1[:], in1=t2[:], op=ALU.add), reads=['mt1', 'mt2'], writes=[rrn])
                            else:
                                P.op('act', lambda e, rs_=rs_: e.copy(out=qr[rs_][:], in_=ps[7][:]), reads=['ps7'], writes=[rrn])
                            qst['cur_qr'] = (rs_, rrn)
                        return (qs, qrn, qst['cur_qr'])

                    issue_wq(0)
                    issue_wq(1)
                    issue_kv(0)
                    qinfo = {0: qprep(0)}
                    for hh in range(16):
                        isA, h, kind = hinfo(hh)
                        if hh + 2 < 16:
                            issue_wq(hh + 2)
                        if hh + 1 < 16:
                            issue_kv(hh + 1)
                            qinfo[hh + 1] = qprep(hh + 1)
                        ks, krn, vs, vrn = kvl[hh]
                        qs, qrn, cq = qinfo[hh]
                        if isA:
                            attention(groups, qT[qs], qrn, ks, krn, vs, vrn, 128.0 ** -0.5, h)
                        else:
                            attention(groups, qT[qs], qrn, ks, krn, vs, vrn, 192.0 ** -0.5, 8 + h,
                                      rope=(qr[cq[0]], cq[1], h % 2, krt, kr_rd))
                    pend_ld = epi_load(eb, 0, u)
                    for mo2 in range(8):
                        so, rno = rwo.next()
                        dma('pool', wo[so][:], attn_w_out[j][:, mo2 * 256:(mo2 + 1) * 256].rearrange("(k p) n -> p k n", p=128), writes=[rno], key=rno)
                        for ml in range(2):
                            mo = mo2 * 2 + ml
                            b = nsb()
                            ld = pend_ld
                            if mo + 1 < KC:
                                pend_ld = epi_load(eb, mo + 1, u)

                            def mmo(e, so=so, ml=ml, b=b):
                                for k in range(KC):
                                    ins = e.matmul(ps[b][:], wo[so][:, k, ml * 128:(ml + 1) * 128], oT[:, k, :], start=(k == 0), stop=(k == KC - 1))
                                return ins
                            P.op('pe', mmo, reads=[rno] + [('moT', c) for c in range(KC)], writes=['ps%d' % b])
                            epi_finish(eb, ld, mo, u, 2, b)
                P.flush()

        def block_final():
            with ExitStack() as st:
                xg = [sb(st, "fxg%d" % i, [128, KC, 128], F32) for i in range(2)]
                sq = [sb(st, "fsq%d" % i, [128, 128], F32) for i in range(2)]
                rstd = [sb(st, "frs%d" % i, [128, 128], F32) for i in range(2)]
                yn = [sb(st, "fyn%d" % i, [128, KC, 128], F32) for i in range(2)]
                yo = [sb(st, "fyo%d" % i, [128, D], F32) for i in range(2)]
                rq = Ring('fsq', 2)
                for t in range(NT // 128):
                    s = t % 2
                    u = t // 4
                    dma('sp', xg[s][:], xT[:, :, t * 128:(t + 1) * 128].rearrange("k p n -> p k n"),
                        reads=[('xT', k, u) for k in range(KC)], writes=['fxg%d' % s], key='fxg%d' % s)
                    for k in range(KC):
                        q, rn = rq.next()
                        P.op('act', lambda e, k=k, q=q, s=s: e.activation(out=sq[q][:], in_=xg[s][:, k, :], func=AF.Square), reads=['fxg%d' % s], writes=[rn])
                        P.op('pe', lambda e, k=k, q=q: e.matmul(ps[0][:, 0:128], ones_f[:], sq[q][:], start=(k == 0), stop=(k == KC - 1)),
                             reads=[rn, 'ones_f'], writes=['ps0'])
                    P.op('act', lambda e, s=s: e.activation(out=rstd[s][:], in_=ps[0][:, 0:128], func=AF.Sqrt, scale=1.0 / D, bias=epsb[:]),
                         reads=['ps0', 'epsb'], writes=['frs%d' % s])
                    P.op('dve', lambda e, s=s: e.reciprocal(out=rstd[s][:], in_=rstd[s][:]), reads=['frs%d' % s], writes=['frs%d' % s])
                    for k in range(KC):
                        P.op('dve', lambda e, k=k, s=s: e.scalar_tensor_tensor(out=yn[s][:, k, :], in0=xg[s][:, k, :], scalar=fnw[:, k:k + 1], in1=rstd[s][:],
                                                                               op0=ALU.mult, op1=ALU.mult),
                             reads=['fxg%d' % s, 'fnw', 'frs%d' % s], writes=[('fyn', s, k // 4)])
                    for g in range(4):
                        b = 1 + (t * 4 + g) % 7

                        def tr(e, s=s, g=g, b=b):
                            for i in range(4):
                                ins = e.transpose(ps[b][:, i * 128:(i + 1) * 128], yn[s][:, g * 4 + i, :], ident[:])
                            return ins
                        P.op('pe', tr, reads=[('fyn', s, g), 'ident'], writes=['ps%d' % b])
                        if g % 2 == 0:
                            P.op('dve', lambda e, s=s, g=g, b=b: e.tensor_copy(out=yo[s][:, g * 512:(g + 1) * 512], in_=ps[b][:]), reads=['ps%d' % b], writes=[('fyo', s, g)])
                        else:
                            P.op('act', lambda e, s=s, g=g, b=b: e.copy(out=yo[s][:, g * 512:(g + 1) * 512], in_=ps[b][:]), reads=['ps%d' % b], writes=[('fyo', s, g)])
                    dma('sp', y_out[t * 128:(t + 1) * 128, :], yo[s][:], reads=[('fyo', s, g) for g in range(4)], key='fyo%d' % s)
                P.flush()

        block_init()
        ada_done = [False] * (DEPTH + 1)
        for L in range(DEPTH):
            if not need_layer[L]:
                continue
            if not ada_done[L]:
                block_ada(L)
                ada_done[L] = True
            cur[0] = L % 2
            if want("mix%d" % L):
                if L % 2 == 1:
                    block_cmlp(L)
                else:
                    block_attn(L)
            if want("ffn%d" % L):
                block_ffn(L)
                if L + 1 < DEPTH and need_layer[L + 1]:
                    ada_done[L + 1] = True
        block_final()
    return nc


def rope_tables(hf):
    t = np.arange(NS) + hf * NS
    row = (t // 64).astype(np.float32)
    col = (t % 64).astype(np.float32)

    def tab(width):
        half = width // 2
        m = half // 2
        inv = (1.0 / (np.float32(10000.0) ** (np.arange(0, half, 2, dtype=np.float32) / np.float32(half)))).astype(np.float32)
        C = np.zeros((width, NS), np.float32)
        S = np.zeros((width, NS), np.float32)
        for p in range(width):
            pos = row if p < half else col
            q = p % half
            f = q % m
            ang = (pos * inv[f]).astype(np.float32)
            C[p] = np.cos(ang)
            S[p] = np.sin(ang) * (-1.0 if q < m else 1.0)
        return C, S
    CA, SA = tab(128)
    CB, SB = tab(64)
    ra = np.stack([CA, SA]).astype(np.float32)
    rb = np.stack([np.concatenate([CB, CB]), np.concatenate([SB, SB])]).astype(np.float32)
    return ra, rb


_CACHE = {}


def kernel(**inputs):
    inp = {k: np.ascontiguousarray(np.asarray(v)) for k, v in inputs.items()}
    import os
    cfg = {"stages": STAGES, "attn_parts": os.environ.get("ATT_PARTS", "123")}
    key = str(STAGES) + cfg["attn_parts"]
    if key not in _CACHE:
        _CACHE[key] = build_program(cfg)
    nc = _CACHE[key]
    def want(name):
        return STAGES is None or name in STAGES
    shared = {k: inp[k] for k in ("ada_b", "norm_mix", "norm_ffn", "attn_q_norm", "attn_k_norm", "attn_kv_norm",
                                  "cmlp_v_norm", "cmlp_w_s", "cmlp_b_s", "final_norm")}
    for L in range(DEPTH):
        if want("mix%d" % L) or want("ffn%d" % L):
            shared["ada_w%d" % L] = inp["ada_w"][L]
        if want("ffn%d" % L):
            shared["ffn_gate%d" % L] = inp["ffn_gate"][L]
            shared["ffn_up%d" % L] = inp["ffn_up"][L]
            shared["ffn_down%d" % L] = inp["ffn_down"][L]
    for j in range(2):
        if want("mix%d" % (2 * j)):
            shared["attn_w_in%d" % j] = inp["attn_w_in"][j]
            shared["attn_w_kv_up%d" % j] = inp["attn_w_kv_up"][j]
            shared["attn_w_out%d" % j] = inp["attn_w_out"][j]
        if want("mix%d" % (2 * j + 1)):
            shared["cmlp_w_in%d" % j] = inp["cmlp_w_in"][j]
            shared["cmlp_w_out%d" % j] = inp["cmlp_w_out"][j]
    ident = np.eye(128, dtype=np.float32)
    in_maps = []
    for c in range(8):
        b, hf = c // 2, c % 2
        xs = inp["x_sample"][b, hf * NS:(hf + 1) * NS]
        xp = inp["x_prompt"][2 * c:2 * c + 2].reshape(NPR, D)
        cond = np.stack([inp["c"][b], inp["c_ctx"]])
        condT = np.ascontiguousarray(cond.reshape(2, KC, 128).transpose(2, 1, 0))
        ra, rb = rope_tables(hf)
        m = dict(shared)
        m.update({
            "xin": np.ascontiguousarray(np.concatenate([xs, xp], axis=0)),
            "condT": condT,
            "cache_k": np.ascontiguousarray(inp["cache_gqa_k"][b].reshape(2, 512, 256)),
            "cache_v": np.ascontiguousarray(inp["cache_gqa_v"][b].reshape(2, 512, 256)),
            "cache_ckv": np.ascontiguousarray(inp["cache_mla_ckv"][b]),
            "cache_kr": np.ascontiguousarray(inp["cache_mla_krope"][b]),
            "ropeA": ra, "ropeB": rb, "ident_in": ident,
        })
        in_maps.append(m)
    res = run_bass_kernel_spmd(nc, in_maps, core_ids=list(range(8)))
    y_prompt = np.zeros((16, 256, D), np.float32)
    y_sample = np.zeros((4, 4096, D), np.float32)
    s_k = np.zeros((16, 2, 256, 2, 128), np.float32)
    s_v = np.zeros((16, 2, 256, 2, 128), np.float32)
    s_ckv = np.zeros((16, 2, 256, 512), np.float32)
    s_kr = np.zeros((16, 2, 256, 64), np.float32)
    for c in range(8):
        r = res.results[c]
        b, hf = c // 2, c % 2
        y = r["y_out"]
        y_sample[b, hf * NS:(hf + 1) * NS] = y[:NS]
        y_prompt[2 * c:2 * c + 2] = y[NS:].reshape(2, 256, D)
        for j in range(2):
            s_k[2 * c:2 * c + 2, j] = r["st_k"][j].reshape(2, 256, 2, 128)
            s_v[2 * c:2 * c + 2, j] = r["st_v"][j].reshape(2, 256, 2, 128)
            s_ckv[2 * c:2 * c + 2, j] = r["st_ckv"][j].reshape(2, 256, 512)
            s_kr[2 * c:2 * c + 2, j] = r["st_kr"][j].reshape(2, 256, 64)
    return (y_prompt, y_sample, s_k, s_v, s_ckv, s_kr)
```
